# Optimizing a Trainium2 kernel written in Bass

```python
import math
import jax, jax.numpy as jnp
from jax import lax
import numpy as np

D_MODEL = 1024
BATCH = 4
SEQ = 4096
DEPTH = 1

D_MIX = D_MODEL
HGRN_W = D_MIX // 2
CONV_W = D_MIX - HGRN_W
HGRN_HEADS = 4
HGRN_DV = HGRN_W // HGRN_HEADS
HGRN_DK = 128
HGRN_F = HGRN_HEADS * HGRN_DK
CONV_GROUPS = 8
CONV_K = 3
CHUNK = 64
MEM_LEN = 256
MEM_HEADS = 4
MEM_HD = D_MODEL // MEM_HEADS
D_FF = int(math.ceil(8 * D_MODEL / 3 / 256) * 256)
EPS = 1e-6
IN_COLS = 3 * HGRN_F // HGRN_F * 0 + HGRN_F + HGRN_F + HGRN_W + HGRN_W + 3 * CONV_W

kernel_name = "hybrid_hgrn2_shortconv_macaron_memxattn"


def rms_norm(x, g):
    xf = x.astype(jnp.float32)
    y = xf * lax.rsqrt(jnp.mean(xf * xf, axis=-1, keepdims=True) + EPS)
    return (y * g.astype(jnp.float32)).astype(x.dtype)


def swiglu(h, w_gate, w_up, w_down):
    return (jax.nn.silu(h @ w_gate) * (h @ w_up)) @ w_down


def hgrn2_chunkwise(q, k, v, logf):
    b_, s_, h_, dk = q.shape
    dv = v.shape[-1]
    n = s_ // CHUNK

    def to_chunks(t):
        return t.astype(jnp.float32).reshape(b_, n, CHUNK, h_, t.shape[-1]).transpose(1, 0, 3, 2, 4)

    qc, kc, vc, lc = to_chunks(q), to_chunks(k), to_chunks(v), to_chunks(logf)
    bc = jnp.cumsum(lc, axis=-2)
    causal = jnp.tril(jnp.ones((CHUNK, CHUNK), dtype=bool))[None, None, :, :, None]

    def step(state, inp):
        qi, ki, vi, bi = inp
        diff = bi[:, :, :, None, :] - bi[:, :, None, :, :]
        decay = jnp.exp(jnp.where(causal, diff, -jnp.inf))
        scores = jnp.einsum('bhtk,bhsk,bhtsk->bhts', qi, ki, decay)
        o = jnp.einsum('bhts,bhsv->bhtv', scores, vi) + \
            jnp.einsum('bhtk,bhkv->bhtv', qi * jnp.exp(bi), state)
        b_last = bi[:, :, -1:, :]
        new_state = jnp.exp(b_last[:, :, 0, :])[..., None] * state + \
            jnp.einsum('bhsk,bhsv->bhkv', ki * jnp.exp(b_last - bi), vi)
        return new_state, o

    s0 = jnp.zeros((b_, h_, dk, dv), jnp.float32)
    _, o = lax.scan(step, s0, (qc, kc, vc, bc))
    return o.transpose(1, 0, 3, 2, 4).reshape(b_, s_, h_, dv)


def causal_depthwise_conv(u, w):
    s_ = u.shape[1]
    up = jnp.pad(u, ((0, 0), (CONV_K - 1, 0), (0, 0)))
    return sum(w[:, j] * up[:, j:j + s_, :] for j in range(CONV_K))


def setup_inputs(seed: int = 0) -> dict:
    key = jax.random.key(seed)
    ks = jax.random.split(key, 24)
    L = DEPTH

    def w(k, shape, fan_in):
        return jax.random.normal(k, shape, jnp.float32) * fan_in ** -0.5

    def gain(k, shape):
        return 1.0 + 0.02 * jax.random.normal(k, shape, jnp.float32)

    return {
        "x": jax.random.normal(ks[0], (BATCH, SEQ, D_MODEL), jnp.float32),
        "mem": jax.random.normal(ks[1], (BATCH, MEM_LEN, D_MODEL), jnp.float32),
        "ffn1_norm": gain(ks[2], (L, D_MODEL)),
        "ffn1_gate": w(ks[3], (L, D_MODEL, D_FF), D_MODEL),
        "ffn1_up": w(ks[4], (L, D_MODEL, D_FF), D_MODEL),
        "ffn1_down": w(ks[5], (L, D_FF, D_MODEL), D_FF),
        "mix_norm": gain(ks[6], (L, D_MODEL)),
        "w_in": w(ks[7], (L, D_MODEL, IN_COLS), D_MODEL),
        "lb_param": 0.1 * jax.random.normal(ks[8], (L + 1, HGRN_F), jnp.float32),
        "hgrn_out_norm": gain(ks[9], (L, HGRN_W)),
        "conv_w": w(ks[10], (L, CONV_W, CONV_K), CONV_K),
        "w_out": w(ks[11], (L, D_MIX, D_MODEL), D_MIX),
        "xattn_norm": gain(ks[12], (L, D_MODEL)),
        "mem_norm": gain(ks[13], (L, D_MODEL)),
        "w_q_mem": w(ks[14], (L, D_MODEL, D_MODEL), D_MODEL),
        "w_kv_mem": w(ks[15], (L, D_MODEL, 2 * D_MODEL), D_MODEL),
        "w_o_mem": w(ks[16], (L, D_MODEL, D_MODEL), D_MODEL),
        "ffn2_norm": gain(ks[17], (L, D_MODEL)),
        "ffn2_gate": w(ks[18], (L, D_MODEL, D_FF), D_MODEL),
        "ffn2_up": w(ks[19], (L, D_MODEL, D_FF), D_MODEL),
        "ffn2_down": w(ks[20], (L, D_FF, D_MODEL), D_FF),
        "final_norm": gain(ks[21], (D_MODEL,)),
    }


def reference(x, mem, ffn1_norm, ffn1_gate, ffn1_up, ffn1_down, mix_norm, w_in,
              lb_param, hgrn_out_norm, conv_w, w_out, xattn_norm, mem_norm,
              w_q_mem, w_kv_mem, w_o_mem, ffn2_norm, ffn2_gate, ffn2_up,
              ffn2_down, final_norm):
    b_, s_, _ = x.shape
    lb_all = jnp.cumsum(jax.nn.softmax(lb_param.astype(jnp.float32), axis=0), axis=0)

    for l in range(DEPTH):
        x = x + 0.5 * swiglu(rms_norm(x, ffn1_norm[l]), ffn1_gate[l], ffn1_up[l], ffn1_down[l])

        h = rms_norm(x, mix_norm[l])
        z = h @ w_in[l]
        o1 = HGRN_F; o2 = o1 + HGRN_F; o3 = o2 + HGRN_W; o4 = o3 + HGRN_W
        o5 = o4 + CONV_W; o6 = o5 + CONV_W
        zq, zf, zi, zg = z[..., :o1], z[..., o1:o2], z[..., o2:o3], z[..., o3:o4]
        zb, zc, zu = z[..., o4:o5], z[..., o5:o6], z[..., o6:]

        lb = lb_all[l]
        f = lb + (1.0 - lb) * jax.nn.sigmoid(zf.astype(jnp.float32))
        logf = jnp.log(f)
        kk = 1.0 - f
        q = jax.nn.silu(zq.astype(jnp.float32)) * HGRN_DK ** -0.5
        shp_k = (b_, s_, HGRN_HEADS, HGRN_DK)
        o_h = hgrn2_chunkwise(q.reshape(shp_k), kk.reshape(shp_k),
                              zi.reshape(b_, s_, HGRN_HEADS, HGRN_DV), logf.reshape(shp_k))
        g_h = hgrn_out_norm[l].astype(jnp.float32).reshape(HGRN_HEADS, HGRN_DV)
        o_h = o_h * lax.rsqrt(jnp.mean(o_h * o_h, axis=-1, keepdims=True) + EPS) * g_h
        y_hgrn = (o_h.reshape(b_, s_, HGRN_W) * jax.nn.silu(zg.astype(jnp.float32))).astype(x.dtype)

        y_conv = zb * causal_depthwise_conv(zc * zu, conv_w[l])

        x = x + jnp.concatenate([y_hgrn, y_conv.astype(x.dtype)], axis=-1) @ w_out[l]

        hq = rms_norm(x, xattn_norm[l])
        mn = rms_norm(mem, mem_norm[l])
        qm = (hq @ w_q_mem[l]).reshape(b_, s_, MEM_HEADS, MEM_HD)
        kv = mn @ w_kv_mem[l]
        km = kv[..., :D_MODEL].reshape(b_, MEM_LEN, MEM_HEADS, MEM_HD)
        vm = kv[..., D_MODEL:].reshape(b_, MEM_LEN, MEM_HEADS, MEM_HD)
        sc = jnp.einsum('bshd,bmhd->bhsm', qm.astype(jnp.float32), km.astype(jnp.float32)) * MEM_HD ** -0.5
        p = jax.nn.softmax(sc, axis=-1)
        att = jnp.einsum('bhsm,bmhd->bshd', p, vm.astype(jnp.float32)).astype(x.dtype)
        x = x + att.reshape(b_, s_, D_MODEL) @ w_o_mem[l]

        x = x + 0.5 * swiglu(rms_norm(x, ffn2_norm[l]), ffn2_gate[l], ffn2_up[l], ffn2_down[l])

    return rms_norm(x, final_norm)
```

```python
import numpy as np
from contextlib import ExitStack
import concourse.bass as bass
import concourse.mybir as mybir
from concourse.bass_utils import run_bass_kernel_spmd

F32 = mybir.dt.float32
BF16 = mybir.dt.bfloat16
AF = mybir.ActivationFunctionType
ALU = mybir.AluOpType
AX = mybir.AxisListType

D = 1024
DFF = 2816
TOK = 2048
NB = TOK // 128
EPS = 1e-6
PRE_TOK = 2048
STOP_AFTER = None


class _Src:
    def __init__(self, name, sem):
        self.name = name
        self.sem = sem
        self.count = 0


class _Eng(_Src):
    def __init__(self, name, sem):
        super().__init__(name, sem)
        self.items = []
        self.waited = {}


class Bf:
    __slots__ = ("ap", "lw", "rd")

    def __init__(self, ap):
        self.ap = ap
        self.lw = None
        self.rd = {}


class Sched:
    def __init__(self, nc, stack):
        self.nc = nc
        self.stack = stack
        self.srcs = []
        self.pe = self._eng("pe")
        self.act = self._eng("act")
        self.dve = self._eng("dve")
        self.pool = self._eng("pool")
        self.sp = self._eng("sp")
        self.engs = [self.pe, self.act, self.dve, self.pool, self.sp]

    def _eng(self, name):
        e = _Eng(name, self.stack.enter_context(self.nc.semaphore("s_" + name)))
        self.srcs.append(e)
        return e

    def slot(self, name):
        s = _Src(name, self.stack.enter_context(self.nc.semaphore("d_" + name)))
        self.srcs.append(s)
        return s

    def _wait(self, eng, src, val):
        if val <= 0 or eng.waited.get(src, 0) >= val:
            return
        eng.waited[src] = val
        eng.items.append(("w", src.sem, val))

    def op(self, eng, fn, reads=(), writes=()):
        for t in reads:
            if t.lw is not None:
                src, val = t.lw
                if src is eng and eng is self.pe:
                    continue
                self._wait(eng, src, val)
        for t in writes:
            if t.lw is not None and t.lw[0] is not eng:
                self._wait(eng, *t.lw)
            for src, val in t.rd.items():
                if src is not eng:
                    self._wait(eng, src, val)
        eng.count += 1
        eng.items.append(("o", fn, eng.sem, 1))
        for t in reads:
            t.rd[eng] = eng.count
        for t in writes:
            t.lw = (eng, eng.count)
            t.rd = {}

    def dma(self, q, slot, fn, reads=(), writes=()):
        for t in reads:
            if t.lw is not None:
                self._wait(q, *t.lw)
        for t in writes:
            if t.lw is not None:
                self._wait(q, *t.lw)
            for src, val in t.rd.items():
                self._wait(q, src, val)
        slot.count += 16
        q.items.append(("o", fn, slot.sem, 16))
        for t in reads:
            t.rd[slot] = slot.count
        for t in writes:
            t.lw = (slot, slot.count)
            t.rd = {}

    def barrier(self):
        for e in self.engs:
            for s in self.srcs:
                if s is not e:
                    self._wait(e, s, s.count)

    def finish(self):
        for s in self.srcs:
            if s is not self.sp:
                self._wait(self.sp, s, s.count)

    def replay(self, block):
        def run(items):
            def body(e):
                for it in items:
                    if it[0] == "w":
                        e.wait_ge(it[1], it[2])
                    else:
                        ins = it[1](e)
                        ins.then_inc(it[2], it[3])
            return body
        block.tensor(run(self.pe.items))
        block.scalar(run(self.act.items))
        block.vector(run(self.dve.items))
        block.gpsimd(run(self.pool.items))
        block.sync(run(self.sp.items))


def build_nc(pre_tok=PRE_TOK, stop_after=STOP_AFTER):
    nc = bass.Bass("TRN2", target_bir_lowering=False)
    npb = pre_tok // 128

    def din(name, shape):
        return nc.dram_tensor(name, list(shape), F32, kind="ExternalInput").ap()

    x_d = din("x", [TOK, D])
    xp_d = din("xp", [max(pre_tok, 128), D])
    mem_d = din("mem", [256, D])
    g_ffn1 = din("ffn1_norm", [1, D]); g_mix = din("mix_norm", [1, D]); g_xat = din("xattn_norm", [1, D])
    g_mem = din("mem_norm", [1, D]); g_ffn2 = din("ffn2_norm", [1, D]); g_fin = din("final_norm", [1, D])
    w1g = din("ffn1_gate", [D, DFF]); w1u = din("ffn1_up", [D, DFF]); w1d = din("ffn1_down", [DFF, D])
    w2g = din("ffn2_gate", [D, DFF]); w2u = din("ffn2_up", [D, DFF]); w2d = din("ffn2_down", [DFF, D])
    w_in = din("w_in", [D, 3584]); w_out = din("w_out", [D, D])
    w_q = din("w_q_mem", [D, D]); w_kv = din("w_kv_mem", [D, 2 * D]); w_o = din("w_o_mem", [D, D])
    lbp_d = din("lb_param", [2, 512]); hg_d = din("hg", [128, 4]); cw_d = din("conv_w", [512, 3])
    tri_d = din("c_tri", [128, 128]); trr_d = din("c_trirev", [128, 128]); cind_d = din("c_cind", [128, 2])
    idn_d = din("c_ident", [128, 128])
    out_d = nc.dram_tensor("out", [TOK, D], F32, kind="ExternalOutput").ap()

    with ExitStack() as st:
        sch = Sched(nc, st)
        sb = lambda name, shape, dt: st.enter_context(nc.sbuf_tensor(name, list(shape), dt))
        x_sb = sb("x_sb", [128, NB, D], F32)
        ring = sb("ring", [128, 6, 2048], BF16)
        a16 = sb("a16", [128, 24576], BF16)
        a32 = sb("a32", [128, 11264], F32)
        gb_sb = sb("gb", [128, D], F32)
        oml_sb = sb("oml", [128, 512], F32)
        lbp_sb = sb("lbp", [128, 2, 512], F32)
        mask_sb = sb("maskb", [128, 4, 128], F32)
        tri_sb = sb("tri", [128, 128], F32)
        trr_sb = sb("trr", [128, 128], F32)
        cind_sb = sb("cind", [128, 2], F32)
        idf_sb = sb("idf", [128, 128], F32)
        idb_sb = sb("idb", [128, 128], BF16)
        one_sb = sb("ones", [128, 128], BF16)
        hg_sb = sb("hgs", [128, 4], F32)
        cw_sb = sb("cws", [128, 4, 3], F32)
        ss_sb = sb("ss", [128, 64], F32)
        sm_sb = sb("smx", [128, 64], F32)
        dec_sb = sb("dec", [128, 4, 8], F32)
        psf = [st.enter_context(nc.psum_tensor("psf%d" % i, [128, 512], F32)) for i in range(6)]
        pst = [st.enter_context(nc.psum_tensor("pst%d" % i, [128, 1024], BF16)) for i in range(2)]
        block = st.enter_context(nc.Block())

        PE, ACT, DVE, POOL, SP = sch.pe, sch.act, sch.dve, sch.pool, sch.sp

        xb = [Bf(x_sb[:, b, :]) for b in range(NB)]
        ringb = [Bf(ring[:, u, :]) for u in range(6)]
        ring_slot = [sch.slot("ring%d" % u) for u in range(6)]
        xl_slot = [sch.slot("xl%d" % b) for b in range(NB)]
        st_slot = [sch.slot("st%d" % b) for b in range(NB)]
        c_slot = sch.slot("const")
        g_slot = sch.slot("gain")
        PSF = [Bf(p[:]) for p in psf]
        PST = [Bf(p[:]) for p in pst]
        gb = Bf(gb_sb[:]); oml = Bf(oml_sb[:]); lbp = Bf(lbp_sb[:]); maskb = Bf(mask_sb[:])
        tri = Bf(tri_sb[:]); trr = Bf(trr_sb[:]); cind = Bf(cind_sb[:]); idf = Bf(idf_sb[:])
        idb = Bf(idb_sb[:]); ones = Bf(one_sb[:]); hg = Bf(hg_sb[:]); cw = Bf(cw_sb[:])
        ss = Bf(ss_sb[:]); sm = Bf(sm_sb[:]); dec = Bf(dec_sb[:])
        cnt = {"psf": 0, "pst": 0, "ring": 0}

        def nps():
            cnt["psf"] += 1
            return PSF[cnt["psf"] % 6]

        def npt():
            cnt["pst"] += 1
            return PST[cnt["pst"] % 2]

        def mmg(groups):
            def fn(e):
                ins = None
                for out, pairs in groups:
                    n = len(pairs)
                    for i, (l, r) in enumerate(pairs):
                        ins = e.matmul(out, l, r, start=(i == 0), stop=(i == n - 1))
                return ins
            return fn

        def pe_mm(groups, reads, writes):
            sch.op(PE, mmg(groups), reads, writes)

        def pe_tr(pairs, reads, writes):
            def fn(e):
                ins = None
                for o, i in pairs:
                    ins = e.transpose(o, i, idb.ap)
                return ins
            sch.op(PE, fn, list(reads) + [idb], writes)

        def act(out, in_, func, reads, writes, **kw):
            sch.op(ACT, lambda e: e.activation(out, in_, func, **kw), reads, writes)

        def tt(out, a, b, op, reads, writes, eng=None):
            sch.op(eng or DVE, lambda e: e.tensor_tensor(out, a, b, op), reads, writes)

        def stt(out, a, s, b, op0, op1, reads, writes, eng=None):
            sch.op(eng or DVE, lambda e: e.scalar_tensor_tensor(out, a, s, b, op0, op1), reads, writes)

        def ts(out, a, s1, s2, op0, op1, reads, writes, eng=None):
            if s2 is None:
                sch.op(eng or DVE, lambda e: e.tensor_scalar(out, a, s1, None, op0), reads, writes)
            else:
                sch.op(eng or DVE, lambda e: e.tensor_scalar(out, a, s1, s2, op0, op1), reads, writes)

        def cp(out, a, reads, writes, eng=None):
            if eng is ACT:
                sch.op(ACT, lambda e: e.activation(out, a, AF.Copy), reads, writes)
            else:
                sch.op(eng or DVE, lambda e: e.tensor_copy(out, a), reads, writes)

        def ms(out, val, writes, eng=None):
            sch.op(eng or DVE, lambda e: e.memset(out, val), (), writes)

        def wload(dst_ap, src_ap):
            u = cnt["ring"] % 6
            cnt["ring"] += 1
            rb = ringb[u]
            sch.dma(POOL, ring_slot[u], lambda e: e.dma_start(out=dst_ap(ring[:, u, :]), in_=src_ap), (), [rb])
            return rb

        def w_cols(wd, c0, n=256):
            src = wd[:, c0:c0 + n].rearrange("(k p) n -> p k n", p=128)
            rb = wload(lambda r: r[:, 0:8 * n].rearrange("p (k n) -> p k n", k=8), src)
            return rb, rb.ap[:, 0:8 * n].rearrange("p (k n) -> p k n", k=8)

        def w_rows(wd, r0):
            src = wd[r0:r0 + 256, :].rearrange("(j p) n -> p j n", p=128)
            rb = wload(lambda r: r.rearrange("p (j n) -> p j n", j=2), src)
            return rb, rb.ap.rearrange("p (j n) -> p j n", j=2)

        def cdma(dst, src, b):
            sch.dma(SP, c_slot, lambda e: e.dma_start(out=dst, in_=src), (), [b])
        cdma(tri.ap, tri_d, tri); cdma(trr.ap, trr_d, trr); cdma(cind.ap, cind_d, cind); cdma(idf.ap, idn_d, idf)
        for h in range(4):
            cdma(mask_sb[:, h, :], tri_d, maskb)
        cdma(hg.ap, hg_d, hg)
        cdma(cw.ap, cw_d.rearrange("(c p) j -> p c j", p=128), cw)
        cdma(lbp_sb[:, 0, :], lbp_d[0:1, :].partition_broadcast(128), lbp)
        cdma(lbp_sb[:, 1, :], lbp_d[1:2, :].partition_broadcast(128), lbp)
        for b_ in (tri, trr, cind, idf, maskb, hg, cw, lbp):
            b_.lw = (c_slot, c_slot.count)
        cp(idb.ap, idf.ap, [idf], [idb])
        ms(ones.ap, 1.0, [ones])
        tt(oml.ap, lbp_sb[:, 1, :], lbp_sb[:, 0, :], ALU.subtract, [lbp], [oml])
        act(oml.ap, oml.ap, AF.Sigmoid, [oml], [oml])

        def load_gain(gd):
            sch.dma(SP, g_slot, lambda e: e.dma_start(out=gb.ap, in_=gd.partition_broadcast(128)), (), [gb])

        def load_x(src, nblk):
            for b in range(nblk):
                sch.dma(SP, xl_slot[b],
                        (lambda b: lambda e: e.dma_start(out=xb[b].ap, in_=src[b * 128:(b + 1) * 128, :]))(b),
                        (), [xb[b]])

        def c16(off, shape):
            n = int(np.prod(shape[1:]))
            ap = a16[:, off:off + n]
            if len(shape) == 3:
                ap = ap.rearrange("p (a b) -> p a b", a=shape[1])
            return ap

        def c32(off, shape):
            n = int(np.prod(shape[1:]))
            ap = a32[:, off:off + n]
            if len(shape) == 3:
                ap = ap.rearrange("p (a b) -> p a b", a=shape[1])
            return ap

        def norm_T(srcs, hT_ap, hT_bufs, hn_bufs, junk, col0=0):
            n = len(srcs)
            ms(ss_sb[:, 0:2 * n], 0.0, [ss])
            for j, s in enumerate(srcs):
                act(junk.ap, s.ap, AF.Square, [s], [junk, ss], accum_out=ss_sb[:, j:j + 1])
            act(ss_sb[:, n:2 * n], ss_sb[:, 0:n], AF.Sqrt, [ss], [ss], scale=1.0 / D, bias=EPS)
            sch.op(DVE, lambda e: e.reciprocal(ss_sb[:, 0:n], ss_sb[:, n:2 * n]), [ss], [ss])
            for j, s in enumerate(srcs):
                hn = hn_bufs[j % len(hn_bufs)]
                stt(hn.ap, s.ap, ss_sb[:, j:j + 1], gb.ap, ALU.mult, ALU.mult, [s, ss, gb], [hn])
                pt = npt()
                pe_tr([(pt.ap[:, k * 128:(k + 1) * 128], hn.ap[:, k * 128:(k + 1) * 128]) for k in range(8)], [hn], [pt])
                cp(hT_ap[:, :, col0 + j * 128:col0 + (j + 1) * 128], pt.ap.rearrange("p (k n) -> p k n", k=8),
                   [pt], [hT_bufs[j]], eng=ACT if j % 2 else DVE)

        def ffn(gd, wg, wu, wd, nblk):
            hT_ap = c16(0, [128, 8, 2048])
            hTb = [Bf(hT_ap[:, :, b * 128:(b + 1) * 128]) for b in range(nblk)]
            hid_ap = [c16(16384 + i * 4096, [128, 2, 2048]) for i in range(2)]
            hidb = [[[Bf(hid_ap[i][:, j, t * 512:(t + 1) * 512]) for t in range(4)] for j in range(2)] for i in range(2)]
            hn = [Bf(c32(i * 1024, [128, 1024]).bitcast(BF16)[:, 0:1024]) for i in range(2)]
            junk = Bf(c32(2048, [128, 1024]).bitcast(BF16)[:, 0:1024])
            sg = [Bf(c32(3072 + i * 512, [128, 512])) for i in range(3)]
            load_gain(gd)
            norm_T(xb[:nblk], hT_ap, hTb, hn, junk)
            ntt = nblk // 4
            NG = DFF // 256
            units = {}

            def loadg(g):
                units[g] = (w_cols(wg, g * 256), w_cols(wu, g * 256), w_rows(wd, g * 256))

            def gu(g):
                (gb_, gap), (ub_, uap), _ = units[g]
                i = g % 2
                for j in range(2):
                    for t in range(ntt):
                        pg = nps(); pu = nps()
                        rd = [gb_, ub_] + hTb[t * 4:t * 4 + 4]
                        pe_mm([(pg.ap, [(gap[:, k, j * 128:(j + 1) * 128], hT_ap[:, k, t * 512:(t + 1) * 512]) for k in range(8)])], rd, [pg])
                        pe_mm([(pu.ap, [(uap[:, k, j * 128:(j + 1) * 128], hT_ap[:, k, t * 512:(t + 1) * 512]) for k in range(8)])], rd, [pu])
                        s = sg[(j * ntt + t) % 3]
                        act(s.ap, pg.ap, AF.Silu, [pg], [s])
                        tt(hidb[i][j][t].ap, s.ap, pu.ap, ALU.mult, [s, pu], [hidb[i][j][t]])

            def down(g):
                _, _, (db_, dap) = units[g]
                i = g % 2
                for b in range(nblk):
                    for ch in range(2):
                        pd = nps()
                        pe_mm([(pd.ap, [(hid_ap[i][:, j, b * 128:(b + 1) * 128], dap[:, j, ch * 512:(ch + 1) * 512]) for j in range(2)])],
                              [db_, hidb[i][0][b // 4], hidb[i][1][b // 4]], [pd])
                        xs = xb[b].ap[:, ch * 512:(ch + 1) * 512]
                        stt(xs, pd.ap, 0.5, xs, ALU.mult, ALU.add, [pd, xb[b]], [xb[b]])
                del units[g]

            loadg(0)
            loadg(1)
            gu(0)
            for g in range(NG):
                if g + 1 < NG:
                    gu(g + 1)
                down(g)
                if g + 2 < NG:
                    loadg(g + 2)

        st_ = {"gc": 0}

        def mix_setup():
            m = {}
            m["hT"] = c16(0, [128, 8, 512]); m["hTb"] = [Bf(m["hT"][:, :, b * 128:(b + 1) * 128]) for b in range(4)]
            m["t16"] = [Bf(c16(4096 + i * 512, [128, 512])) for i in range(5)]
            m["v"] = [Bf(c16(4096 + 2560 + i * 512, [128, 512])) for i in range(4)]
            m["ktT"] = c16(8704, [128, 4, 512]); m["ktTb"] = [Bf(m["ktT"][:, :, b * 128:(b + 1) * 128]) for b in range(4)]
            m["qT"] = c16(10752, [128, 4, 512]); m["qTb"] = [Bf(m["qT"][:, h, :]) for h in range(4)]
            m["PT"] = [Bf(c16(12800 + i * 512, [128, 4, 128])) for i in range(2)]
            m["Sbf"] = [Bf(c16(13824 + i * 512, [128, 512])) for i in range(9)]
            m["GT"] = c16(18432, [128, 4, 512]); m["GTb"] = [Bf(m["GT"][:, h, :]) for h in range(4)]
            m["yT"] = c16(20480, [128, 8, 512]); m["yTb"] = [Bf(m["yT"][:, c, :]) for c in range(8)]
            m["t32"] = [Bf(c32(i * 512, [128, 512])) for i in range(7)]
            m["ET"] = c32(3584, [128, 4, 512]); m["ETb"] = [Bf(m["ET"][:, :, b * 128:(b + 1) * 128]) for b in range(4)]
            m["S"] = [Bf(c32(5632 + i * 512, [128, 512])) for i in range(2)]
            m["u"] = c32(6656, [128, 4, 514]); m["ub"] = [Bf(m["u"][:, c, :]) for c in range(4)]
            m["hn"] = [Bf(c32(8712 + i * 512, [128, 512]).bitcast(BF16)) for i in range(2)]
            m["junk"] = Bf(c32(9736, [128, 512]).bitcast(BF16))
            m["i16"] = 0; m["i32"] = 0
            return m

        def t16(m):
            m["i16"] += 1
            return m["t16"][m["i16"] % 5]

        def t32(m):
            m["i32"] += 1
            return m["t32"][m["i32"] % 7]

        def mix_init_state(m):
            ms(m["S"][0].ap, 0.0, [m["S"][0]])
            ms(m["Sbf"][0].ap, 0.0, [m["Sbf"][0]])
            ms(m["u"], 0.0, m["ub"])
            st_["gc"] = 0
            st_["si"] = 0

        def mix_restore_state(m):
            sb_c = m["Sbf"][st_["gc"] % 9]
            cp(sb_c.ap, m["S"][st_["si"] % 2].ap, [m["S"][st_["si"] % 2]], [sb_c], eng=ACT)

        def fm_proj(m, c0, nchunks, consume):
            for q in range(0, nchunks, 2):
                rb, wap = w_cols(w_in, c0 + q * 128)
                for j in range(2):
                    p = nps()
                    pe_mm([(p.ap, [(wap[:, k, j * 128:(j + 1) * 128], m["hT"][:, k, :]) for k in range(8)])],
                          [rb] + m["hTb"], [p])
                    consume(q + j, p)

        def mix_tile(m, blks, state_only, want_u):
            load_gain(g_mix)
            norm_T(blks, m["hT"], m["hTb"], m["hn"], m["junk"])
            hT = m["hT"]
            wf = [w_cols(w_in, 512), w_cols(w_in, 768)]
            wi = [w_cols(w_in, 1024), w_cols(w_in, 1280)]
            vbs = []; Sidx = []
            for b in range(4):
                tb = slice(b * 128, (b + 1) * 128)
                pzf = nps(); pzi = nps()
                pe_mm([(pzf.ap[:, q * 256:(q + 1) * 256], [(hT[:, k, tb], wf[q][1][:, k, :]) for k in range(8)]) for q in range(2)],
                      [wf[0][0], wf[1][0], m["hTb"][b]], [pzf])
                pe_mm([(pzi.ap[:, q * 256:(q + 1) * 256], [(hT[:, k, tb], wi[q][1][:, k, :]) for k in range(8)]) for q in range(2)],
                      [wi[0][0], wi[1][0], m["hTb"][b]], [pzi])
                sneg = t32(m)
                act(sneg.ap, pzf.ap, AF.Sigmoid, [pzf], [sneg], scale=-1.0)
                k32 = t32(m)
                tt(k32.ap, sneg.ap, oml.ap, ALU.mult, [sneg, oml], [k32])
                logf = t32(m)
                act(logf.ap, k32.ap, AF.Ln, [k32], [logf], scale=-1.0, bias=1.0)
                v = m["v"][b]
                act(v.ap, pzi.ap, AF.Copy, [pzi], [v])
                vbs.append(v)
                pBr = nps()
                pe_mm([(pBr.ap, [(trr.ap, logf.ap)])], [trr, logf], [pBr])
                pDc = nps()
                pe_mm([(pDc.ap[:, h * 2:(h + 1) * 2], [(logf.ap[:, h * 128:(h + 1) * 128], cind.ap)]) for h in range(4)],
                      [cind, logf], [pDc])
                eR = t32(m)
                act(eR.ap, pBr.ap, AF.Exp, [pBr], [eR])
                kh = t16(m)
                tt(kh.ap, k32.ap, eR.ap, ALU.mult, [k32, eR], [kh])
                dsl = dec_sb[:, b, :]
                act(dsl, pDc.ap[:, 0:8], AF.Exp, [pDc], [dec])
                if not state_only:
                    pB = nps()
                    pe_mm([(pB.ap, [(tri.ap, logf.ap)])], [tri, logf], [pB])
                    pBT = nps()
                    pe_mm([(pBT.ap[:, h * 128:(h + 1) * 128], [(logf.ap[:, h * 128:(h + 1) * 128], tri.ap)]) for h in range(4)],
                          [tri, logf], [pBT])
                    eB = t32(m)
                    act(eB.ap, pB.ap, AF.Exp, [pB], [eB], scale=-1.0)
                    kt = t16(m)
                    tt(kt.ap, k32.ap, eB.ap, ALU.mult, [k32, eB], [kt])
                    pt = npt()
                    pe_tr([(pt.ap[:, h * 128:(h + 1) * 128], kt.ap[:, h * 128:(h + 1) * 128]) for h in range(4)], [kt], [pt])
                    cp(m["ktT"][:, :, tb], pt.ap[:, 0:512].rearrange("p (h n) -> p h n", h=4), [pt], [m["ktTb"][b]])
                    act(m["ET"][:, :, tb], pBT.ap.rearrange("p (h n) -> p h n", h=4), AF.Exp, [pBT], [m["ETb"][b]])
                for c in range(2):
                    cs = slice(c * 64, (c + 1) * 64)
                    pS = nps()
                    pe_mm([(pS.ap[:, h * 128:(h + 1) * 128], [(kh.ap[cs, h * 128:(h + 1) * 128], v.ap[cs, h * 128:(h + 1) * 128])]) for h in range(4)],
                          [kh, v], [pS])
                    So = m["S"][st_["si"] % 2]; Sn = m["S"][(st_["si"] + 1) % 2]
                    st_["si"] += 1
                    for h in range(4):
                        hs = slice(h * 128, (h + 1) * 128)
                        stt(Sn.ap[:, hs], So.ap[:, hs], dec_sb[:, b, h * 2 + c:h * 2 + c + 1], pS.ap[:, hs], ALU.mult, ALU.add,
                            [So, dec, pS], [Sn])
                    Sidx.append(st_["gc"] % 9)
                    st_["gc"] += 1
                    sb_n = m["Sbf"][st_["gc"] % 9]
                    cp(sb_n.ap, Sn.ap, [Sn], [sb_n], eng=ACT)
            if state_only and not want_u:
                return

            pzc = {}

            def take_c(c, p):
                zcs = t32(m)
                act(zcs.ap, p.ap, AF.Copy, [p], [zcs])
                pzc[c] = zcs

            def take_u(c, p):
                tt(m["u"][:, c, 2:514], pzc[c].ap, p.ap, ALU.mult, [pzc[c], p], [m["ub"][c]])

            def halo(c):
                cp(m["u"][:, c, 0:2], m["u"][:, c, 512:514], [m["ub"][c]], [m["ub"][c]])

            if state_only:
                fm_proj(m, 2560, 4, take_c)
                fm_proj(m, 3072, 4, take_u)
                for c in range(4):
                    halo(c)
                return

            def take_q(h, p):
                sq = t32(m)
                act(sq.ap, p.ap, AF.Silu, [p], [sq])
                stt(m["qT"][:, h, :], sq.ap, 128.0 ** -0.5, m["ET"][:, h, :], ALU.mult, ALU.mult, [sq] + m["ETb"], [m["qTb"][h]])
            fm_proj(m, 0, 4, take_q)

            def take_g(h, p):
                act(m["GT"][:, h, :], p.ap, AF.Silu, [p], [m["GTb"][h]])
            fm_proj(m, 1536, 4, take_g)

            for b in range(4):
                tb = slice(b * 128, (b + 1) * 128)
                v = vbs[b]
                pSc = nps()
                pe_mm([(pSc.ap[:, h * 128:(h + 1) * 128], [(m["ktT"][:, h, tb], m["qT"][:, h, tb])]) for h in range(4)],
                      [m["ktTb"][b]] + m["qTb"], [pSc])
                PT = m["PT"][b % 2]
                tt(PT.ap, pSc.ap.rearrange("p (h n) -> p h n", h=4), maskb.ap, ALU.mult, [pSc, maskb], [PT])
                pO = nps()
                groups = []
                rds = [v, PT] + m["qTb"]
                for h in range(4):
                    for c in range(2):
                        cs = slice(c * 64, (c + 1) * 64)
                        sbf = m["Sbf"][Sidx[b * 2 + c]]
                        rds.append(sbf)
                        groups.append((pO.ap[:, h * 128 + c * 64:h * 128 + (c + 1) * 64],
                                       [(v.ap[cs, h * 128:(h + 1) * 128], PT.ap[cs, h, c * 64:(c + 1) * 64]),
                                        (sbf.ap[:, h * 128:(h + 1) * 128], m["qT"][:, h, b * 128 + c * 64:b * 128 + (c + 1) * 64])]))
                pe_mm(groups, rds, [pO])
                sqo = t16(m)
                act(sqo.ap, pO.ap, AF.Square, [pO], [sqo])
                pN = nps()
                pe_mm([(pN.ap[:, h * 128:(h + 1) * 128], [(ones.ap, sqo.ap[:, h * 128:(h + 1) * 128])]) for h in range(4)], [ones, sqo], [pN])
                rt = t32(m)
                act(rt.ap, pN.ap, AF.Sqrt, [pN], [rt], scale=1.0 / 128, bias=EPS)
                rs = t32(m)
                sch.op(DVE, (lambda o, i: lambda e: e.reciprocal(o, i))(rs.ap, rt.ap), [rt], [rs])
                y1 = t32(m)
                tt(y1.ap, pO.ap, rs.ap, ALU.mult, [pO, rs], [y1])
                for h in range(4):
                    stt(m["yT"][:, h, tb], y1.ap[:, h * 128:(h + 1) * 128], hg_sb[:, h:h + 1], m["GT"][:, h, tb], ALU.mult, ALU.mult,
                        [y1, hg, m["GTb"][h]], [m["yTb"][h]])

            fm_proj(m, 2560, 4, take_c)
            fm_proj(m, 3072, 4, take_u)

            def take_b(c, p):
                a1 = t32(m)
                ts(a1.ap, m["u"][:, c, 2:514], cw_sb[:, c, 2:3], None, ALU.mult, None, [m["ub"][c], cw], [a1])
                a2 = t32(m)
                stt(a2.ap, m["u"][:, c, 1:513], cw_sb[:, c, 1:2], a1.ap, ALU.mult, ALU.add, [m["ub"][c], cw, a1], [a2])
                a3 = t32(m)
                stt(a3.ap, m["u"][:, c, 0:512], cw_sb[:, c, 0:1], a2.ap, ALU.mult, ALU.add, [m["ub"][c], cw, a2], [a3])
                tt(m["yT"][:, 4 + c, :], a3.ap, p.ap, ALU.mult, [a3, p], [m["yTb"][4 + c]])
                halo(c)
            fm_proj(m, 2048, 4, take_b)

            for q in range(4):
                rb, wap = w_cols(w_out, q * 256)
                for b in range(4):
                    tb = slice(b * 128, (b + 1) * 128)
                    p = nps()
                    pe_mm([(p.ap[:, 0:256], [(m["yT"][:, fc, tb], wap[:, fc, :]) for fc in range(8)])], [rb] + m["yTb"], [p])
                    xs = blks[b].ap[:, q * 256:(q + 1) * 256]
                    tt(xs, p.ap[:, 0:256], xs, ALU.add, [p, blks[b]], [blks[b]])

        def xattn():
            hT = c16(0, [128, 8, 512]); hTb = [Bf(hT[:, :, b * 128:(b + 1) * 128]) for b in range(4)]
            qmT = c16(4096, [128, 8, 512]); qmTb = [Bf(qmT[:, c, :]) for c in range(8)]
            kmT = c16(8192, [128, 8, 256]); kmTb = Bf(kmT)
            vm = c16(10240, [128, 2, 1024]); vmb = Bf(vm)
            mnT = c16(12288, [128, 8, 256]); mnTb = [Bf(mnT[:, :, b * 128:(b + 1) * 128]) for b in range(2)]
            pbuf = [Bf(c16(14336 + i * 1024, [128, 4, 256])) for i in range(2)]
            pTb = [Bf(c16(16384 + i * 1024, [128, 8, 128])) for i in range(2)]
            attT = c16(18432, [128, 8, 512]); attTb = [Bf(attT[:, :, b * 128:(b + 1) * 128]) for b in range(4)]
            hn = [Bf(c32(i * 512, [128, 512]).bitcast(BF16)) for i in range(2)]
            junk = Bf(c32(1024, [128, 512]).bitcast(BF16))
            memx = [Bf(c32(1536 + i * 1024, [128, 1024])) for i in range(2)]
            pe_ = [Bf(c32(3584 + i * 1024, [128, 4, 256])) for i in range(2)]
            scale = 256.0 ** -0.5
            for i in range(2):
                sch.dma(SP, xl_slot[i], (lambda i: lambda e: e.dma_start(out=memx[i].ap, in_=mem_d[i * 128:(i + 1) * 128, :]))(i), (), [memx[i]])
            load_gain(g_mem)
            norm_T(memx, mnT, mnTb, hn, junk)
            for q in range(4):
                rb, wap = w_cols(w_kv, q * 256)
                for j in range(2):
                    p = nps()
                    pe_mm([(p.ap[:, 0:256], [(wap[:, k, j * 128:(j + 1) * 128], mnT[:, k, :]) for k in range(8)])], [rb] + mnTb, [p])
                    cp(kmT[:, q * 2 + j, :], p.ap[:, 0:256], [p], [kmTb], eng=ACT)
            for q in range(4):
                rb, wap = w_cols(w_kv, 1024 + q * 256)
                for mb in range(2):
                    p = nps()
                    pe_mm([(p.ap[:, 0:256], [(mnT[:, k, mb * 128:(mb + 1) * 128], wap[:, k, :]) for k in range(8)])], [rb] + mnTb, [p])
                    cp(vm[:, mb, q * 256:(q + 1) * 256], p.ap[:, 0:256], [p], [vmb], eng=ACT)
            for ti in range(4):
                blks = xb[ti * 4:ti * 4 + 4]
                load_gain(g_xat)
                norm_T(blks, hT, hTb, hn, junk)
                for q in range(4):
                    rb, wap = w_cols(w_q, q * 256)
                    for j in range(2):
                        p = nps()
                        pe_mm([(p.ap, [(wap[:, k, j * 128:(j + 1) * 128], hT[:, k, :]) for k in range(8)])], [rb] + hTb, [p])
                        cp(qmT[:, q * 2 + j, :], p.ap, [p], [qmTb[q * 2 + j]], eng=ACT if j else DVE)
                for b in range(4):
                    tb = slice(b * 128, (b + 1) * 128)
                    ms(sm_sb[:, 0:4], 0.0, [sm])
                    pe_b = pe_[b % 2]
                    pscs = []
                    for hp in range(2):
                        p = nps()
                        pe_mm([(p.ap[:, hh * 256:(hh + 1) * 256],
                                [(qmT[:, (hp * 2 + hh) * 2 + dc, tb], kmT[:, (hp * 2 + hh) * 2 + dc, :]) for dc in range(2)]) for hh in range(2)],
                              qmTb + [kmTb], [p])
                        sch.op(DVE, (lambda o, i: lambda e: e.tensor_reduce(o, i, AX.X, ALU.max))(
                            sm_sb[:, 8 + hp * 2:8 + hp * 2 + 2], p.ap.rearrange("p (h n) -> p h n", h=2)), [p], [sm])
                        pscs.append(p)
                    ts(sm_sb[:, 16:20], sm_sb[:, 8:12], -scale, None, ALU.mult, None, [sm], [sm])
                    for h in range(4):
                        p = pscs[h // 2]
                        act(pe_b.ap[:, h, :], p.ap[:, (h % 2) * 256:(h % 2 + 1) * 256], AF.Exp, [p, sm], [pe_b, sm],
                            scale=scale, bias=sm_sb[:, 16 + h:17 + h], accum_out=sm_sb[:, h:h + 1])
                    sch.op(DVE, lambda e: e.reciprocal(sm_sb[:, 24:28], sm_sb[:, 0:4]), [sm], [sm])
                    pb = pbuf[b % 2]
                    for h in range(4):
                        ts(pb.ap[:, h, :], pe_b.ap[:, h, :], sm_sb[:, 24 + h:25 + h], None, ALU.mult, None, [pe_b, sm], [pb])
                    pt = npt()
                    pe_tr([(pt.ap[:, (h * 2 + mc) * 128:(h * 2 + mc + 1) * 128], pb.ap[:, h, mc * 128:(mc + 1) * 128])
                           for h in range(4) for mc in range(2)], [pb], [pt])
                    pT = pTb[b % 2]
                    cp(pT.ap, pt.ap.rearrange("p (k n) -> p k n", k=8), [pt], [pT], eng=ACT)
                    pA = nps()
                    pe_mm([(pA.ap[:, ci * 128:(ci + 1) * 128],
                            [(vm[:, mc, ci * 128:(ci + 1) * 128], pT.ap[:, (ci // 2) * 2 + mc, :]) for mc in range(2)]) for ci in range(4)],
                          [vmb, pT], [pA])
                    pA2 = nps()
                    pe_mm([(pA2.ap[:, (ci - 4) * 128:(ci - 3) * 128],
                            [(vm[:, mc, ci * 128:(ci + 1) * 128], pT.ap[:, (ci // 2) * 2 + mc, :]) for mc in range(2)]) for ci in range(4, 8)],
                          [vmb, pT], [pA2])
                    cp(attT[:, 0:4, tb], pA.ap.rearrange("p (k n) -> p k n", k=4), [pA], [attTb[b]], eng=ACT)
                    cp(attT[:, 4:8, tb], pA2.ap.rearrange("p (k n) -> p k n", k=4), [pA2], [attTb[b]])
                for q in range(4):
                    rb, wap = w_cols(w_o, q * 256)
                    for b in range(4):
                        tb = slice(b * 128, (b + 1) * 128)
                        p = nps()
                        pe_mm([(p.ap[:, 0:256], [(attT[:, fc, tb], wap[:, fc, :]) for fc in range(8)])], [rb] + attTb, [p])
                        xs = blks[b].ap[:, q * 256:(q + 1) * 256]
                        tt(xs, p.ap[:, 0:256], xs, ALU.add, [p, blks[b]], [blks[b]])

        def final(do_norm=True):
            ob = [Bf(c32(i * 1024, [128, 1024])) for i in range(2)]
            junk = Bf(c32(2048, [128, 512]).bitcast(BF16))
            if do_norm:
                load_gain(g_fin)
                ms(ss_sb[:, 0:32], 0.0, [ss])
                for b in range(NB):
                    act(junk.ap, xb[b].ap, AF.Square, [xb[b]], [junk, ss], accum_out=ss_sb[:, b:b + 1])
                act(ss_sb[:, 16:32], ss_sb[:, 0:16], AF.Sqrt, [ss], [ss], scale=1.0 / D, bias=EPS)
                sch.op(DVE, lambda e: e.reciprocal(ss_sb[:, 0:16], ss_sb[:, 16:32]), [ss], [ss])
            for b in range(NB):
                o = ob[b % 2]
                if do_norm:
                    stt(o.ap, xb[b].ap, ss_sb[:, b:b + 1], gb.ap, ALU.mult, ALU.mult, [xb[b], ss, gb], [o])
                else:
                    cp(o.ap, xb[b].ap, [xb[b]], [o])
                sch.dma(SP, st_slot[b], (lambda b, o: lambda e: e.dma_start(out=out_d[b * 128:(b + 1) * 128, :], in_=o.ap))(b, o), [o], ())

        m = None
        if pre_tok > 0 and (stop_after is None or stop_after >= 2):
            load_x(xp_d, npb)
            ffn(g_ffn1, w1g, w1u, w1d, npb)
            sch.barrier()
            m = mix_setup()
            mix_init_state(m)
            for ti in range(npb // 4):
                mix_tile(m, xb[ti * 4:ti * 4 + 4], True, ti == npb // 4 - 1)
            sch.barrier()
        load_x(x_d, NB)
        ffn(g_ffn1, w1g, w1u, w1d, NB)
        sch.barrier()
        if stop_after is None or stop_after >= 2:
            if m is None:
                m = mix_setup()
                mix_init_state(m)
            else:
                mix_restore_state(m)
            for ti in range(4):
                mix_tile(m, xb[ti * 4:ti * 4 + 4], False, False)
            sch.barrier()
        if stop_after is None or stop_after >= 3:
            xattn()
            sch.barrier()
        if stop_after is None or stop_after >= 4:
            ffn(g_ffn2, w2g, w2u, w2d, NB)
            sch.barrier()
        final(do_norm=(stop_after is None))
        sch.finish()
        sch.replay(block)
    return nc


def _consts():
    i = np.arange(128)
    same = (i[:, None] // 64) == (i[None, :] // 64)
    tri = (same & (i[:, None] <= i[None, :])).astype(np.float32)
    trirev = (same & (i[:, None] > i[None, :])).astype(np.float32)
    cind = np.stack([(i < 64), (i >= 64)], axis=1).astype(np.float32)
    return tri, trirev, cind, np.eye(128, dtype=np.float32)


def make_in_maps(inputs, pre_tok=PRE_TOK):
    f = lambda k: np.ascontiguousarray(np.asarray(inputs[k], dtype=np.float32))
    x = f("x"); mem = f("mem")
    tri, trirev, cind, ident = _consts()
    shared = {
        "ffn1_norm": f("ffn1_norm").reshape(1, D), "mix_norm": f("mix_norm").reshape(1, D),
        "xattn_norm": f("xattn_norm").reshape(1, D), "mem_norm": f("mem_norm").reshape(1, D),
        "ffn2_norm": f("ffn2_norm").reshape(1, D), "final_norm": f("final_norm").reshape(1, D),
        "ffn1_gate": f("ffn1_gate")[0], "ffn1_up": f("ffn1_up")[0], "ffn1_down": f("ffn1_down")[0],
        "ffn2_gate": f("ffn2_gate")[0], "ffn2_up": f("ffn2_up")[0], "ffn2_down": f("ffn2_down")[0],
        "w_in": f("w_in")[0], "w_out": f("w_out")[0], "w_q_mem": f("w_q_mem")[0],
        "w_kv_mem": f("w_kv_mem")[0], "w_o_mem": f("w_o_mem")[0],
        "lb_param": f("lb_param"), "hg": np.ascontiguousarray(f("hgrn_out_norm").reshape(4, 128).T),
        "conv_w": f("conv_w")[0],
        "c_tri": tri, "c_trirev": trirev, "c_cind": cind, "c_ident": ident,
    }
    maps = []
    npre = max(pre_tok, 128)
    for c in range(8):
        b, half = c // 2, c % 2
        t0 = half * TOK
        if half == 0 or pre_tok == 0:
            xp = np.zeros((npre, D), np.float32)
        else:
            xp = np.ascontiguousarray(x[b, t0 - npre:t0])
        d = dict(shared)
        d["x"] = np.ascontiguousarray(x[b, t0:t0 + TOK])
        d["xp"] = xp
        d["mem"] = mem[b]
        maps.append(d)
    return maps


def kernel(**inputs):
    nc = build_nc()
    maps = make_in_maps(inputs)
    res = run_bass_kernel_spmd(nc, maps, core_ids=list(range(8)))
    out = np.empty((4, 2 * TOK, D), np.float32)
    for c in range(8):
        out[c // 2, (c % 2) * TOK:(c % 2 + 1) * TOK] = res.results[c]["out"]
    return out
```

```python
import numpy as np
from contextlib import ExitStack
import concourse.bass as bass
import concourse.mybir as mybir
from concourse.bass_utils import run_bass_kernel_spmd

F32 = mybir.dt.float32
BF16 = mybir.dt.bfloat16
AF = mybir.ActivationFunctionType
ALU = mybir.AluOpType
AX = mybir.AxisListType

D = 1024
DFF = 2816
TOK = 2048
NB = TOK // 128
EPS = 1e-6
PRE_TOK = 2048
STOP_AFTER = None


class _Src:
    def __init__(self, name, sem):
        self.name = name
        self.sem = sem
        self.count = 0


class _Eng(_Src):
    def __init__(self, name, sem):
        super().__init__(name, sem)
        self.items = []
        self.waited = {}


class Bf:
    __slots__ = ("ap", "lw", "rd")

    def __init__(self, ap):
        self.ap = ap
        self.lw = None
        self.rd = {}


class Sched:
    def __init__(self, nc, stack):
        self.nc = nc
        self.stack = stack
        self.srcs = []
        self.pe = self._eng("pe")
        self.act = self._eng("act")
        self.dve = self._eng("dve")
        self.pool = self._eng("pool")
        self.sp = self._eng("sp")
        self.engs = [self.pe, self.act, self.dve, self.pool, self.sp]

    def _eng(self, name):
        e = _Eng(name, self.stack.enter_context(self.nc.semaphore("s_" + name)))
        self.srcs.append(e)
        return e

    def slot(self, name):
        s = _Src(name, self.stack.enter_context(self.nc.semaphore("d_" + name)))
        self.srcs.append(s)
        return s

    def _wait(self, eng, src, val):
        if val <= 0 or eng.waited.get(src, 0) >= val:
            return
        eng.waited[src] = val
        eng.items.append(("w", src.sem, val))

    def op(self, eng, fn, reads=(), writes=()):
        for t in reads:
            if t.lw is not None:
                src, val = t.lw
                if src is eng and eng is self.pe:
                    continue
                self._wait(eng, src, val)
        for t in writes:
            if t.lw is not None and t.lw[0] is not eng:
                self._wait(eng, *t.lw)
            for src, val in t.rd.items():
                if src is not eng:
                    self._wait(eng, src, val)
        eng.count += 1
        eng.items.append(("o", fn, eng.sem, 1))
        for t in reads:
            t.rd[eng] = eng.count
        for t in writes:
            t.lw = (eng, eng.count)
            t.rd = {}

    def dma(self, q, slot, fn, reads=(), writes=()):
        for t in reads:
            if t.lw is not None:
                self._wait(q, *t.lw)
        for t in writes:
            if t.lw is not None:
                self._wait(q, *t.lw)
            for src, val in t.rd.items():
                self._wait(q, src, val)
        slot.count += 16
        q.items.append(("o", fn, slot.sem, 16))
        for t in reads:
            t.rd[slot] = slot.count
        for t in writes:
            t.lw = (slot, slot.count)
            t.rd = {}

    def barrier(self):
        for e in self.engs:
            for s in self.srcs:
                if s is not e:
                    self._wait(e, s, s.count)

    def finish(self):
        for s in self.srcs:
            if s is not self.sp:
                self._wait(self.sp, s, s.count)

    def replay(self, block):
        def run(items):
            def body(e):
                for it in items:
                    if it[0] == "w":
                        e.wait_ge(it[1], it[2])
                    else:
                        ins = it[1](e)
                        ins.then_inc(it[2], it[3])
            return body
        block.tensor(run(self.pe.items))
        block.scalar(run(self.act.items))
        block.vector(run(self.dve.items))
        block.gpsimd(run(self.pool.items))
        block.sync(run(self.sp.items))


def build_nc(pre_tok=PRE_TOK, stop_after=STOP_AFTER):
    nc = bass.Bass("TRN2", target_bir_lowering=False)
    npb = pre_tok // 128

    def din(name, shape):
        return nc.dram_tensor(name, list(shape), F32, kind="ExternalInput").ap()

    x_d = din("x", [TOK, D])
    xp_d = din("xp", [max(pre_tok, 128), D])
    mem_d = din("mem", [256, D])
    g_ffn1 = din("ffn1_norm", [1, D]); g_mix = din("mix_norm", [1, D]); g_xat = din("xattn_norm", [1, D])
    g_mem = din("mem_norm", [1, D]); g_ffn2 = din("ffn2_norm", [1, D]); g_fin = din("final_norm", [1, D])
    w1g = din("ffn1_gate", [D, DFF]); w1u = din("ffn1_up", [D, DFF]); w1d = din("ffn1_down", [DFF, D])
    w2g = din("ffn2_gate", [D, DFF]); w2u = din("ffn2_up", [D, DFF]); w2d = din("ffn2_down", [DFF, D])
    w_in = din("w_in", [D, 3584]); w_out = din("w_out", [D, D])
    w_q = din("w_q_mem", [D, D]); w_kv = din("w_kv_mem", [D, 2 * D]); w_o = din("w_o_mem", [D, D])
    lbp_d = din("lb_param", [2, 512]); hg_d = din("hg", [128, 4]); cw_d = din("conv_w", [512, 3])
    tri_d = din("c_tri", [128, 128]); trr_d = din("c_trirev", [128, 128]); cind_d = din("c_cind", [128, 2])
    idn_d = din("c_ident", [128, 128])
    out_d = nc.dram_tensor("out", [TOK, D], F32, kind="ExternalOutput").ap()

    with ExitStack() as st:
        sch = Sched(nc, st)
        sb = lambda name, shape, dt: st.enter_context(nc.sbuf_tensor(name, list(shape), dt))
        x_sb = sb("x_sb", [128, NB, D], F32)
        ring = sb("ring", [128, 6, 2048], BF16)
        a16 = sb("a16", [128, 24576], BF16)
        a32 = sb("a32", [128, 12352], F32)
        gb_sb = sb("gb", [128, D], F32)
        oml_sb = sb("oml", [128, 512], F32)
        lbp_sb = sb("lbp", [128, 2, 512], F32)
        mask_sb = sb("maskb", [128, 4, 128], F32)
        tri_sb = sb("tri", [128, 128], F32)
        trr_sb = sb("trr", [128, 128], F32)
        cind_sb = sb("cind", [128, 2], F32)
        idf_sb = sb("idf", [128, 128], F32)
        idb_sb = sb("idb", [128, 128], BF16)
        one_sb = sb("ones", [128, 128], BF16)
        hg_sb = sb("hgs", [128, 4], F32)
        cw_sb = sb("cws", [128, 4, 3], F32)
        ss_sb = sb("ss", [128, 64], F32)
        sm_sb = sb("smx", [128, 64], F32)
        dec_sb = sb("dec", [128, 4, 8], F32)
        psf = [st.enter_context(nc.psum_tensor("psf%d" % i, [128, 512], F32)) for i in range(6)]
        pst = [st.enter_context(nc.psum_tensor("pst%d" % i, [128, 1024], BF16)) for i in range(2)]
        block = st.enter_context(nc.Block())

        PE, ACT, DVE, POOL, SP = sch.pe, sch.act, sch.dve, sch.pool, sch.sp

        xb = [Bf(x_sb[:, b, :]) for b in range(NB)]
        ringb = [Bf(ring[:, u, :]) for u in range(6)]
        ring_slot = [sch.slot("ring%d" % u) for u in range(6)]
        xl_slot = [sch.slot("xl%d" % b) for b in range(NB)]
        st_slot = [sch.slot("st%d" % b) for b in range(NB)]
        c_slot = sch.slot("const")
        g_slot = sch.slot("gain")
        PSF = [Bf(p[:]) for p in psf]
        PST = [Bf(p[:]) for p in pst]
        gb = Bf(gb_sb[:]); oml = Bf(oml_sb[:]); lbp = Bf(lbp_sb[:]); maskb = Bf(mask_sb[:])
        tri = Bf(tri_sb[:]); trr = Bf(trr_sb[:]); cind = Bf(cind_sb[:]); idf = Bf(idf_sb[:])
        idb = Bf(idb_sb[:]); ones = Bf(one_sb[:]); hg = Bf(hg_sb[:]); cw = Bf(cw_sb[:])
        ss = Bf(ss_sb[:]); sm = Bf(sm_sb[:]); dec = Bf(dec_sb[:])
        cnt = {"psf": 0, "pst": 0, "ring": 0}

        def nps():
            cnt["psf"] += 1
            return PSF[cnt["psf"] % 6]

        def npt():
            cnt["pst"] += 1
            return PST[cnt["pst"] % 2]

        def mmg(groups):
            def fn(e):
                ins = None
                for out, pairs in groups:
                    n = len(pairs)
                    for i, (l, r) in enumerate(pairs):
                        ins = e.matmul(out, l, r, start=(i == 0), stop=(i == n - 1))
                return ins
            return fn

        def pe_mm(groups, reads, writes):
            sch.op(PE, mmg(groups), reads, writes)

        def pe_tr(pairs, reads, writes):
            def fn(e):
                ins = None
                for o, i in pairs:
                    ins = e.transpose(o, i, idb.ap)
                return ins
            sch.op(PE, fn, list(reads) + [idb], writes)

        def act(out, in_, func, reads, writes, **kw):
            sch.op(ACT, lambda e: e.activation(out, in_, func, **kw), reads, writes)

        def tt(out, a, b, op, reads, writes, eng=None):
            sch.op(eng or DVE, lambda e: e.tensor_tensor(out, a, b, op), reads, writes)

        def stt(out, a, s, b, op0, op1, reads, writes, eng=None):
            sch.op(eng or DVE, lambda e: e.scalar_tensor_tensor(out, a, s, b, op0, op1), reads, writes)

        def ts(out, a, s1, s2, op0, op1, reads, writes, eng=None):
            if s2 is None:
                sch.op(eng or DVE, lambda e: e.tensor_scalar(out, a, s1, None, op0), reads, writes)
            else:
                sch.op(eng or DVE, lambda e: e.tensor_scalar(out, a, s1, s2, op0, op1), reads, writes)

        def cp(out, a, reads, writes, eng=None):
            if eng is ACT:
                sch.op(ACT, lambda e: e.activation(out, a, AF.Copy), reads, writes)
            else:
                sch.op(eng or DVE, lambda e: e.tensor_copy(out, a), reads, writes)

        def ms(out, val, writes, eng=None):
            sch.op(eng or DVE, lambda e: e.memset(out, val), (), writes)

        def wload(dst_ap, src_ap):
            u = cnt["ring"] % 6
            cnt["ring"] += 1
            rb = ringb[u]
            sch.dma(POOL, ring_slot[u], lambda e: e.dma_start(out=dst_ap(ring[:, u, :]), in_=src_ap), (), [rb])
            return rb

        def w_cols(wd, c0, n=256):
            src = wd[:, c0:c0 + n].rearrange("(k p) n -> p k n", p=128)
            rb = wload(lambda r: r[:, 0:8 * n].rearrange("p (k n) -> p k n", k=8), src)
            return rb, rb.ap[:, 0:8 * n].rearrange("p (k n) -> p k n", k=8)

        def w_rows(wd, r0):
            src = wd[r0:r0 + 256, :].rearrange("(j p) n -> p j n", p=128)
            rb = wload(lambda r: r.rearrange("p (j n) -> p j n", j=2), src)
            return rb, rb.ap.rearrange("p (j n) -> p j n", j=2)

        def cdma(dst, src, b):
            sch.dma(SP, c_slot, lambda e: e.dma_start(out=dst, in_=src), (), [b])
        cdma(tri.ap, tri_d, tri); cdma(trr.ap, trr_d, trr); cdma(cind.ap, cind_d, cind); cdma(idf.ap, idn_d, idf)
        for h in range(4):
            cdma(mask_sb[:, h, :], tri_d, maskb)
        cdma(hg.ap, hg_d, hg)
        cdma(cw.ap, cw_d.rearrange("(c p) j -> p c j", p=128), cw)
        cdma(lbp_sb[:, 0, :], lbp_d[0:1, :].partition_broadcast(128), lbp)
        cdma(lbp_sb[:, 1, :], lbp_d[1:2, :].partition_broadcast(128), lbp)
        for b_ in (tri, trr, cind, idf, maskb, hg, cw, lbp):
            b_.lw = (c_slot, c_slot.count)
        cp(idb.ap, idf.ap, [idf], [idb])
        ms(ones.ap, 1.0, [ones])
        tt(oml.ap, lbp_sb[:, 1, :], lbp_sb[:, 0, :], ALU.subtract, [lbp], [oml])
        act(oml.ap, oml.ap, AF.Sigmoid, [oml], [oml])

        def load_gain(gd):
            sch.dma(SP, g_slot, lambda e: e.dma_start(out=gb.ap, in_=gd.partition_broadcast(128)), (), [gb])

        def load_x(src, nblk):
            for b in range(nblk):
                sch.dma(SP, xl_slot[b],
                        (lambda b: lambda e: e.dma_start(out=xb[b].ap, in_=src[b * 128:(b + 1) * 128, :]))(b),
                        (), [xb[b]])

        def c16(off, shape):
            n = int(np.prod(shape[1:]))
            ap = a16[:, off:off + n]
            if len(shape) == 3:
                ap = ap.rearrange("p (a b) -> p a b", a=shape[1])
            return ap

        def c32(off, shape):
            n = int(np.prod(shape[1:]))
            ap = a32[:, off:off + n]
            if len(shape) == 3:
                ap = ap.rearrange("p (a b) -> p a b", a=shape[1])
            return ap

        def norm_T(srcs, hT_ap, hT_bufs, hn_bufs, junk, col0=0):
            n = len(srcs)
            ms(ss_sb[:, 0:2 * n], 0.0, [ss])
            for j, s in enumerate(srcs):
                act(junk.ap, s.ap, AF.Square, [s], [junk, ss], accum_out=ss_sb[:, j:j + 1])
            act(ss_sb[:, n:2 * n], ss_sb[:, 0:n], AF.Sqrt, [ss], [ss], scale=1.0 / D, bias=EPS)
            sch.op(DVE, lambda e: e.reciprocal(ss_sb[:, 0:n], ss_sb[:, n:2 * n]), [ss], [ss])
            for j, s in enumerate(srcs):
                hn = hn_bufs[j % len(hn_bufs)]
                stt(hn.ap, s.ap, ss_sb[:, j:j + 1], gb.ap, ALU.mult, ALU.mult, [s, ss, gb], [hn])
                pt = npt()
                pe_tr([(pt.ap[:, k * 128:(k + 1) * 128], hn.ap[:, k * 128:(k + 1) * 128]) for k in range(8)], [hn], [pt])
                cp(hT_ap[:, :, col0 + j * 128:col0 + (j + 1) * 128], pt.ap.rearrange("p (k n) -> p k n", k=8),
                   [pt], [hT_bufs[j]], eng=ACT)

        def ffn(gd, wg, wu, wd, nblk):
            hT_ap = c16(0, [128, 8, 2048])
            hTb = [Bf(hT_ap[:, :, b * 128:(b + 1) * 128]) for b in range(nblk)]
            hid_ap = [c16(16384 + i * 4096, [128, 2, 2048]) for i in range(2)]
            hidb = [[[Bf(hid_ap[i][:, j, t * 512:(t + 1) * 512]) for t in range(4)] for j in range(2)] for i in range(2)]
            hn = [Bf(c32(i * 1024, [128, 1024]).bitcast(BF16)[:, 0:1024]) for i in range(2)]
            junk = Bf(c32(2048, [128, 1024]).bitcast(BF16)[:, 0:1024])
            sg = [Bf(c32(3072 + i * 512, [128, 512])) for i in range(3)]
            load_gain(gd)
            norm_T(xb[:nblk], hT_ap, hTb, hn, junk)
            ntt = nblk // 4
            NG = DFF // 256
            units = {}

            def loadg(g):
                units[g] = (w_cols(wg, g * 256), w_cols(wu, g * 256), w_rows(wd, g * 256))

            def gu(g):
                (gb_, gap), (ub_, uap), _ = units[g]
                i = g % 2
                for j in range(2):
                    for t in range(ntt):
                        pg = nps(); pu = nps()
                        rd = [gb_, ub_] + hTb[t * 4:t * 4 + 4]
                        pe_mm([(pg.ap, [(gap[:, k, j * 128:(j + 1) * 128], hT_ap[:, k, t * 512:(t + 1) * 512]) for k in range(8)])], rd, [pg])
                        pe_mm([(pu.ap, [(uap[:, k, j * 128:(j + 1) * 128], hT_ap[:, k, t * 512:(t + 1) * 512]) for k in range(8)])], rd, [pu])
                        s = sg[(j * ntt + t) % 3]
                        act(s.ap, pg.ap, AF.Silu, [pg], [s])
                        tt(hidb[i][j][t].ap, s.ap, pu.ap, ALU.mult, [s, pu], [hidb[i][j][t]])

            def down(g):
                _, _, (db_, dap) = units[g]
                i = g % 2
                for b in range(nblk):
                    for ch in range(2):
                        pd = nps()
                        pe_mm([(pd.ap, [(hid_ap[i][:, j, b * 128:(b + 1) * 128], dap[:, j, ch * 512:(ch + 1) * 512]) for j in range(2)])],
                              [db_, hidb[i][0][b // 4], hidb[i][1][b // 4]], [pd])
                        xs = xb[b].ap[:, ch * 512:(ch + 1) * 512]
                        stt(xs, pd.ap, 0.5, xs, ALU.mult, ALU.add, [pd, xb[b]], [xb[b]])
                del units[g]

            loadg(0)
            loadg(1)
            gu(0)
            for g in range(NG):
                if g + 1 < NG:
                    gu(g + 1)
                down(g)
                if g + 2 < NG:
                    loadg(g + 2)

        st_ = {"gc": 0}

        def mix_setup():
            m = {}
            m["hT"] = c16(0, [128, 8, 512]); m["hTb"] = [Bf(m["hT"][:, :, b * 128:(b + 1) * 128]) for b in range(4)]
            m["t16"] = [Bf(c16(4096 + i * 512, [128, 512])) for i in range(5)]
            m["v"] = [Bf(c16(4096 + 2560 + i * 512, [128, 512])) for i in range(4)]
            m["ktT"] = c16(8704, [128, 4, 512]); m["ktTb"] = [Bf(m["ktT"][:, :, b * 128:(b + 1) * 128]) for b in range(4)]
            m["qT"] = c16(10752, [128, 4, 512]); m["qTb"] = [Bf(m["qT"][:, h, :]) for h in range(4)]
            m["PT"] = [Bf(c16(12800 + i * 512, [128, 4, 128])) for i in range(2)]
            m["Sbf"] = [Bf(c16(13824 + i * 512, [128, 512])) for i in range(9)]
            m["GT"] = c16(18432, [128, 4, 512]); m["GTb"] = [Bf(m["GT"][:, h, :]) for h in range(4)]
            m["yT"] = c16(20480, [128, 8, 512]); m["yTb"] = [Bf(m["yT"][:, c, :]) for c in range(8)]
            m["blk32"] = [Bf(c32(i * 512, [128, 512])) for i in range(6)]
            m["t32"] = [Bf(c32(3072 + i * 512, [128, 512])) for i in range(5)]
            m["ET"] = c32(5632, [128, 4, 512]); m["ETb"] = [Bf(m["ET"][:, :, b * 128:(b + 1) * 128]) for b in range(4)]
            m["S"] = [Bf(c32(7680 + i * 512, [128, 512])) for i in range(2)]
            m["u"] = c32(8704, [128, 4, 514]); m["ub"] = [Bf(m["u"][:, c, :]) for c in range(4)]
            m["hn"] = [Bf(c32(10760 + i * 512, [128, 512]).bitcast(BF16)) for i in range(2)]
            m["junk"] = Bf(c32(11784, [128, 512]).bitcast(BF16))
            m["i16"] = 0; m["i32"] = 0
            return m

        def t16(m):
            m["i16"] += 1
            return m["t16"][m["i16"] % 5]

        def t32(m):
            m["i32"] += 1
            return m["t32"][m["i32"] % 5]

        def mix_init_state(m):
            ms(m["S"][0].ap, 0.0, [m["S"][0]])
            ms(m["Sbf"][0].ap, 0.0, [m["Sbf"][0]])
            ms(m["u"], 0.0, m["ub"])
            st_["gc"] = 0
            st_["si"] = 0

        def mix_restore_state(m):
            sb_c = m["Sbf"][st_["gc"] % 9]
            cp(sb_c.ap, m["S"][st_["si"] % 2].ap, [m["S"][st_["si"] % 2]], [sb_c], eng=ACT)

        def fm_proj(m, c0, nchunks, consume):
            for q in range(0, nchunks, 2):
                rb, wap = w_cols(w_in, c0 + q * 128)
                for j in range(2):
                    p = nps()
                    pe_mm([(p.ap, [(wap[:, k, j * 128:(j + 1) * 128], m["hT"][:, k, :]) for k in range(8)])],
                          [rb] + m["hTb"], [p])
                    consume(q + j, p)

        def mix_tile(m, blks, state_only, want_u):
            load_gain(g_mix)
            norm_T(blks, m["hT"], m["hTb"], m["hn"], m["junk"])
            hT = m["hT"]
            wf = [w_cols(w_in, 512), w_cols(w_in, 768)]
            wi = [w_cols(w_in, 1024), w_cols(w_in, 1280)]
            vbs = m["v"]; Sidx = [None] * 8
            TB = [slice(b * 128, (b + 1) * 128) for b in range(4)]
            for g0 in (0, 2):
                grp = (g0, g0 + 1)
                A = {b: m["blk32"][(b % 2) * 3] for b in grp}
                Bq = {b: m["blk32"][(b % 2) * 3 + 1] for b in grp}
                C = {b: m["blk32"][(b % 2) * 3 + 2] for b in grp}
                pzf = {}; pzi = {}
                for b in grp:
                    pzf[b] = nps()
                    pe_mm([(pzf[b].ap[:, q * 256:(q + 1) * 256], [(hT[:, k, TB[b]], wf[q][1][:, k, :]) for k in range(8)]) for q in range(2)],
                          [wf[0][0], wf[1][0], m["hTb"][b]], [pzf[b]])
                for b in grp:
                    pzi[b] = nps()
                    pe_mm([(pzi[b].ap[:, q * 256:(q + 1) * 256], [(hT[:, k, TB[b]], wi[q][1][:, k, :]) for k in range(8)]) for q in range(2)],
                          [wi[0][0], wi[1][0], m["hTb"][b]], [pzi[b]])
                for b in grp:
                    act(A[b].ap, pzf[b].ap, AF.Sigmoid, [pzf[b]], [A[b]], scale=-1.0)
                for b in grp:
                    act(vbs[b].ap, pzi[b].ap, AF.Copy, [pzi[b]], [vbs[b]])
                for b in grp:
                    tt(A[b].ap, A[b].ap, oml.ap, ALU.mult, [A[b], oml], [A[b]])
                for b in grp:
                    act(Bq[b].ap, A[b].ap, AF.Ln, [A[b]], [Bq[b]], scale=-1.0, bias=1.0)
                pBr = {}; pDc = {}; pB = {}; pBT = {}
                for b in grp:
                    pBr[b] = nps()
                    pe_mm([(pBr[b].ap, [(trr.ap, Bq[b].ap)])], [trr, Bq[b]], [pBr[b]])
                    if state_only:
                        pDc[b] = nps()
                        pe_mm([(pDc[b].ap[:, h * 2:(h + 1) * 2], [(Bq[b].ap[:, h * 128:(h + 1) * 128], cind.ap)]) for h in range(4)],
                              [cind, Bq[b]], [pDc[b]])
                    else:
                        pB[b] = nps()
                        pe_mm([(pB[b].ap, [(tri.ap, Bq[b].ap)])], [tri, Bq[b]], [pB[b]])
                        pBT[b] = nps()
                        pe_mm([(pBT[b].ap[:, h * 128:(h + 1) * 128], [(Bq[b].ap[:, h * 128:(h + 1) * 128], tri.ap)]) for h in range(4)],
                              [tri, Bq[b]], [pBT[b]])
                for b in grp:
                    act(C[b].ap, pBr[b].ap, AF.Exp, [pBr[b]], [C[b]])
                kh = {}
                for b in grp:
                    kh[b] = t16(m)
                    tt(kh[b].ap, A[b].ap, C[b].ap, ALU.mult, [A[b], C[b]], [kh[b]])
                if state_only:
                    for b in grp:
                        act(dec_sb[:, b, :], pDc[b].ap[:, 0:8], AF.Exp, [pDc[b]], [dec])
                else:
                    for b in grp:
                        act(m["ET"][:, :, TB[b]], pBT[b].ap.rearrange("p (h n) -> p h n", h=4), AF.Exp, [pBT[b]], [m["ETb"][b]])
                    for b in grp:
                        act(C[b].ap, pB[b].ap, AF.Exp, [pB[b]], [C[b]], scale=-1.0)
                    kt = {}
                    for b in grp:
                        kt[b] = t16(m)
                        tt(kt[b].ap, A[b].ap, C[b].ap, ALU.mult, [A[b], C[b]], [kt[b]])
                    for b in grp:
                        pt = npt()
                        pe_tr([(pt.ap[:, h * 128:(h + 1) * 128], kt[b].ap[:, h * 128:(h + 1) * 128]) for h in range(4)], [kt[b]], [pt])
                        cp(m["ktT"][:, :, TB[b]], pt.ap[:, 0:512].rearrange("p (h n) -> p h n", h=4), [pt], [m["ktTb"][b]])
                pS = {}
                for b in grp:
                    for c in range(2):
                        cs = slice(c * 64, (c + 1) * 64)
                        pS[b, c] = nps()
                        pe_mm([(pS[b, c].ap[:, h * 128:(h + 1) * 128], [(kh[b].ap[cs, h * 128:(h + 1) * 128], vbs[b].ap[cs, h * 128:(h + 1) * 128])]) for h in range(4)],
                              [kh[b], vbs[b]], [pS[b, c]])
                for b in grp:
                    for c in range(2):
                        So = m["S"][st_["si"] % 2]; Sn = m["S"][(st_["si"] + 1) % 2]
                        st_["si"] += 1
                        for h in range(4):
                            hs = slice(h * 128, (h + 1) * 128)
                            if state_only:
                                dap = dec_sb[:, b, h * 2 + c:h * 2 + c + 1]; dbf = dec
                            else:
                                col = b * 128 + c * 64 + 63
                                dap = m["ET"][:, h, col:col + 1]; dbf = m["ETb"][b]
                            stt(Sn.ap[:, hs], So.ap[:, hs], dap, pS[b, c].ap[:, hs], ALU.mult, ALU.add, [So, dbf, pS[b, c]], [Sn])
                        Sidx[b * 2 + c] = st_["gc"] % 9
                        st_["gc"] += 1
                        sb_n = m["Sbf"][st_["gc"] % 9]
                        cp(sb_n.ap, Sn.ap, [Sn], [sb_n])
            if state_only and not want_u:
                return

            pzc = {}

            def take_c(c, p):
                zcs = t32(m)
                act(zcs.ap, p.ap, AF.Copy, [p], [zcs])
                pzc[c] = zcs

            def take_u(c, p):
                tt(m["u"][:, c, 2:514], pzc[c].ap, p.ap, ALU.mult, [pzc[c], p], [m["ub"][c]])

            def halo(c):
                cp(m["u"][:, c, 0:2], m["u"][:, c, 512:514], [m["ub"][c]], [m["ub"][c]])

            def conv_u():
                for q in (0, 2):
                    fm_proj(m, 2560 + q * 128, 2, (lambda q: lambda c, p: take_c(q + c, p))(q))
                    fm_proj(m, 3072 + q * 128, 2, (lambda q: lambda c, p: take_u(q + c, p))(q))

            if state_only:
                conv_u()
                for c in range(4):
                    halo(c)
                return

            def take_q(h, p):
                sq = t32(m)
                act(sq.ap, p.ap, AF.Silu, [p], [sq])
                stt(m["qT"][:, h, :], sq.ap, 128.0 ** -0.5, m["ET"][:, h, :], ALU.mult, ALU.mult, [sq] + m["ETb"], [m["qTb"][h]])
            fm_proj(m, 0, 4, take_q)

            def take_g(h, p):
                act(m["GT"][:, h, :], p.ap, AF.Silu, [p], [m["GTb"][h]])
            fm_proj(m, 1536, 4, take_g)

            for g0 in (0, 2):
                grp = (g0, g0 + 1)
                A = {b: m["blk32"][(b % 2) * 3] for b in grp}
                Bq = {b: m["blk32"][(b % 2) * 3 + 1] for b in grp}
                C = {b: m["blk32"][(b % 2) * 3 + 2] for b in grp}
                pSc = {}; pO = {}; pN = {}; sqo = {}
                for b in grp:
                    pSc[b] = nps()
                    pe_mm([(pSc[b].ap[:, h * 128:(h + 1) * 128], [(m["ktT"][:, h, TB[b]], m["qT"][:, h, TB[b]])]) for h in range(4)],
                          [m["ktTb"][b]] + m["qTb"], [pSc[b]])
                for b in grp:
                    PT = m["PT"][b % 2]
                    tt(PT.ap, pSc[b].ap.rearrange("p (h n) -> p h n", h=4), maskb.ap, ALU.mult, [pSc[b], maskb], [PT])
                for b in grp:
                    PT = m["PT"][b % 2]; v = vbs[b]
                    pO[b] = nps()
                    groups = []
                    rds = [v, PT] + m["qTb"]
                    for h in range(4):
                        for c in range(2):
                            cs = slice(c * 64, (c + 1) * 64)
                            sbf = m["Sbf"][Sidx[b * 2 + c]]
                            rds.append(sbf)
                            groups.append((pO[b].ap[:, h * 128 + c * 64:h * 128 + (c + 1) * 64],
                                           [(v.ap[cs, h * 128:(h + 1) * 128], PT.ap[cs, h, c * 64:(c + 1) * 64]),
                                            (sbf.ap[:, h * 128:(h + 1) * 128], m["qT"][:, h, b * 128 + c * 64:b * 128 + (c + 1) * 64])]))
                    pe_mm(groups, rds, [pO[b]])
                for b in grp:
                    sqo[b] = t16(m)
                    act(sqo[b].ap, pO[b].ap, AF.Square, [pO[b]], [sqo[b]])
                for b in grp:
                    pN[b] = nps()
                    pe_mm([(pN[b].ap[:, h * 128:(h + 1) * 128], [(ones.ap, sqo[b].ap[:, h * 128:(h + 1) * 128])]) for h in range(4)],
                          [ones, sqo[b]], [pN[b]])
                for b in grp:
                    act(A[b].ap, pN[b].ap, AF.Sqrt, [pN[b]], [A[b]], scale=1.0 / 128, bias=EPS)
                for b in grp:
                    sch.op(DVE, (lambda o, i: lambda e: e.reciprocal(o, i))(Bq[b].ap, A[b].ap), [A[b]], [Bq[b]])
                for b in grp:
                    tt(C[b].ap, pO[b].ap, Bq[b].ap, ALU.mult, [pO[b], Bq[b]], [C[b]])
                for b in grp:
                    for h in range(4):
                        stt(m["yT"][:, h, TB[b]], C[b].ap[:, h * 128:(h + 1) * 128], hg_sb[:, h:h + 1], m["GT"][:, h, TB[b]], ALU.mult, ALU.mult,
                            [C[b], hg, m["GTb"][h]], [m["yTb"][h]])

            conv_u()

            def take_b(c, p):
                a1 = t32(m)
                ts(a1.ap, m["u"][:, c, 2:514], cw_sb[:, c, 2:3], None, ALU.mult, None, [m["ub"][c], cw], [a1])
                a2 = t32(m)
                stt(a2.ap, m["u"][:, c, 1:513], cw_sb[:, c, 1:2], a1.ap, ALU.mult, ALU.add, [m["ub"][c], cw, a1], [a2])
                a3 = t32(m)
                stt(a3.ap, m["u"][:, c, 0:512], cw_sb[:, c, 0:1], a2.ap, ALU.mult, ALU.add, [m["ub"][c], cw, a2], [a3])
                tt(m["yT"][:, 4 + c, :], a3.ap, p.ap, ALU.mult, [a3, p], [m["yTb"][4 + c]])
                halo(c)
            fm_proj(m, 2048, 4, take_b)

            for q in range(4):
                rb, wap = w_cols(w_out, q * 256)
                for b in range(4):
                    tb = slice(b * 128, (b + 1) * 128)
                    p = nps()
                    pe_mm([(p.ap[:, 0:256], [(m["yT"][:, fc, tb], wap[:, fc, :]) for fc in range(8)])], [rb] + m["yTb"], [p])
                    xs = blks[b].ap[:, q * 256:(q + 1) * 256]
                    tt(xs, p.ap[:, 0:256], xs, ALU.add, [p, blks[b]], [blks[b]])

        def xattn():
            hT = c16(0, [128, 8, 512]); hTb = [Bf(hT[:, :, b * 128:(b + 1) * 128]) for b in range(4)]
            qmT = c16(4096, [128, 8, 512]); qmTb = [Bf(qmT[:, c, :]) for c in range(8)]
            kmT = c16(8192, [128, 8, 256]); kmTb = Bf(kmT)
            vm = c16(10240, [128, 2, 1024]); vmb = Bf(vm)
            mnT = c16(12288, [128, 8, 256]); mnTb = [Bf(mnT[:, :, b * 128:(b + 1) * 128]) for b in range(2)]
            pbuf = [Bf(c16(14336 + i * 1024, [128, 4, 256])) for i in range(2)]
            pTb = [Bf(c16(16384 + i * 1024, [128, 8, 128])) for i in range(2)]
            attT = c16(18432, [128, 8, 512]); attTb = [Bf(attT[:, :, b * 128:(b + 1) * 128]) for b in range(4)]
            hn = [Bf(c32(i * 512, [128, 512]).bitcast(BF16)) for i in range(2)]
            junk = Bf(c32(1024, [128, 512]).bitcast(BF16))
            memx = [Bf(c32(1536 + i * 1024, [128, 1024])) for i in range(2)]
            pe_ = [Bf(c32(3584 + i * 1024, [128, 4, 256])) for i in range(2)]
            scale = 256.0 ** -0.5
            for i in range(2):
                sch.dma(SP, xl_slot[i], (lambda i: lambda e: e.dma_start(out=memx[i].ap, in_=mem_d[i * 128:(i + 1) * 128, :]))(i), (), [memx[i]])
            load_gain(g_mem)
            norm_T(memx, mnT, mnTb, hn, junk)
            for q in range(4):
                rb, wap = w_cols(w_kv, q * 256)
                for j in range(2):
                    p = nps()
                    pe_mm([(p.ap[:, 0:256], [(wap[:, k, j * 128:(j + 1) * 128], mnT[:, k, :]) for k in range(8)])], [rb] + mnTb, [p])
                    cp(kmT[:, q * 2 + j, :], p.ap[:, 0:256], [p], [kmTb], eng=ACT)
            for q in range(4):
                rb, wap = w_cols(w_kv, 1024 + q * 256)
                for mb in range(2):
                    p = nps()
                    pe_mm([(p.ap[:, 0:256], [(mnT[:, k, mb * 128:(mb + 1) * 128], wap[:, k, :]) for k in range(8)])], [rb] + mnTb, [p])
                    cp(vm[:, mb, q * 256:(q + 1) * 256], p.ap[:, 0:256], [p], [vmb], eng=ACT)
            for ti in range(4):
                blks = xb[ti * 4:ti * 4 + 4]
                load_gain(g_xat)
                norm_T(blks, hT, hTb, hn, junk)
                for q in range(4):
                    rb, wap = w_cols(w_q, q * 256)
                    for j in range(2):
                        p = nps()
                        pe_mm([(p.ap, [(wap[:, k, j * 128:(j + 1) * 128], hT[:, k, :]) for k in range(8)])], [rb] + hTb, [p])
                        cp(qmT[:, q * 2 + j, :], p.ap, [p], [qmTb[q * 2 + j]], eng=ACT if j else DVE)
                TB = [slice(b * 128, (b + 1) * 128) for b in range(4)]
                for g0 in (0, 2):
                    grp = (g0, g0 + 1)
                    smb = {b: Bf(sm_sb[:, (b % 2) * 32:(b % 2 + 1) * 32]) for b in grp}
                    so = {b: (b % 2) * 32 for b in grp}
                    pscs = {}
                    for b in grp:
                        ms(sm_sb[:, so[b]:so[b] + 4], 0.0, [sm])
                    for b in grp:
                        for hp in range(2):
                            p = nps()
                            pe_mm([(p.ap[:, hh * 256:(hh + 1) * 256],
                                    [(qmT[:, (hp * 2 + hh) * 2 + dc, TB[b]], kmT[:, (hp * 2 + hh) * 2 + dc, :]) for dc in range(2)]) for hh in range(2)],
                                  qmTb + [kmTb], [p])
                            pscs[b, hp] = p
                    for b in grp:
                        for hp in range(2):
                            p = pscs[b, hp]
                            sch.op(DVE, (lambda o, i: lambda e: e.tensor_reduce(o, i, AX.X, ALU.max))(
                                sm_sb[:, so[b] + 8 + hp * 2:so[b] + 8 + hp * 2 + 2], p.ap.rearrange("p (h n) -> p h n", h=2)), [p], [sm])
                        ts(sm_sb[:, so[b] + 16:so[b] + 20], sm_sb[:, so[b] + 8:so[b] + 12], -scale, None, ALU.mult, None, [sm], [sm])
                    for b in grp:
                        pe_b = pe_[b % 2]
                        for h in range(4):
                            p = pscs[b, h // 2]
                            act(pe_b.ap[:, h, :], p.ap[:, (h % 2) * 256:(h % 2 + 1) * 256], AF.Exp, [p, sm], [pe_b, sm],
                                scale=scale, bias=sm_sb[:, so[b] + 16 + h:so[b] + 17 + h], accum_out=sm_sb[:, so[b] + h:so[b] + h + 1])
                    for b in grp:
                        pe_b = pe_[b % 2]; pb = pbuf[b % 2]
                        sch.op(DVE, (lambda o: lambda e: e.reciprocal(sm_sb[:, o + 24:o + 28], sm_sb[:, o:o + 4]))(so[b]), [sm], [sm])
                        for h in range(4):
                            ts(pb.ap[:, h, :], pe_b.ap[:, h, :], sm_sb[:, so[b] + 24 + h:so[b] + 25 + h], None, ALU.mult, None, [pe_b, sm], [pb])
                    pts = {}
                    for b in grp:
                        pb = pbuf[b % 2]
                        pts[b] = npt()
                        pe_tr([(pts[b].ap[:, (h * 2 + mc) * 128:(h * 2 + mc + 1) * 128], pb.ap[:, h, mc * 128:(mc + 1) * 128])
                               for h in range(4) for mc in range(2)], [pb], [pts[b]])
                    for b in grp:
                        pT = pTb[b % 2]
                        cp(pT.ap, pts[b].ap.rearrange("p (k n) -> p k n", k=8), [pts[b]], [pT], eng=ACT)
                    pAs = {}
                    for b in grp:
                        pT = pTb[b % 2]
                        for half in range(2):
                            pA = nps()
                            pe_mm([(pA.ap[:, (ci - half * 4) * 128:(ci - half * 4 + 1) * 128],
                                    [(vm[:, mc, ci * 128:(ci + 1) * 128], pT.ap[:, (ci // 2) * 2 + mc, :]) for mc in range(2)])
                                   for ci in range(half * 4, half * 4 + 4)], [vmb, pT], [pA])
                            pAs[b, half] = pA
                    for b in grp:
                        cp(attT[:, 0:4, TB[b]], pAs[b, 0].ap.rearrange("p (k n) -> p k n", k=4), [pAs[b, 0]], [attTb[b]], eng=ACT)
                        cp(attT[:, 4:8, TB[b]], pAs[b, 1].ap.rearrange("p (k n) -> p k n", k=4), [pAs[b, 1]], [attTb[b]])
                for q in range(4):
                    rb, wap = w_cols(w_o, q * 256)
                    for b in range(4):
                        tb = slice(b * 128, (b + 1) * 128)
                        p = nps()
                        pe_mm([(p.ap[:, 0:256], [(attT[:, fc, tb], wap[:, fc, :]) for fc in range(8)])], [rb] + attTb, [p])
                        xs = blks[b].ap[:, q * 256:(q + 1) * 256]
                        tt(xs, p.ap[:, 0:256], xs, ALU.add, [p, blks[b]], [blks[b]])

        def final(do_norm=True):
            ob = [Bf(c32(i * 1024, [128, 1024])) for i in range(2)]
            junk = Bf(c32(2048, [128, 512]).bitcast(BF16))
            if do_norm:
                load_gain(g_fin)
                ms(ss_sb[:, 0:32], 0.0, [ss])
                for b in range(NB):
                    act(junk.ap, xb[b].ap, AF.Square, [xb[b]], [junk, ss], accum_out=ss_sb[:, b:b + 1])
                act(ss_sb[:, 16:32], ss_sb[:, 0:16], AF.Sqrt, [ss], [ss], scale=1.0 / D, bias=EPS)
                sch.op(DVE, lambda e: e.reciprocal(ss_sb[:, 0:16], ss_sb[:, 16:32]), [ss], [ss])
            for b in range(NB):
                o = ob[b % 2]
                if do_norm:
                    stt(o.ap, xb[b].ap, ss_sb[:, b:b + 1], gb.ap, ALU.mult, ALU.mult, [xb[b], ss, gb], [o])
                else:
                    cp(o.ap, xb[b].ap, [xb[b]], [o])
                sch.dma(SP, st_slot[b], (lambda b, o: lambda e: e.dma_start(out=out_d[b * 128:(b + 1) * 128, :], in_=o.ap))(b, o), [o], ())

        m = None
        if pre_tok > 0 and (stop_after is None or stop_after >= 2):
            load_x(xp_d, npb)
            ffn(g_ffn1, w1g, w1u, w1d, npb)
            sch.barrier()
            m = mix_setup()
            mix_init_state(m)
            for ti in range(npb // 4):
                mix_tile(m, xb[ti * 4:ti * 4 + 4], True, ti == npb // 4 - 1)
            sch.barrier()
        load_x(x_d, NB)
        ffn(g_ffn1, w1g, w1u, w1d, NB)
        sch.barrier()
        if stop_after is None or stop_after >= 2:
            if m is None:
                m = mix_setup()
                mix_init_state(m)
            else:
                mix_restore_state(m)
            for ti in range(4):
                mix_tile(m, xb[ti * 4:ti * 4 + 4], False, False)
            sch.barrier()
        if stop_after is None or stop_after >= 3:
            xattn()
            sch.barrier()
        if stop_after is None or stop_after >= 4:
            ffn(g_ffn2, w2g, w2u, w2d, NB)
            sch.barrier()
        final(do_norm=(stop_after is None))
        sch.finish()
        sch.replay(block)
    return nc


def _consts():
    i = np.arange(128)
    same = (i[:, None] // 64) == (i[None, :] // 64)
    tri = (same & (i[:, None] <= i[None, :])).astype(np.float32)
    trirev = (same & (i[:, None] > i[None, :])).astype(np.float32)
    cind = np.stack([(i < 64), (i >= 64)], axis=1).astype(np.float32)
    return tri, trirev, cind, np.eye(128, dtype=np.float32)


def make_in_maps(inputs, pre_tok=PRE_TOK):
    f = lambda k: np.ascontiguousarray(np.asarray(inputs[k], dtype=np.float32))
    x = f("x"); mem = f("mem")
    tri, trirev, cind, ident = _consts()
    shared = {
        "ffn1_norm": f("ffn1_norm").reshape(1, D), "mix_norm": f("mix_norm").reshape(1, D),
        "xattn_norm": f("xattn_norm").reshape(1, D), "mem_norm": f("mem_norm").reshape(1, D),
        "ffn2_norm": f("ffn2_norm").reshape(1, D), "final_norm": f("final_norm").reshape(1, D),
        "ffn1_gate": f("ffn1_gate")[0], "ffn1_up": f("ffn1_up")[0], "ffn1_down": f("ffn1_down")[0],
        "ffn2_gate": f("ffn2_gate")[0], "ffn2_up": f("ffn2_up")[0], "ffn2_down": f("ffn2_down")[0],
        "w_in": f("w_in")[0], "w_out": f("w_out")[0], "w_q_mem": f("w_q_mem")[0],
        "w_kv_mem": f("w_kv_mem")[0], "w_o_mem": f("w_o_mem")[0],
        "lb_param": f("lb_param"), "hg": np.ascontiguousarray(f("hgrn_out_norm").reshape(4, 128).T),
        "conv_w": f("conv_w")[0],
        "c_tri": tri, "c_trirev": trirev, "c_cind": cind, "c_ident": ident,
    }
    maps = []
    npre = max(pre_tok, 128)
    for c in range(8):
        b, half = c // 2, c % 2
        t0 = half * TOK
        if half == 0 or pre_tok == 0:
            xp = np.zeros((npre, D), np.float32)
        else:
            xp = np.ascontiguousarray(x[b, t0 - npre:t0])
        d = dict(shared)
        d["x"] = np.ascontiguousarray(x[b, t0:t0 + TOK])
        d["xp"] = xp
        d["mem"] = mem[b]
        maps.append(d)
    return maps


def kernel(**inputs):
    nc = build_nc()
    maps = make_in_maps(inputs)
    res = run_bass_kernel_spmd(nc, maps, core_ids=list(range(8)))
    out = np.empty((4, 2 * TOK, D), np.float32)
    for c in range(8):
        out[c // 2, (c % 2) * TOK:(c % 2 + 1) * TOK] = res.results[c]["out"]
    return out
```

```python
import numpy as np
from contextlib import ExitStack
import concourse.bass as bass
import concourse.mybir as mybir
from concourse.bass_utils import run_bass_kernel_spmd

F32 = mybir.dt.float32
BF16 = mybir.dt.bfloat16
AF = mybir.ActivationFunctionType
ALU = mybir.AluOpType
AX = mybir.AxisListType

D = 1024
DFF = 2816
TOK = 2048
NB = TOK // 128
EPS = 1e-6
PRE_TOK = 2048
STOP_AFTER = None
TM_OFF = [0, 3, 6, 9, 2]
O_OFF = [0, 2, 4, 6, 1]
A_OFF = [0, 2, 4, 6]


class _Src:
    def __init__(self, name, sem):
        self.name = name
        self.sem = sem
        self.count = 0


class _Eng(_Src):
    def __init__(self, name, sem):
        super().__init__(name, sem)
        self.items = []
        self.waited = {}


class Bf:
    __slots__ = ("ap", "lw", "rd")

    def __init__(self, ap):
        self.ap = ap
        self.lw = None
        self.rd = {}


class Sched:
    def __init__(self, nc, stack):
        self.nc = nc
        self.stack = stack
        self.srcs = []
        self.pe = self._eng("pe")
        self.act = self._eng("act")
        self.dve = self._eng("dve")
        self.pool = self._eng("pool")
        self.sp = self._eng("sp")
        self.engs = [self.pe, self.act, self.dve, self.pool, self.sp]

    def _eng(self, name):
        e = _Eng(name, self.stack.enter_context(self.nc.semaphore("s_" + name)))
        self.srcs.append(e)
        return e

    def slot(self, name):
        s = _Src(name, self.stack.enter_context(self.nc.semaphore("d_" + name)))
        self.srcs.append(s)
        return s

    def _wait(self, eng, src, val):
        if val <= 0 or eng.waited.get(src, 0) >= val:
            return
        eng.waited[src] = val
        eng.items.append(("w", src, val))

    def op(self, eng, fn, reads=(), writes=()):
        for t in reads:
            if t.lw is not None:
                src, val = t.lw
                if src is eng and eng is self.pe:
                    continue
                self._wait(eng, src, val)
        for t in writes:
            if t.lw is not None and t.lw[0] is not eng:
                self._wait(eng, *t.lw)
            for src, val in t.rd.items():
                if src is not eng:
                    self._wait(eng, src, val)
        eng.count += 1
        eng.items.append(("o", fn, eng, eng.count))
        for t in reads:
            t.rd[eng] = eng.count
        for t in writes:
            t.lw = (eng, eng.count)
            t.rd = {}

    def dma(self, q, slot, fn, reads=(), writes=()):
        for t in reads:
            if t.lw is not None:
                self._wait(q, *t.lw)
        for t in writes:
            if t.lw is not None:
                self._wait(q, *t.lw)
            for src, val in t.rd.items():
                self._wait(q, src, val)
        slot.count += 16
        q.items.append(("o", fn, slot, slot.count))
        for t in reads:
            t.rd[slot] = slot.count
        for t in writes:
            t.lw = (slot, slot.count)
            t.rd = {}

    def barrier(self):
        for e in self.engs:
            for s in self.srcs:
                if s is not e:
                    self._wait(e, s, s.count)

    def finish(self):
        for s in self.srcs:
            if s is not self.sp:
                self._wait(self.sp, s, s.count)

    def replay(self, block):
        import bisect
        need = {}
        for e in self.engs:
            for it in e.items:
                if it[0] == "w":
                    need.setdefault(it[1], set()).add(it[2])
        rank = {src: sorted(v) for src, v in need.items()}
        engset = set(self.engs)

        def run(items):
            def body(e):
                for it in items:
                    if it[0] == "w":
                        src, val = it[1], it[2]
                        if src in engset:
                            val = bisect.bisect_left(rank[src], val) + 1
                        e.wait_ge(src.sem, val)
                    else:
                        ins = it[1](e)
                        src, c = it[2], it[3]
                        if src in engset:
                            if c in need.get(src, ()):
                                ins.then_inc(src.sem, 1)
                        else:
                            ins.then_inc(src.sem, 16)
            return body
        block.tensor(run(self.pe.items))
        block.scalar(run(self.act.items))
        block.vector(run(self.dve.items))
        block.gpsimd(run(self.pool.items))
        block.sync(run(self.sp.items))


def build_nc(pre_tok=PRE_TOK, stop_after=STOP_AFTER):
    nc = bass.Bass("TRN2", target_bir_lowering=False)
    npb = pre_tok // 128

    def din(name, shape):
        return nc.dram_tensor(name, list(shape), F32, kind="ExternalInput").ap()

    x_d = din("x", [TOK, D])
    xp_d = din("xp", [max(pre_tok, 128), D])
    mem_d = din("mem", [256, D])
    g_ffn1 = din("ffn1_norm", [1, D]); g_mix = din("mix_norm", [1, D]); g_xat = din("xattn_norm", [1, D])
    g_mem = din("mem_norm", [1, D]); g_ffn2 = din("ffn2_norm", [1, D]); g_fin = din("final_norm", [1, D])
    w1g = din("ffn1_gate", [D, DFF]); w1u = din("ffn1_up", [D, DFF]); w1d = din("ffn1_down", [DFF, D])
    w2g = din("ffn2_gate", [D, DFF]); w2u = din("ffn2_up", [D, DFF]); w2d = din("ffn2_down", [DFF, D])
    w_in = din("w_in", [D, 3584]); w_out = din("w_out", [D, D])
    w_q = din("w_q_mem", [D, D]); w_kv = din("w_kv_mem", [D, 2 * D]); w_o = din("w_o_mem", [D, D])
    lbp_d = din("lb_param", [2, 512]); hg_d = din("hg", [128, 4]); cw_d = din("conv_w", [512, 3])
    tri_d = din("c_tri", [128, 128]); trr_d = din("c_trirev", [128, 128]); cind_d = din("c_cind", [128, 2])
    idn_d = din("c_ident", [128, 128])
    out_d = nc.dram_tensor("out", [TOK, D], F32, kind="ExternalOutput").ap()

    with ExitStack() as st:
        sch = Sched(nc, st)
        sb = lambda name, shape, dt: st.enter_context(nc.sbuf_tensor(name, list(shape), dt))
        x_sb = sb("x_sb", [128, NB, D], F32)
        ring = sb("ring", [128, 6, 2048], BF16)
        a16 = sb("a16", [128, 26112], BF16)
        a32 = sb("a32", [128, 13376], F32)
        gb_sb = sb("gb", [128, D], F32)
        oml_sb = sb("oml", [128, 512], F32)
        mask_sb = sb("maskb", [128, 4, 128], F32)
        tri_sb = sb("tri", [128, 128], F32)
        trr_sb = sb("trr", [128, 128], F32)
        cind_sb = sb("cind", [128, 2], F32)
        idf_sb = sb("idf", [128, 128], F32)
        idb_sb = sb("idb", [128, 128], BF16)
        one_sb = sb("ones", [128, 128], BF16)
        hg_sb = sb("hgs", [128, 4], F32)
        cw_sb = sb("cws", [128, 4, 3], F32)
        ss_sb = sb("ss", [128, 64], F32)
        sm_sb = sb("smx", [128, 64], F32)
        dec_sb = sb("dec", [128, 4, 8], F32)
        psf = [st.enter_context(nc.psum_tensor("psf%d" % i, [128, 512], F32)) for i in range(6)]
        pst = [st.enter_context(nc.psum_tensor("pst%d" % i, [128, 1024], BF16)) for i in range(2)]
        block = st.enter_context(nc.Block())

        PE, ACT, DVE, POOL, SP = sch.pe, sch.act, sch.dve, sch.pool, sch.sp

        xb = [Bf(x_sb[:, b, :]) for b in range(NB)]
        ringb = [Bf(ring[:, u, :]) for u in range(6)]
        ring_slot = [sch.slot("ring%d" % u) for u in range(6)]
        xl_slot = [sch.slot("xl%d" % b) for b in range(NB)]
        st_slot = [sch.slot("st%d" % b) for b in range(NB)]
        c_slot = sch.slot("const")
        g_slot = sch.slot("gain")
        PSF = [Bf(p[:]) for p in psf]
        PST = [Bf(p[:]) for p in pst]
        gb = Bf(gb_sb[:]); oml = Bf(oml_sb[:]); lbp_sb = a32[:, 0:1024].rearrange("p (a b) -> p a b", a=2); lbp = Bf(lbp_sb); maskb = Bf(mask_sb[:])
        tri = Bf(tri_sb[:]); trr = Bf(trr_sb[:]); cind = Bf(cind_sb[:]); idf = Bf(idf_sb[:])
        idb = Bf(idb_sb[:]); ones = Bf(one_sb[:]); hg = Bf(hg_sb[:]); cw = Bf(cw_sb[:])
        ss = Bf(ss_sb[:]); sm = Bf(sm_sb[:]); dec = Bf(dec_sb[:])
        cnt = {"psf": 0, "pst": 0, "ring": 0}

        from collections import deque
        free_f = deque(PSF); free_t = deque(PST)

        def nps():
            if not free_f:
                raise RuntimeError("PSUM fp32 pool exhausted at emission")
            return free_f.popleft()

        def npt():
            if not free_t:
                raise RuntimeError("PSUM bf16 pool exhausted at emission")
            return free_t.popleft()

        def rel(*bs):
            for b in bs:
                (free_t if b in PST else free_f).append(b)

        def mmg(groups):
            def fn(e):
                ins = None
                for out, pairs in groups:
                    n = len(pairs)
                    for i, (l, r) in enumerate(pairs):
                        ins = e.matmul(out, l, r, start=(i == 0), stop=(i == n - 1))
                return ins
            return fn

        def pe_mm(groups, reads, writes):
            sch.op(PE, mmg(groups), reads, writes)

        def pe_tr(pairs, reads, writes):
            def fn(e):
                ins = None
                for o, i in pairs:
                    ins = e.transpose(o, i, idb.ap)
                return ins
            sch.op(PE, fn, list(reads) + [idb], writes)

        def act(out, in_, func, reads, writes, **kw):
            sch.op(ACT, lambda e: e.activation(out, in_, func, **kw), reads, writes)

        def tt(out, a, b, op, reads, writes, eng=None):
            sch.op(eng or DVE, lambda e: e.tensor_tensor(out, a, b, op), reads, writes)

        def stt(out, a, s, b, op0, op1, reads, writes, eng=None):
            sch.op(eng or DVE, lambda e: e.scalar_tensor_tensor(out, a, s, b, op0, op1), reads, writes)

        def ts(out, a, s1, s2, op0, op1, reads, writes, eng=None):
            if s2 is None:
                sch.op(eng or DVE, lambda e: e.tensor_scalar(out, a, s1, None, op0), reads, writes)
            else:
                sch.op(eng or DVE, lambda e: e.tensor_scalar(out, a, s1, s2, op0, op1), reads, writes)

        def cp(out, a, reads, writes, eng=None):
            if eng is ACT:
                sch.op(ACT, lambda e: e.activation(out, a, AF.Copy), reads, writes)
            else:
                sch.op(eng or DVE, lambda e: e.tensor_copy(out, a), reads, writes)

        def ms(out, val, writes, eng=None):
            sch.op(eng or DVE, lambda e: e.memset(out, val), (), writes)

        def wload(dst_ap, src_ap):
            if not ring_free:
                raise RuntimeError("weight ring exhausted at emission")
            u = ring_free.popleft()
            rb = ringb[u]
            sch.dma(POOL, ring_slot[u], lambda e: e.dma_start(out=dst_ap(ring[:, u, :]), in_=src_ap), (), [rb])
            return rb

        ring_free = deque(range(6))

        def relw(*rbs):
            for rb in rbs:
                ring_free.append(ringb.index(rb))

        def w_cols(wd, c0, n=256):
            src = wd[:, c0:c0 + n].rearrange("(k p) n -> p k n", p=128)
            rb = wload(lambda r: r[:, 0:8 * n].rearrange("p (k n) -> p k n", k=8), src)
            return rb, rb.ap[:, 0:8 * n].rearrange("p (k n) -> p k n", k=8)

        def w_rows(wd, r0):
            src = wd[r0:r0 + 256, :].rearrange("(j p) n -> p j n", p=128)
            rb = wload(lambda r: r.rearrange("p (j n) -> p j n", j=2), src)
            return rb, rb.ap.rearrange("p (j n) -> p j n", j=2)

        def cdma(dst, src, b):
            sch.dma(SP, c_slot, lambda e: e.dma_start(out=dst, in_=src), (), [b])
        cdma(tri.ap, tri_d, tri); cdma(trr.ap, trr_d, trr); cdma(cind.ap, cind_d, cind); cdma(idf.ap, idn_d, idf)
        for h in range(4):
            cdma(mask_sb[:, h, :], tri_d, maskb)
        cdma(hg.ap, hg_d, hg)
        cdma(cw.ap, cw_d.rearrange("(c p) j -> p c j", p=128), cw)
        cdma(lbp_sb[:, 0, :], lbp_d[0:1, :].partition_broadcast(128), lbp)
        cdma(lbp_sb[:, 1, :], lbp_d[1:2, :].partition_broadcast(128), lbp)
        for b_ in (tri, trr, cind, idf, maskb, hg, cw, lbp):
            b_.lw = (c_slot, c_slot.count)
        cp(idb.ap, idf.ap, [idf], [idb])
        ms(ones.ap, 1.0, [ones])
        tt(oml.ap, lbp_sb[:, 1, :], lbp_sb[:, 0, :], ALU.subtract, [lbp], [oml])
        act(oml.ap, oml.ap, AF.Sigmoid, [oml], [oml])
        sch.barrier()

        def load_gain(gd):
            sch.dma(SP, g_slot, lambda e: e.dma_start(out=gb.ap, in_=gd.partition_broadcast(128)), (), [gb])

        def load_x(src, nblk):
            for b in range(nblk):
                sch.dma(SP, xl_slot[b],
                        (lambda b: lambda e: e.dma_start(out=xb[b].ap, in_=src[b * 128:(b + 1) * 128, :]))(b),
                        (), [xb[b]])

        def c16(off, shape):
            n = int(np.prod(shape[1:]))
            ap = a16[:, off:off + n]
            if len(shape) == 3:
                ap = ap.rearrange("p (a b) -> p a b", a=shape[1])
            return ap

        def c32(off, shape):
            n = int(np.prod(shape[1:]))
            ap = a32[:, off:off + n]
            if len(shape) == 3:
                ap = ap.rearrange("p (a b) -> p a b", a=shape[1])
            return ap

        def norm_T(srcs, hT_ap, hT_bufs, hn_bufs, junk, col0=0):
            n = len(srcs)
            ms(ss_sb[:, 0:2 * n], 0.0, [ss])
            for j, s in enumerate(srcs):
                act(junk.ap, s.ap, AF.Square, [s], [junk, ss], accum_out=ss_sb[:, j:j + 1])
            act(ss_sb[:, n:2 * n], ss_sb[:, 0:n], AF.Sqrt, [ss], [ss], scale=1.0 / D, bias=EPS)
            sch.op(DVE, lambda e: e.reciprocal(ss_sb[:, 0:n], ss_sb[:, n:2 * n]), [ss], [ss])
            for j, s in enumerate(srcs):
                hn = hn_bufs[j % len(hn_bufs)]
                stt(hn.ap, s.ap, ss_sb[:, j:j + 1], gb.ap, ALU.mult, ALU.mult, [s, ss, gb], [hn])
                pt = npt()
                pe_tr([(pt.ap[:, k * 128:(k + 1) * 128], hn.ap[:, k * 128:(k + 1) * 128]) for k in range(8)], [hn], [pt])
                cp(hT_ap[:, :, col0 + j * 128:col0 + (j + 1) * 128], pt.ap.rearrange("p (k n) -> p k n", k=8),
                   [pt], [hT_bufs[j]], eng=ACT)
                rel(pt)

        def ffn(gd, wg, wu, wd, nblk):
            hT_ap = c16(0, [128, 8, 2048])
            hTb = [Bf(hT_ap[:, :, b * 128:(b + 1) * 128]) for b in range(nblk)]
            hid_ap = [c16(16384 + i * 4096, [128, 2, 2048]) for i in range(2)]
            hidb = [[[Bf(hid_ap[i][:, j, t * 512:(t + 1) * 512]) for t in range(4)] for j in range(2)] for i in range(2)]
            hn = [Bf(c32(i * 1024, [128, 1024]).bitcast(BF16)[:, 0:1024]) for i in range(2)]
            junk = Bf(c32(2048, [128, 1024]).bitcast(BF16)[:, 0:1024])
            sg = [Bf(c32(3072 + i * 512, [128, 512])) for i in range(3)]
            load_gain(gd)
            norm_T(xb[:nblk], hT_ap, hTb, hn, junk)
            ntt = nblk // 4
            NG = DFF // 256
            units = {}

            def loadg(g):
                units[g] = (w_cols(wg, g * 256), w_cols(wu, g * 256), w_rows(wd, g * 256))

            def gu(g):
                (gb_, gap), (ub_, uap), _ = units[g]
                i = g % 2
                for j in range(2):
                    for t in range(ntt):
                        pg = nps(); pu = nps()
                        rd = [gb_, ub_] + hTb[t * 4:t * 4 + 4]
                        pe_mm([(pg.ap, [(gap[:, k, j * 128:(j + 1) * 128], hT_ap[:, k, t * 512:(t + 1) * 512]) for k in range(8)])], rd, [pg])
                        pe_mm([(pu.ap, [(uap[:, k, j * 128:(j + 1) * 128], hT_ap[:, k, t * 512:(t + 1) * 512]) for k in range(8)])], rd, [pu])
                        s = sg[(j * ntt + t) % 3]
                        act(s.ap, pg.ap, AF.Silu, [pg], [s])
                        tt(hidb[i][j][t].ap, s.ap, pu.ap, ALU.mult, [s, pu], [hidb[i][j][t]])
                        rel(pg, pu)

            def down(g):
                _, _, (db_, dap) = units[g]
                i = g % 2
                for b in range(nblk):
                    for ch in range(2):
                        pd = nps()
                        pe_mm([(pd.ap, [(hid_ap[i][:, j, b * 128:(b + 1) * 128], dap[:, j, ch * 512:(ch + 1) * 512]) for j in range(2)])],
                              [db_, hidb[i][0][b // 4], hidb[i][1][b // 4]], [pd])
                        xs = xb[b].ap[:, ch * 512:(ch + 1) * 512]
                        stt(xs, pd.ap, 0.5, xs, ALU.mult, ALU.add, [pd, xb[b]], [xb[b]])
                        rel(pd)
                relw(units[g][0][0], units[g][1][0], units[g][2][0])
                del units[g]

            loadg(0)
            loadg(1)
            gu(0)
            for g in range(NG):
                if g + 1 < NG:
                    gu(g + 1)
                down(g)
                if g + 2 < NG:
                    loadg(g + 2)

        def interleave(lists, offsets):
            T = max(o + len(l) for l, o in zip(lists, offsets))
            for t in range(T):
                for l, o in zip(lists, offsets):
                    k = t - o
                    if 0 <= k < len(l):
                        l[k]()

        st_ = {"gc": 0}

        def mix_setup():
            m = {}
            m["hT"] = c16(0, [128, 8, 512]); m["hTb"] = [Bf(m["hT"][:, :, b * 128:(b + 1) * 128]) for b in range(4)]
            m["t16"] = [Bf(c16(4096 + i * 512, [128, 512])) for i in range(4)]
            m["v"] = [Bf(c16(6144 + i * 512, [128, 512])) for i in range(4)]
            m["kh"] = [Bf(c16(8192 + i * 512, [128, 512])) for i in range(4)]
            m["ktT"] = c16(10240, [128, 4, 512]); m["ktTb"] = [Bf(m["ktT"][:, :, b * 128:(b + 1) * 128]) for b in range(4)]
            m["qT"] = c16(12288, [128, 4, 512]); m["qTb"] = [Bf(m["qT"][:, h, :]) for h in range(4)]
            m["PT"] = [Bf(c16(14336 + i * 512, [128, 4, 128])) for i in range(2)]
            m["Sbf"] = [Bf(c16(15360 + i * 512, [128, 512])) for i in range(9)]
            m["GT"] = c16(19968, [128, 4, 512]); m["GTb"] = [Bf(m["GT"][:, h, :]) for h in range(4)]
            m["yT"] = c16(22016, [128, 8, 512]); m["yTb"] = [Bf(m["yT"][:, c, :]) for c in range(8)]
            m["A"] = [Bf(c32(i * 512, [128, 512])) for i in range(4)]
            m["Bq"] = [Bf(c32(2048 + i * 512, [128, 512])) for i in range(2)]
            m["C"] = [Bf(c32(3072 + i * 512, [128, 512])) for i in range(4)]
            m["t32"] = [Bf(c32(5120 + i * 512, [128, 512])) for i in range(3)]
            m["ET"] = c32(6656, [128, 4, 512]); m["ETb"] = [Bf(m["ET"][:, :, b * 128:(b + 1) * 128]) for b in range(4)]
            m["S"] = [Bf(c32(8704 + i * 512, [128, 512])) for i in range(2)]
            m["u"] = c32(9728, [128, 4, 514]); m["ub"] = [Bf(m["u"][:, c, :]) for c in range(4)]
            m["hn"] = [Bf(c32(11784 + i * 512, [128, 512]).bitcast(BF16)) for i in range(2)]
            m["junk"] = Bf(c32(12808, [128, 512]).bitcast(BF16))
            m["i16"] = 0; m["i32"] = 0
            return m

        def t16(m):
            m["i16"] += 1
            return m["t16"][m["i16"] % 4]

        def t32(m):
            m["i32"] += 1
            return m["t32"][m["i32"] % 3]

        def mix_init_state(m):
            ms(m["S"][0].ap, 0.0, [m["S"][0]])
            ms(m["Sbf"][0].ap, 0.0, [m["Sbf"][0]])
            ms(m["u"], 0.0, m["ub"])
            st_["gc"] = 0
            st_["si"] = 0

        def mix_restore_state(m):
            sb_c = m["Sbf"][st_["gc"] % 9]
            cp(sb_c.ap, m["S"][st_["si"] % 2].ap, [m["S"][st_["si"] % 2]], [sb_c], eng=ACT)

        def fm_units(m, c0, nchunks, consume):
            units = []
            hold = {}
            for q in range(0, nchunks, 2):
                for j in range(2):
                    def u(q=q, j=j):
                        if j == 0:
                            hold[q] = w_cols(w_in, c0 + q * 128)
                        rb, wap = hold[q]
                        p = nps()
                        pe_mm([(p.ap, [(wap[:, k, j * 128:(j + 1) * 128], m["hT"][:, k, :]) for k in range(8)])],
                              [rb] + m["hTb"], [p])
                        consume(q + j, p)
                        rel(p)
                        if j == 1:
                            relw(rb)
                    units.append(u)
            return units

        def fm_proj(m, c0, nchunks, consume):
            for u in fm_units(m, c0, nchunks, consume):
                u()

        def mix_tile(m, blks, state_only, want_u):
            load_gain(g_mix)
            norm_T(blks, m["hT"], m["hTb"], m["hn"], m["junk"])
            hT = m["hT"]
            wf = [w_cols(w_in, 512), w_cols(w_in, 768)]
            wi = [w_cols(w_in, 1024), w_cols(w_in, 1280)]
            vbs = m["v"]; Sidx = [None] * 8
            TB = [slice(b * 128, (b + 1) * 128) for b in range(4)]
            A = m["A"]; C = m["C"]; Bq = [m["Bq"][b % 2] for b in range(4)]; khb = m["kh"]

            def tm_stages(grp):
                pzf = {}; pzi = {}; pBr = {}; pDc = {}; pB = {}; pBT = {}; kt = {}; pS = {}
                L = []

                def s_zf():
                    for b in grp:
                        pzf[b] = nps()
                        pe_mm([(pzf[b].ap[:, q * 256:(q + 1) * 256], [(hT[:, k, TB[b]], wf[q][1][:, k, :]) for k in range(8)]) for q in range(2)],
                              [wf[0][0], wf[1][0], m["hTb"][b]], [pzf[b]])
                L.append(s_zf)

                def s_zi():
                    for b in grp:
                        pzi[b] = nps()
                        pe_mm([(pzi[b].ap[:, q * 256:(q + 1) * 256], [(hT[:, k, TB[b]], wi[q][1][:, k, :]) for k in range(8)]) for q in range(2)],
                              [wi[0][0], wi[1][0], m["hTb"][b]], [pzi[b]])
                L.append(s_zi)

                def s_sig():
                    for b in grp:
                        act(A[b].ap, pzf[b].ap, AF.Sigmoid, [pzf[b]], [A[b]], scale=-1.0)
                        rel(pzf[b])
                L.append(s_sig)

                def s_v():
                    for b in grp:
                        act(vbs[b].ap, pzi[b].ap, AF.Copy, [pzi[b]], [vbs[b]])
                        rel(pzi[b])
                L.append(s_v)

                def s_k():
                    for b in grp:
                        tt(A[b].ap, A[b].ap, oml.ap, ALU.mult, [A[b], oml], [A[b]])
                L.append(s_k)

                def s_ln():
                    for b in grp:
                        act(Bq[b].ap, A[b].ap, AF.Ln, [A[b]], [Bq[b]], scale=-1.0, bias=1.0)
                L.append(s_ln)

                def s_cum():
                    for b in grp:
                        pBr[b] = nps()
                        pe_mm([(pBr[b].ap, [(trr.ap, Bq[b].ap)])], [trr, Bq[b]], [pBr[b]])
                        if state_only:
                            pDc[b] = nps()
                            pe_mm([(pDc[b].ap[:, h * 2:(h + 1) * 2], [(Bq[b].ap[:, h * 128:(h + 1) * 128], cind.ap)]) for h in range(4)],
                                  [cind, Bq[b]], [pDc[b]])
                        else:
                            pB[b] = nps()
                            pe_mm([(pB[b].ap, [(tri.ap, Bq[b].ap)])], [tri, Bq[b]], [pB[b]])
                            pBT[b] = nps()
                            pe_mm([(pBT[b].ap[:, h * 128:(h + 1) * 128], [(Bq[b].ap[:, h * 128:(h + 1) * 128], tri.ap)]) for h in range(4)],
                                  [tri, Bq[b]], [pBT[b]])
                L.append(s_cum)

                def s_er():
                    for b in grp:
                        act(C[b].ap, pBr[b].ap, AF.Exp, [pBr[b]], [C[b]])
                        rel(pBr[b])
                L.append(s_er)

                def s_kh():
                    for b in grp:
                        tt(khb[b].ap, A[b].ap, C[b].ap, ALU.mult, [A[b], C[b]], [khb[b]])
                L.append(s_kh)

                if state_only:
                    def s_dec():
                        for b in grp:
                            act(dec_sb[:, b, :], pDc[b].ap[:, 0:8], AF.Exp, [pDc[b]], [dec])
                            rel(pDc[b])
                    L.append(s_dec)
                else:
                    def s_et():
                        for b in grp:
                            act(m["ET"][:, :, TB[b]], pBT[b].ap.rearrange("p (h n) -> p h n", h=4), AF.Exp, [pBT[b]], [m["ETb"][b]])
                            rel(pBT[b])
                    L.append(s_et)

                    def s_eb():
                        for b in grp:
                            act(C[b].ap, pB[b].ap, AF.Exp, [pB[b]], [C[b]], scale=-1.0)
                            rel(pB[b])
                    L.append(s_eb)

                    def s_kt():
                        for b in grp:
                            kt[b] = t16(m)
                            tt(kt[b].ap, A[b].ap, C[b].ap, ALU.mult, [A[b], C[b]], [kt[b]])
                    L.append(s_kt)

                    def s_tr():
                        for b in grp:
                            pt = npt()
                            pe_tr([(pt.ap[:, h * 128:(h + 1) * 128], kt[b].ap[:, h * 128:(h + 1) * 128]) for h in range(4)], [kt[b]], [pt])
                            cp(m["ktT"][:, :, TB[b]], pt.ap[:, 0:512].rearrange("p (h n) -> p h n", h=4), [pt], [m["ktTb"][b]])
                            rel(pt)
                    L.append(s_tr)

                def s_ps():
                    for b in grp:
                        for c in range(2):
                            cs = slice(c * 64, (c + 1) * 64)
                            pS[b, c] = nps()
                            pe_mm([(pS[b, c].ap[:, h * 128:(h + 1) * 128], [(khb[b].ap[cs, h * 128:(h + 1) * 128], vbs[b].ap[cs, h * 128:(h + 1) * 128])]) for h in range(4)],
                                  [khb[b], vbs[b]], [pS[b, c]])
                L.append(s_ps)

                def s_chain():
                    for b in grp:
                        for c in range(2):
                            So = m["S"][st_["si"] % 2]; Sn = m["S"][(st_["si"] + 1) % 2]
                            st_["si"] += 1
                            for h in range(4):
                                hs = slice(h * 128, (h + 1) * 128)
                                if state_only:
                                    dap = dec_sb[:, b, h * 2 + c:h * 2 + c + 1]; dbf = dec
                                else:
                                    col = b * 128 + c * 64 + 63
                                    dap = m["ET"][:, h, col:col + 1]; dbf = m["ETb"][b]
                                stt(Sn.ap[:, hs], So.ap[:, hs], dap, pS[b, c].ap[:, hs], ALU.mult, ALU.add, [So, dbf, pS[b, c]], [Sn])
                            Sidx[b * 2 + c] = st_["gc"] % 9
                            st_["gc"] += 1
                            sb_n = m["Sbf"][st_["gc"] % 9]
                            cp(sb_n.ap, Sn.ap, [Sn], [sb_n])
                            rel(pS[b, c])
                L.append(s_chain)
                return L

            pzc = {}

            def take_c(c, p):
                zcs = t32(m)
                act(zcs.ap, p.ap, AF.Copy, [p], [zcs])
                pzc[c] = zcs

            def take_u(c, p):
                tt(m["u"][:, c, 2:514], pzc[c].ap, p.ap, ALU.mult, [pzc[c], p], [m["ub"][c]])

            def halo(c):
                cp(m["u"][:, c, 0:2], m["u"][:, c, 512:514], [m["ub"][c]], [m["ub"][c]])

            def conv_u_units():
                us = []
                for q in (0, 2):
                    us += fm_units(m, 2560 + q * 128, 2, (lambda q: lambda c, p: take_c(q + c, p))(q))
                    us += fm_units(m, 3072 + q * 128, 2, (lambda q: lambda c, p: take_u(q + c, p))(q))
                return us

            def take_g(h, p):
                act(m["GT"][:, h, :], p.ap, AF.Silu, [p], [m["GTb"][h]])

            fill = []
            if not state_only:
                fill = fm_units(m, 1536, 4, take_g) + conv_u_units()
            elif want_u:
                fill = conv_u_units()
            interleave([tm_stages((b,)) for b in range(4)] + [fill], TM_OFF)
            relw(wf[0][0], wf[1][0], wi[0][0], wi[1][0])
            if state_only:
                if want_u:
                    for c in range(4):
                        halo(c)
                return
            def take_q(h, p):
                sq = t32(m)
                act(sq.ap, p.ap, AF.Silu, [p], [sq])
                stt(m["qT"][:, h, :], sq.ap, 128.0 ** -0.5, m["ET"][:, h, :], ALU.mult, ALU.mult, [sq] + m["ETb"], [m["qTb"][h]])
            fm_proj(m, 0, 4, take_q)


            def o_stages(grp):
                pSc = {}; pO = {}; pN = {}; sqo = {}
                L = []

                def s0():
                    for b in grp:
                        pSc[b] = nps()
                        pe_mm([(pSc[b].ap[:, h * 128:(h + 1) * 128], [(m["ktT"][:, h, TB[b]], m["qT"][:, h, TB[b]])]) for h in range(4)],
                              [m["ktTb"][b]] + m["qTb"], [pSc[b]])
                L.append(s0)

                def s1():
                    for b in grp:
                        PT = m["PT"][b % 2]
                        tt(PT.ap, pSc[b].ap.rearrange("p (h n) -> p h n", h=4), maskb.ap, ALU.mult, [pSc[b], maskb], [PT])
                        rel(pSc[b])
                L.append(s1)

                def s2():
                    for b in grp:
                        PT = m["PT"][b % 2]; v = vbs[b]
                        pO[b] = nps()
                        groups = []
                        rds = [v, PT] + m["qTb"]
                        for h in range(4):
                            for c in range(2):
                                cs = slice(c * 64, (c + 1) * 64)
                                sbf = m["Sbf"][Sidx[b * 2 + c]]
                                rds.append(sbf)
                                groups.append((pO[b].ap[:, h * 128 + c * 64:h * 128 + (c + 1) * 64],
                                               [(v.ap[cs, h * 128:(h + 1) * 128], PT.ap[cs, h, c * 64:(c + 1) * 64]),
                                                (sbf.ap[:, h * 128:(h + 1) * 128], m["qT"][:, h, b * 128 + c * 64:b * 128 + (c + 1) * 64])]))
                        pe_mm(groups, rds, [pO[b]])
                L.append(s2)

                def s3():
                    for b in grp:
                        sqo[b] = t16(m)
                        act(sqo[b].ap, pO[b].ap, AF.Square, [pO[b]], [sqo[b]])
                L.append(s3)

                def s4():
                    for b in grp:
                        pN[b] = nps()
                        pe_mm([(pN[b].ap[:, h * 128:(h + 1) * 128], [(ones.ap, sqo[b].ap[:, h * 128:(h + 1) * 128])]) for h in range(4)],
                              [ones, sqo[b]], [pN[b]])
                L.append(s4)

                def s5():
                    for b in grp:
                        act(A[b].ap, pN[b].ap, AF.Sqrt, [pN[b]], [A[b]], scale=1.0 / 128, bias=EPS)
                        rel(pN[b])
                L.append(s5)

                def s6():
                    for b in grp:
                        sch.op(DVE, (lambda o, i: lambda e: e.reciprocal(o, i))(C[b].ap, A[b].ap), [A[b]], [C[b]])
                L.append(s6)

                def s7():
                    for b in grp:
                        tt(A[b].ap, pO[b].ap, C[b].ap, ALU.mult, [pO[b], C[b]], [A[b]])
                        rel(pO[b])
                L.append(s7)

                def s8():
                    for b in grp:
                        for h in range(4):
                            stt(m["yT"][:, h, TB[b]], A[b].ap[:, h * 128:(h + 1) * 128], hg_sb[:, h:h + 1], m["GT"][:, h, TB[b]], ALU.mult, ALU.mult,
                                [A[b], hg, m["GTb"][h]], [m["yTb"][h]])
                L.append(s8)
                return L

            def take_b(c, p):
                a1 = t32(m)
                ts(a1.ap, m["u"][:, c, 2:514], cw_sb[:, c, 2:3], None, ALU.mult, None, [m["ub"][c], cw], [a1])
                a2 = t32(m)
                stt(a2.ap, m["u"][:, c, 1:513], cw_sb[:, c, 1:2], a1.ap, ALU.mult, ALU.add, [m["ub"][c], cw, a1], [a2])
                a3 = t32(m)
                stt(a3.ap, m["u"][:, c, 0:512], cw_sb[:, c, 0:1], a2.ap, ALU.mult, ALU.add, [m["ub"][c], cw, a2], [a3])
                tt(m["yT"][:, 4 + c, :], a3.ap, p.ap, ALU.mult, [a3, p], [m["yTb"][4 + c]])
                halo(c)

            interleave([o_stages((b,)) for b in range(4)] + [fm_units(m, 2048, 4, take_b)], O_OFF)

            for q in range(4):
                rb, wap = w_cols(w_out, q * 256)
                for b in range(4):
                    tb = slice(b * 128, (b + 1) * 128)
                    p = nps()
                    pe_mm([(p.ap[:, 0:256], [(m["yT"][:, fc, tb], wap[:, fc, :]) for fc in range(8)])], [rb] + m["yTb"], [p])
                    xs = blks[b].ap[:, q * 256:(q + 1) * 256]
                    tt(xs, p.ap[:, 0:256], xs, ALU.add, [p, blks[b]], [blks[b]])
                    rel(p)
                relw(rb)

        def xattn():
            hT = c16(0, [128, 8, 512]); hTb = [Bf(hT[:, :, b * 128:(b + 1) * 128]) for b in range(4)]
            qmT = c16(4096, [128, 8, 512]); qmTb = [Bf(qmT[:, c, :]) for c in range(8)]
            kmT = c16(8192, [128, 8, 256]); kmTb = Bf(kmT)
            vm = c16(10240, [128, 2, 1024]); vmb = Bf(vm)
            mnT = c16(12288, [128, 8, 256]); mnTb = [Bf(mnT[:, :, b * 128:(b + 1) * 128]) for b in range(2)]
            pbuf = [Bf(c16(12288 + i * 1024, [128, 4, 256])) for i in range(4)]
            pTb = [Bf(c16(16384 + i * 1024, [128, 8, 128])) for i in range(4)]
            attT = c16(20480, [128, 8, 512]); attTb = [Bf(attT[:, :, b * 128:(b + 1) * 128]) for b in range(4)]
            hn = [Bf(c32(i * 512, [128, 512]).bitcast(BF16)) for i in range(2)]
            junk = Bf(c32(1024, [128, 512]).bitcast(BF16))
            memx = [Bf(c32(1536 + i * 1024, [128, 1024])) for i in range(2)]
            pe_ = [Bf(c32(3584 + i * 1024, [128, 4, 256])) for i in range(4)]
            scale = 256.0 ** -0.5
            sm4 = c32(7680, [128, 128])
            for i in range(2):
                sch.dma(SP, xl_slot[i], (lambda i: lambda e: e.dma_start(out=memx[i].ap, in_=mem_d[i * 128:(i + 1) * 128, :]))(i), (), [memx[i]])
            load_gain(g_mem)
            norm_T(memx, mnT, mnTb, hn, junk)
            for q in range(4):
                rb, wap = w_cols(w_kv, q * 256)
                for j in range(2):
                    p = nps()
                    pe_mm([(p.ap[:, 0:256], [(wap[:, k, j * 128:(j + 1) * 128], mnT[:, k, :]) for k in range(8)])], [rb] + mnTb, [p])
                    cp(kmT[:, q * 2 + j, :], p.ap[:, 0:256], [p], [kmTb], eng=ACT)
                    rel(p)
                relw(rb)
            for q in range(4):
                rb, wap = w_cols(w_kv, 1024 + q * 256)
                for mb in range(2):
                    p = nps()
                    pe_mm([(p.ap[:, 0:256], [(mnT[:, k, mb * 128:(mb + 1) * 128], wap[:, k, :]) for k in range(8)])], [rb] + mnTb, [p])
                    cp(vm[:, mb, q * 256:(q + 1) * 256], p.ap[:, 0:256], [p], [vmb], eng=ACT)
                    rel(p)
                relw(rb)
            sch.barrier()
            for ti in range(4):
                blks = xb[ti * 4:ti * 4 + 4]
                load_gain(g_xat)
                norm_T(blks, hT, hTb, hn, junk)
                for q in range(4):
                    rb, wap = w_cols(w_q, q * 256)
                    for j in range(2):
                        p = nps()
                        pe_mm([(p.ap, [(wap[:, k, j * 128:(j + 1) * 128], hT[:, k, :]) for k in range(8)])], [rb] + hTb, [p])
                        cp(qmT[:, q * 2 + j, :], p.ap, [p], [qmTb[q * 2 + j]], eng=ACT if j else DVE)
                        rel(p)
                    relw(rb)
                TB = [slice(b * 128, (b + 1) * 128) for b in range(4)]
                def a_stages(grp):
                    so = {b: b * 32 for b in grp}
                    smb = {b: Bf(sm4[:, b * 32:(b + 1) * 32]) for b in grp}
                    pscs = {}; pts = {}; pAs = {}
                    L = []

                    def s0():
                        for b in grp:
                            ms(sm4[:, so[b]:so[b] + 4], 0.0, [smb[b]])
                        for b in grp:
                            for hp in range(2):
                                p = nps()
                                pe_mm([(p.ap[:, hh * 256:(hh + 1) * 256],
                                        [(qmT[:, (hp * 2 + hh) * 2 + dc, TB[b]], kmT[:, (hp * 2 + hh) * 2 + dc, :]) for dc in range(2)]) for hh in range(2)],
                                      qmTb + [kmTb], [p])
                                pscs[b, hp] = p
                    L.append(s0)

                    def s1():
                        for b in grp:
                            for hp in range(2):
                                p = pscs[b, hp]
                                sch.op(DVE, (lambda o, i: lambda e: e.tensor_reduce(o, i, AX.X, ALU.max))(
                                    sm4[:, so[b] + 8 + hp * 2:so[b] + 8 + hp * 2 + 2], p.ap.rearrange("p (h n) -> p h n", h=2)), [p], [smb[b]])
                            ts(sm4[:, so[b] + 16:so[b] + 20], sm4[:, so[b] + 8:so[b] + 12], -scale, None, ALU.mult, None, [smb[b]], [smb[b]])
                    L.append(s1)

                    def s2():
                        for b in grp:
                            for h in range(4):
                                p = pscs[b, h // 2]
                                act(pe_[b].ap[:, h, :], p.ap[:, (h % 2) * 256:(h % 2 + 1) * 256], AF.Exp, [p, smb[b]], [pe_[b], smb[b]],
                                    scale=scale, bias=sm4[:, so[b] + 16 + h:so[b] + 17 + h], accum_out=sm4[:, so[b] + h:so[b] + h + 1])
                            rel(pscs[b, 0], pscs[b, 1])
                    L.append(s2)

                    def s3():
                        for b in grp:
                            sch.op(DVE, (lambda o: lambda e: e.reciprocal(sm4[:, o + 24:o + 28], sm4[:, o:o + 4]))(so[b]), [smb[b]], [smb[b]])
                            for h in range(4):
                                ts(pbuf[b].ap[:, h, :], pe_[b].ap[:, h, :], sm4[:, so[b] + 24 + h:so[b] + 25 + h], None, ALU.mult, None,
                                   [pe_[b], smb[b]], [pbuf[b]])
                    L.append(s3)

                    def s4():
                        for b in grp:
                            pts[b] = npt()
                            pe_tr([(pts[b].ap[:, (h * 2 + mc) * 128:(h * 2 + mc + 1) * 128], pbuf[b].ap[:, h, mc * 128:(mc + 1) * 128])
                                   for h in range(4) for mc in range(2)], [pbuf[b]], [pts[b]])
                    L.append(s4)

                    def s5():
                        for b in grp:
                            cp(pTb[b].ap, pts[b].ap.rearrange("p (k n) -> p k n", k=8), [pts[b]], [pTb[b]], eng=ACT)
                            rel(pts[b])
                    L.append(s5)

                    def s6():
                        for b in grp:
                            pT = pTb[b]
                            for half in range(2):
                                pA = nps()
                                pe_mm([(pA.ap[:, (ci - half * 4) * 128:(ci - half * 4 + 1) * 128],
                                        [(vm[:, mc, ci * 128:(ci + 1) * 128], pT.ap[:, (ci // 2) * 2 + mc, :]) for mc in range(2)])
                                       for ci in range(half * 4, half * 4 + 4)], [vmb, pT], [pA])
                                pAs[b, half] = pA
                    L.append(s6)

                    def s7():
                        for b in grp:
                            cp(attT[:, 0:4, TB[b]], pAs[b, 0].ap.rearrange("p (k n) -> p k n", k=4), [pAs[b, 0]], [attTb[b]], eng=ACT)
                            cp(attT[:, 4:8, TB[b]], pAs[b, 1].ap.rearrange("p (k n) -> p k n", k=4), [pAs[b, 1]], [attTb[b]])
                            rel(pAs[b, 0], pAs[b, 1])
                    L.append(s7)
                    return L

                interleave([a_stages((b,)) for b in range(4)], A_OFF)
                for q in range(4):
                    rb, wap = w_cols(w_o, q * 256)
                    for b in range(4):
                        tb = slice(b * 128, (b + 1) * 128)
                        p = nps()
                        pe_mm([(p.ap[:, 0:256], [(attT[:, fc, tb], wap[:, fc, :]) for fc in range(8)])], [rb] + attTb, [p])
                        xs = blks[b].ap[:, q * 256:(q + 1) * 256]
                        tt(xs, p.ap[:, 0:256], xs, ALU.add, [p, blks[b]], [blks[b]])
                        rel(p)
                    relw(rb)

        def final(do_norm=True):
            ob = [Bf(c32(i * 1024, [128, 1024])) for i in range(2)]
            junk = Bf(c32(2048, [128, 512]).bitcast(BF16))
            if do_norm:
                load_gain(g_fin)
                ms(ss_sb[:, 0:32], 0.0, [ss])
                for b in range(NB):
                    act(junk.ap, xb[b].ap, AF.Square, [xb[b]], [junk, ss], accum_out=ss_sb[:, b:b + 1])
                act(ss_sb[:, 16:32], ss_sb[:, 0:16], AF.Sqrt, [ss], [ss], scale=1.0 / D, bias=EPS)
                sch.op(DVE, lambda e: e.reciprocal(ss_sb[:, 0:16], ss_sb[:, 16:32]), [ss], [ss])
            for b in range(NB):
                o = ob[b % 2]
                if do_norm:
                    stt(o.ap, xb[b].ap, ss_sb[:, b:b + 1], gb.ap, ALU.mult, ALU.mult, [xb[b], ss, gb], [o])
                else:
                    cp(o.ap, xb[b].ap, [xb[b]], [o])
                sch.dma(SP, st_slot[b], (lambda b, o: lambda e: e.dma_start(out=out_d[b * 128:(b + 1) * 128, :], in_=o.ap))(b, o), [o], ())

        m = None
        if pre_tok > 0 and (stop_after is None or stop_after >= 2):
            load_x(xp_d, npb)
            ffn(g_ffn1, w1g, w1u, w1d, npb)
            sch.barrier()
            m = mix_setup()
            mix_init_state(m)
            for ti in range(npb // 4):
                mix_tile(m, xb[ti * 4:ti * 4 + 4], True, ti == npb // 4 - 1)
            sch.barrier()
        load_x(x_d, NB)
        ffn(g_ffn1, w1g, w1u, w1d, NB)
        sch.barrier()
        if stop_after is None or stop_after >= 2:
            if m is None:
                m = mix_setup()
                mix_init_state(m)
            else:
                mix_restore_state(m)
            for ti in range(4):
                mix_tile(m, xb[ti * 4:ti * 4 + 4], False, False)
            sch.barrier()
        if stop_after is None or stop_after >= 3:
            xattn()
            sch.barrier()
        if stop_after is None or stop_after >= 4:
            ffn(g_ffn2, w2g, w2u, w2d, NB)
            sch.barrier()
        final(do_norm=(stop_after is None))
        sch.finish()
        sch.replay(block)
    return nc


def _consts():
    i = np.arange(128)
    same = (i[:, None] // 64) == (i[None, :] // 64)
    tri = (same & (i[:, None] <= i[None, :])).astype(np.float32)
    trirev = (same & (i[:, None] > i[None, :])).astype(np.float32)
    cind = np.stack([(i < 64), (i >= 64)], axis=1).astype(np.float32)
    return tri, trirev, cind, np.eye(128, dtype=np.float32)


def make_in_maps(inputs, pre_tok=PRE_TOK):
    f = lambda k: np.ascontiguousarray(np.asarray(inputs[k], dtype=np.float32))
    x = f("x"); mem = f("mem")
    tri, trirev, cind, ident = _consts()
    shared = {
        "ffn1_norm": f("ffn1_norm").reshape(1, D), "mix_norm": f("mix_norm").reshape(1, D),
        "xattn_norm": f("xattn_norm").reshape(1, D), "mem_norm": f("mem_norm").reshape(1, D),
        "ffn2_norm": f("ffn2_norm").reshape(1, D), "final_norm": f("final_norm").reshape(1, D),
        "ffn1_gate": f("ffn1_gate")[0], "ffn1_up": f("ffn1_up")[0], "ffn1_down": f("ffn1_down")[0],
        "ffn2_gate": f("ffn2_gate")[0], "ffn2_up": f("ffn2_up")[0], "ffn2_down": f("ffn2_down")[0],
        "w_in": f("w_in")[0], "w_out": f("w_out")[0], "w_q_mem": f("w_q_mem")[0],
        "w_kv_mem": f("w_kv_mem")[0], "w_o_mem": f("w_o_mem")[0],
        "lb_param": f("lb_param"), "hg": np.ascontiguousarray(f("hgrn_out_norm").reshape(4, 128).T),
        "conv_w": f("conv_w")[0],
        "c_tri": tri, "c_trirev": trirev, "c_cind": cind, "c_ident": ident,
    }
    maps = []
    npre = max(pre_tok, 128)
    for c in range(8):
        b, half = c // 2, c % 2
        t0 = half * TOK
        if half == 0 or pre_tok == 0:
            xp = np.zeros((npre, D), np.float32)
        else:
            xp = np.ascontiguousarray(x[b, t0 - npre:t0])
        d = dict(shared)
        d["x"] = np.ascontiguousarray(x[b, t0:t0 + TOK])
        d["xp"] = xp
        d["mem"] = mem[b]
        maps.append(d)
    return maps


def kernel(**inputs):
    nc = build_nc()
    maps = make_in_maps(inputs)
    res = run_bass_kernel_spmd(nc, maps, core_ids=list(range(8)))
    out = np.empty((4, 2 * TOK, D), np.float32)
    for c in range(8):
        out[c // 2, (c % 2) * TOK:(c % 2 + 1) * TOK] = res.results[c]["out"]
    return out
```

```python
import numpy as np
from contextlib import ExitStack
import concourse.bass as bass
import concourse.mybir as mybir
from concourse.bass_utils import run_bass_kernel_spmd

F32 = mybir.dt.float32
BF16 = mybir.dt.bfloat16
AF = mybir.ActivationFunctionType
ALU = mybir.AluOpType
AX = mybir.AxisListType

D = 1024
DFF = 2816
TOK = 2048
NB = TOK // 128
EPS = 1e-6
PRE_TOK = 2048
STOP_AFTER = None
TM_OFF = [0, 3, 6, 9, 2]
O_OFF = [0, 2, 4, 6, 1]
A_OFF = [0, 2, 4, 6]


class _Src:
    def __init__(self, name, sem):
        self.name = name
        self.sem = sem
        self.count = 0


class _Eng(_Src):
    def __init__(self, name, sem):
        super().__init__(name, sem)
        self.items = []
        self.waited = {}


class Bf:
    __slots__ = ("ap", "lw", "rd")

    def __init__(self, ap):
        self.ap = ap
        self.lw = None
        self.rd = {}


class Sched:
    def __init__(self, nc, stack):
        self.nc = nc
        self.stack = stack
        self.srcs = []
        self.pe = self._eng("pe")
        self.act = self._eng("act")
        self.dve = self._eng("dve")
        self.pool = self._eng("pool")
        self.sp = self._eng("sp")
        self.engs = [self.pe, self.act, self.dve, self.pool, self.sp]

    def _eng(self, name):
        e = _Eng(name, self.stack.enter_context(self.nc.semaphore("s_" + name)))
        self.srcs.append(e)
        return e

    def slot(self, name):
        s = _Src(name, self.stack.enter_context(self.nc.semaphore("d_" + name)))
        self.srcs.append(s)
        return s

    def _wait(self, eng, src, val):
        if val <= 0 or eng.waited.get(src, 0) >= val:
            return
        eng.waited[src] = val
        eng.items.append(("w", src, val))

    def op(self, eng, fn, reads=(), writes=()):
        for t in reads:
            if t.lw is not None:
                src, val = t.lw
                if src is eng and eng is self.pe:
                    continue
                self._wait(eng, src, val)
        for t in writes:
            if t.lw is not None and t.lw[0] is not eng:
                self._wait(eng, *t.lw)
            for src, val in t.rd.items():
                if src is not eng:
                    self._wait(eng, src, val)
        eng.count += 1
        eng.items.append(("o", fn, eng, eng.count))
        for t in reads:
            t.rd[eng] = eng.count
        for t in writes:
            t.lw = (eng, eng.count)
            t.rd = {}

    def dma(self, q, slot, fn, reads=(), writes=()):
        for t in reads:
            if t.lw is not None:
                self._wait(q, *t.lw)
        for t in writes:
            if t.lw is not None:
                self._wait(q, *t.lw)
            for src, val in t.rd.items():
                self._wait(q, src, val)
        slot.count += 16
        q.items.append(("o", fn, slot, slot.count))
        for t in reads:
            t.rd[slot] = slot.count
        for t in writes:
            t.lw = (slot, slot.count)
            t.rd = {}

    def barrier(self):
        for e in self.engs:
            for s in self.srcs:
                if s is not e:
                    self._wait(e, s, s.count)

    def finish(self):
        for s in self.srcs:
            if s is not self.sp:
                self._wait(self.sp, s, s.count)

    def replay(self, block):
        import bisect
        need = {}
        for e in self.engs:
            for it in e.items:
                if it[0] == "w":
                    need.setdefault(it[1], set()).add(it[2])
        rank = {src: sorted(v) for src, v in need.items()}
        engset = set(self.engs)

        def run(items):
            def body(e):
                for it in items:
                    if it[0] == "w":
                        src, val = it[1], it[2]
                        if src in engset:
                            val = bisect.bisect_left(rank[src], val) + 1
                        e.wait_ge(src.sem, val)
                    else:
                        ins = it[1](e)
                        src, c = it[2], it[3]
                        if src in engset:
                            if c in need.get(src, ()):
                                ins.then_inc(src.sem, 1)
                        else:
                            ins.then_inc(src.sem, 16)
            return body
        block.tensor(run(self.pe.items))
        block.scalar(run(self.act.items))
        block.vector(run(self.dve.items))
        block.gpsimd(run(self.pool.items))
        block.sync(run(self.sp.items))


def build_nc(pre_tok=PRE_TOK, stop_after=STOP_AFTER):
    nc = bass.Bass("TRN2", target_bir_lowering=False)
    npb = pre_tok // 128

    def din(name, shape):
        return nc.dram_tensor(name, list(shape), F32, kind="ExternalInput").ap()

    x_d = din("x", [TOK, D])
    xp_d = din("xp", [max(pre_tok, 128), D])
    mem_d = din("mem", [256, D])
    g_ffn1 = din("ffn1_norm", [1, D]); g_mix = din("mix_norm", [1, D]); g_xat = din("xattn_norm", [1, D])
    g_mem = din("mem_norm", [1, D]); g_ffn2 = din("ffn2_norm", [1, D]); g_fin = din("final_norm", [1, D])
    w1g = din("ffn1_gate", [D, DFF]); w1u = din("ffn1_up", [D, DFF]); w1d = din("ffn1_down", [DFF, D])
    w2g = din("ffn2_gate", [D, DFF]); w2u = din("ffn2_up", [D, DFF]); w2d = din("ffn2_down", [DFF, D])
    w_in = din("w_in", [D, 3584]); w_out = din("w_out", [D, D])
    w_q = din("w_q_mem", [D, D]); w_kv = din("w_kv_mem", [D, 2 * D]); w_o = din("w_o_mem", [D, D])
    lbp_d = din("lb_param", [2, 512]); hg_d = din("hg", [128, 4]); cw_d = din("conv_w", [512, 3])
    tri_d = din("c_tri", [128, 128]); trr_d = din("c_trirev", [128, 128]); cind_d = din("c_cind", [128, 2])
    idn_d = din("c_ident", [128, 128])
    out_d = nc.dram_tensor("out", [TOK, D], F32, kind="ExternalOutput").ap()
    wscr_d = nc.dram_tensor("wscr", [26, 128, 2048], BF16).ap()

    with ExitStack() as st:
        sch = Sched(nc, st)
        sb = lambda name, shape, dt: st.enter_context(nc.sbuf_tensor(name, list(shape), dt))
        x_sb = sb("x_sb", [128, NB, D], F32)
        ring = sb("ring", [128, 6, 2048], BF16)
        a16 = sb("a16", [128, 26112], BF16)
        a32 = sb("a32", [128, 13376], F32)
        gb_sb = sb("gb", [128, D], F32)
        oml_sb = sb("oml", [128, 512], F32)
        mask_sb = sb("maskb", [128, 4, 128], F32)
        tri_sb = sb("tri", [128, 128], F32)
        trr_sb = sb("trr", [128, 128], F32)
        cind_sb = sb("cind", [128, 2], F32)
        idf_sb = sb("idf", [128, 128], F32)
        idb_sb = sb("idb", [128, 128], BF16)
        one_sb = sb("ones", [128, 128], BF16)
        hg_sb = sb("hgs", [128, 4], F32)
        cw_sb = sb("cws", [128, 4, 3], F32)
        ss_sb = sb("ss", [128, 64], F32)
        sm_sb = sb("smx", [128, 64], F32)
        dec_sb = sb("dec", [128, 4, 8], F32)
        psf = [st.enter_context(nc.psum_tensor("psf%d" % i, [128, 512], F32)) for i in range(6)]
        pst = [st.enter_context(nc.psum_tensor("pst%d" % i, [128, 1024], BF16)) for i in range(2)]
        block = st.enter_context(nc.Block())

        PE, ACT, DVE, POOL, SP = sch.pe, sch.act, sch.dve, sch.pool, sch.sp

        xb = [Bf(x_sb[:, b, :]) for b in range(NB)]
        ringb = [Bf(ring[:, u, :]) for u in range(6)]
        ring_slot = [sch.slot("ring%d" % u) for u in range(6)]
        xl_slot = [sch.slot("xl%d" % b) for b in range(NB)]
        st_slot = [sch.slot("st%d" % b) for b in range(NB)]
        c_slot = sch.slot("const")
        g_slot = sch.slot("gain")
        PSF = [Bf(p[:]) for p in psf]
        PST = [Bf(p[:]) for p in pst]
        gb = Bf(gb_sb[:]); oml = Bf(oml_sb[:]); lbp_sb = a32[:, 0:1024].rearrange("p (a b) -> p a b", a=2); lbp = Bf(lbp_sb); maskb = Bf(mask_sb[:])
        tri = Bf(tri_sb[:]); trr = Bf(trr_sb[:]); cind = Bf(cind_sb[:]); idf = Bf(idf_sb[:])
        idb = Bf(idb_sb[:]); ones = Bf(one_sb[:]); hg = Bf(hg_sb[:]); cw = Bf(cw_sb[:])
        ss = Bf(ss_sb[:]); sm = Bf(sm_sb[:]); dec = Bf(dec_sb[:])
        cnt = {"psf": 0, "pst": 0, "ring": 0}

        from collections import deque
        free_f = deque(PSF); free_t = deque(PST)

        def nps():
            if not free_f:
                raise RuntimeError("PSUM fp32 pool exhausted at emission")
            return free_f.popleft()

        def npt():
            if not free_t:
                raise RuntimeError("PSUM bf16 pool exhausted at emission")
            return free_t.popleft()

        def rel(*bs):
            for b in bs:
                (free_t if b in PST else free_f).append(b)

        def mmg(groups):
            def fn(e):
                ins = None
                for out, pairs in groups:
                    n = len(pairs)
                    for i, (l, r) in enumerate(pairs):
                        ins = e.matmul(out, l, r, start=(i == 0), stop=(i == n - 1))
                return ins
            return fn

        def pe_mm(groups, reads, writes):
            sch.op(PE, mmg(groups), reads, writes)

        def pe_tr(pairs, reads, writes):
            def fn(e):
                ins = None
                for o, i in pairs:
                    ins = e.transpose(o, i, idb.ap)
                return ins
            sch.op(PE, fn, list(reads) + [idb], writes)

        def act(out, in_, func, reads, writes, **kw):
            sch.op(ACT, lambda e: e.activation(out, in_, func, **kw), reads, writes)

        def tt(out, a, b, op, reads, writes, eng=None):
            sch.op(eng or DVE, lambda e: e.tensor_tensor(out, a, b, op), reads, writes)

        def stt(out, a, s, b, op0, op1, reads, writes, eng=None):
            sch.op(eng or DVE, lambda e: e.scalar_tensor_tensor(out, a, s, b, op0, op1), reads, writes)

        def ts(out, a, s1, s2, op0, op1, reads, writes, eng=None):
            if s2 is None:
                sch.op(eng or DVE, lambda e: e.tensor_scalar(out, a, s1, None, op0), reads, writes)
            else:
                sch.op(eng or DVE, lambda e: e.tensor_scalar(out, a, s1, s2, op0, op1), reads, writes)

        def cp(out, a, reads, writes, eng=None):
            if eng is ACT:
                sch.op(ACT, lambda e: e.activation(out, a, AF.Copy), reads, writes)
            else:
                sch.op(eng or DVE, lambda e: e.tensor_copy(out, a), reads, writes)

        def ms(out, val, writes, eng=None):
            sch.op(eng or DVE, lambda e: e.memset(out, val), (), writes)

        def wload(dst_ap, src_ap):
            if not ring_free:
                raise RuntimeError("weight ring exhausted at emission")
            u = ring_free.popleft()
            rb = ringb[u]
            sch.dma(POOL, ring_slot[u], lambda e: e.dma_start(out=dst_ap(ring[:, u, :]), in_=src_ap), (), [rb])
            return rb

        ring_free = deque(range(6))

        def relw(*rbs):
            for rb in rbs:
                ring_free.append(ringb.index(rb))

        scrB = [Bf(wscr_d[u]) for u in range(26)]
        scr_slot = [sch.slot("scr%d" % u) for u in range(26)]
        SCR = {"w_in": 0, "w_out": 14, "w_q": 18, "w_o": 22}
        scr_src = [(w_in, i * 256) for i in range(14)] + [(w_out, q * 256) for q in range(4)] \
            + [(w_q, q * 256) for q in range(4)] + [(w_o, q * 256) for q in range(4)]
        scr_pending = [2, 3, 4, 5] + [u for u in range(26) if u not in (2, 3, 4, 5)]

        def precast(n):
            for _ in range(n):
                if not scr_pending:
                    return
                u = scr_pending.pop(0)
                wd, c0 = scr_src[u]
                src = wd[:, c0:c0 + 256].rearrange("(k p) n -> p k n", p=128)
                dst = wscr_d[u].rearrange("p (k n) -> p k n", k=8)
                sch.dma(POOL, scr_slot[u], (lambda dst, src: lambda e: e.dma_start(out=dst, in_=src))(dst, src), (), [scrB[u]])

        def w_scr(name, q):
            u = SCR[name] + q
            assert u not in scr_pending
            if not ring_free:
                raise RuntimeError("weight ring exhausted at emission")
            r = ring_free.popleft()
            rb = ringb[r]
            sch.dma(SP, ring_slot[r], (lambda r, u: lambda e: e.dma_start(out=ring[:, r, :], in_=wscr_d[u]))(r, u), [scrB[u]], [rb])
            return rb, rb.ap.rearrange("p (k n) -> p k n", k=8)

        def w_cols(wd, c0, n=256):
            src = wd[:, c0:c0 + n].rearrange("(k p) n -> p k n", p=128)
            rb = wload(lambda r: r[:, 0:8 * n].rearrange("p (k n) -> p k n", k=8), src)
            return rb, rb.ap[:, 0:8 * n].rearrange("p (k n) -> p k n", k=8)

        def w_rows(wd, r0):
            src = wd[r0:r0 + 256, :].rearrange("(j p) n -> p j n", p=128)
            rb = wload(lambda r: r.rearrange("p (j n) -> p j n", j=2), src)
            return rb, rb.ap.rearrange("p (j n) -> p j n", j=2)

        def cdma(dst, src, b):
            sch.dma(SP, c_slot, lambda e: e.dma_start(out=dst, in_=src), (), [b])
        cdma(tri.ap, tri_d, tri); cdma(trr.ap, trr_d, trr); cdma(cind.ap, cind_d, cind); cdma(idf.ap, idn_d, idf)
        for h in range(4):
            cdma(mask_sb[:, h, :], tri_d, maskb)
        cdma(hg.ap, hg_d, hg)
        cdma(cw.ap, cw_d.rearrange("(c p) j -> p c j", p=128), cw)
        cdma(lbp_sb[:, 0, :], lbp_d[0:1, :].partition_broadcast(128), lbp)
        cdma(lbp_sb[:, 1, :], lbp_d[1:2, :].partition_broadcast(128), lbp)
        for b_ in (tri, trr, cind, idf, maskb, hg, cw, lbp):
            b_.lw = (c_slot, c_slot.count)
        cp(idb.ap, idf.ap, [idf], [idb])
        ms(ones.ap, 1.0, [ones])
        tt(oml.ap, lbp_sb[:, 1, :], lbp_sb[:, 0, :], ALU.subtract, [lbp], [oml])
        act(oml.ap, oml.ap, AF.Sigmoid, [oml], [oml])
        sch.barrier()

        def load_gain(gd):
            sch.dma(SP, g_slot, lambda e: e.dma_start(out=gb.ap, in_=gd.partition_broadcast(128)), (), [gb])

        def load_x(src, nblk):
            for b in range(nblk):
                sch.dma(SP, xl_slot[b],
                        (lambda b: lambda e: e.dma_start(out=xb[b].ap, in_=src[b * 128:(b + 1) * 128, :]))(b),
                        (), [xb[b]])

        def c16(off, shape):
            n = int(np.prod(shape[1:]))
            ap = a16[:, off:off + n]
            if len(shape) == 3:
                ap = ap.rearrange("p (a b) -> p a b", a=shape[1])
            return ap

        def c32(off, shape):
            n = int(np.prod(shape[1:]))
            ap = a32[:, off:off + n]
            if len(shape) == 3:
                ap = ap.rearrange("p (a b) -> p a b", a=shape[1])
            return ap

        def norm_T(srcs, hT_ap, hT_bufs, hn_bufs, junk, col0=0):
            n = len(srcs)
            ms(ss_sb[:, 0:2 * n], 0.0, [ss])
            for j, s in enumerate(srcs):
                act(junk.ap, s.ap, AF.Square, [s], [junk, ss], accum_out=ss_sb[:, j:j + 1])
            act(ss_sb[:, n:2 * n], ss_sb[:, 0:n], AF.Sqrt, [ss], [ss], scale=1.0 / D, bias=EPS)
            sch.op(DVE, lambda e: e.reciprocal(ss_sb[:, 0:n], ss_sb[:, n:2 * n]), [ss], [ss])
            for j, s in enumerate(srcs):
                hn = hn_bufs[j % len(hn_bufs)]
                stt(hn.ap, s.ap, ss_sb[:, j:j + 1], gb.ap, ALU.mult, ALU.mult, [s, ss, gb], [hn])
                pt = npt()
                pe_tr([(pt.ap[:, k * 128:(k + 1) * 128], hn.ap[:, k * 128:(k + 1) * 128]) for k in range(8)], [hn], [pt])
                cp(hT_ap[:, :, col0 + j * 128:col0 + (j + 1) * 128], pt.ap.rearrange("p (k n) -> p k n", k=8),
                   [pt], [hT_bufs[j]], eng=ACT)
                rel(pt)

        def ffn(gd, wg, wu, wd, nblk):
            hT_ap = c16(0, [128, 8, 2048])
            hTb = [Bf(hT_ap[:, :, b * 128:(b + 1) * 128]) for b in range(nblk)]
            hid_ap = [c16(16384 + i * 4096, [128, 2, 2048]) for i in range(2)]
            hidb = [[[Bf(hid_ap[i][:, j, t * 512:(t + 1) * 512]) for t in range(4)] for j in range(2)] for i in range(2)]
            hn = [Bf(c32(i * 1024, [128, 1024]).bitcast(BF16)[:, 0:1024]) for i in range(2)]
            junk = Bf(c32(2048, [128, 1024]).bitcast(BF16)[:, 0:1024])
            sg = [Bf(c32(3072 + i * 512, [128, 512])) for i in range(3)]
            load_gain(gd)
            norm_T(xb[:nblk], hT_ap, hTb, hn, junk)
            ntt = nblk // 4
            NG = DFF // 256
            units = {}

            def loadg(g):
                units[g] = (w_cols(wg, g * 256), w_cols(wu, g * 256), w_rows(wd, g * 256))

            def gu(g):
                (gb_, gap), (ub_, uap), _ = units[g]
                i = g % 2
                for j in range(2):
                    for t in range(ntt):
                        pg = nps(); pu = nps()
                        rd = [gb_, ub_] + hTb[t * 4:t * 4 + 4]
                        pe_mm([(pg.ap, [(gap[:, k, j * 128:(j + 1) * 128], hT_ap[:, k, t * 512:(t + 1) * 512]) for k in range(8)])], rd, [pg])
                        pe_mm([(pu.ap, [(uap[:, k, j * 128:(j + 1) * 128], hT_ap[:, k, t * 512:(t + 1) * 512]) for k in range(8)])], rd, [pu])
                        s = sg[(j * ntt + t) % 3]
                        act(s.ap, pg.ap, AF.Silu, [pg], [s])
                        tt(hidb[i][j][t].ap, s.ap, pu.ap, ALU.mult, [s, pu], [hidb[i][j][t]])
                        rel(pg, pu)

            def down(g):
                _, _, (db_, dap) = units[g]
                i = g % 2
                for b in range(nblk):
                    for ch in range(2):
                        pd = nps()
                        pe_mm([(pd.ap, [(hid_ap[i][:, j, b * 128:(b + 1) * 128], dap[:, j, ch * 512:(ch + 1) * 512]) for j in range(2)])],
                              [db_, hidb[i][0][b // 4], hidb[i][1][b // 4]], [pd])
                        xs = xb[b].ap[:, ch * 512:(ch + 1) * 512]
                        stt(xs, pd.ap, 0.5, xs, ALU.mult, ALU.add, [pd, xb[b]], [xb[b]])
                        rel(pd)
                relw(units[g][0][0], units[g][1][0], units[g][2][0])
                del units[g]

            loadg(0)
            loadg(1)
            precast(4)
            gu(0)
            for g in range(NG):
                if g + 1 < NG:
                    gu(g + 1)
                down(g)
                if g + 2 < NG:
                    loadg(g + 2)
                precast(3)

        def interleave(lists, offsets):
            T = max(o + len(l) for l, o in zip(lists, offsets))
            for t in range(T):
                for l, o in zip(lists, offsets):
                    k = t - o
                    if 0 <= k < len(l):
                        l[k]()

        st_ = {"gc": 0}

        def mix_setup():
            m = {}
            m["hT"] = c16(0, [128, 8, 512]); m["hTb"] = [Bf(m["hT"][:, :, b * 128:(b + 1) * 128]) for b in range(4)]
            m["t16"] = [Bf(c16(4096 + i * 512, [128, 512])) for i in range(4)]
            m["v"] = [Bf(c16(6144 + i * 512, [128, 512])) for i in range(4)]
            m["kh"] = [Bf(c16(8192 + i * 512, [128, 512])) for i in range(4)]
            m["ktT"] = c16(10240, [128, 4, 512]); m["ktTb"] = [Bf(m["ktT"][:, :, b * 128:(b + 1) * 128]) for b in range(4)]
            m["qT"] = c16(12288, [128, 4, 512]); m["qTb"] = [Bf(m["qT"][:, h, :]) for h in range(4)]
            m["PT"] = [Bf(c16(14336 + i * 512, [128, 4, 128])) for i in range(2)]
            m["Sbf"] = [Bf(c16(15360 + i * 512, [128, 512])) for i in range(9)]
            m["GT"] = c16(19968, [128, 4, 512]); m["GTb"] = [Bf(m["GT"][:, h, :]) for h in range(4)]
            m["yT"] = c16(22016, [128, 8, 512]); m["yTb"] = [Bf(m["yT"][:, c, :]) for c in range(8)]
            m["A"] = [Bf(c32(i * 512, [128, 512])) for i in range(4)]
            m["Bq"] = [Bf(c32(2048 + i * 512, [128, 512])) for i in range(2)]
            m["C"] = [Bf(c32(3072 + i * 512, [128, 512])) for i in range(4)]
            m["t32"] = [Bf(c32(5120 + i * 512, [128, 512])) for i in range(3)]
            m["ET"] = c32(6656, [128, 4, 512]); m["ETb"] = [Bf(m["ET"][:, :, b * 128:(b + 1) * 128]) for b in range(4)]
            m["S"] = [Bf(c32(8704 + i * 512, [128, 512])) for i in range(2)]
            m["u"] = c32(9728, [128, 4, 514]); m["ub"] = [Bf(m["u"][:, c, :]) for c in range(4)]
            m["hn"] = [Bf(c32(11784 + i * 512, [128, 512]).bitcast(BF16)) for i in range(2)]
            m["junk"] = Bf(c32(12808, [128, 512]).bitcast(BF16))
            m["i16"] = 0; m["i32"] = 0
            return m

        def t16(m):
            m["i16"] += 1
            return m["t16"][m["i16"] % 4]

        def t32(m):
            m["i32"] += 1
            return m["t32"][m["i32"] % 3]

        def mix_init_state(m):
            ms(m["S"][0].ap, 0.0, [m["S"][0]])
            ms(m["Sbf"][0].ap, 0.0, [m["Sbf"][0]])
            ms(m["u"], 0.0, m["ub"])
            st_["gc"] = 0
            st_["si"] = 0

        def mix_restore_state(m):
            sb_c = m["Sbf"][st_["gc"] % 9]
            cp(sb_c.ap, m["S"][st_["si"] % 2].ap, [m["S"][st_["si"] % 2]], [sb_c], eng=ACT)

        def fm_units(m, c0, nchunks, consume):
            units = []
            hold = {}
            for q in range(0, nchunks, 2):
                for j in range(2):
                    def u(q=q, j=j):
                        if j == 0:
                            hold[q] = w_scr("w_in", (c0 + q * 128) // 256)
                        rb, wap = hold[q]
                        p = nps()
                        pe_mm([(p.ap, [(wap[:, k, j * 128:(j + 1) * 128], m["hT"][:, k, :]) for k in range(8)])],
                              [rb] + m["hTb"], [p])
                        consume(q + j, p)
                        rel(p)
                        if j == 1:
                            relw(rb)
                    units.append(u)
            return units

        def fm_proj(m, c0, nchunks, consume):
            for u in fm_units(m, c0, nchunks, consume):
                u()

        def mix_tile(m, blks, state_only, want_u):
            load_gain(g_mix)
            norm_T(blks, m["hT"], m["hTb"], m["hn"], m["junk"])
            hT = m["hT"]
            wf = [w_scr("w_in", 2), w_scr("w_in", 3)]
            wi = [w_scr("w_in", 4), w_scr("w_in", 5)]
            vbs = m["v"]; Sidx = [None] * 8
            TB = [slice(b * 128, (b + 1) * 128) for b in range(4)]
            A = m["A"]; C = m["C"]; Bq = [m["Bq"][b % 2] for b in range(4)]; khb = m["kh"]

            def tm_stages(grp):
                pzf = {}; pzi = {}; pBr = {}; pDc = {}; pB = {}; pBT = {}; kt = {}; pS = {}
                L = []

                def s_zf():
                    for b in grp:
                        pzf[b] = nps()
                        pe_mm([(pzf[b].ap[:, q * 256:(q + 1) * 256], [(hT[:, k, TB[b]], wf[q][1][:, k, :]) for k in range(8)]) for q in range(2)],
                              [wf[0][0], wf[1][0], m["hTb"][b]], [pzf[b]])
                L.append(s_zf)

                def s_zi():
                    for b in grp:
                        pzi[b] = nps()
                        pe_mm([(pzi[b].ap[:, q * 256:(q + 1) * 256], [(hT[:, k, TB[b]], wi[q][1][:, k, :]) for k in range(8)]) for q in range(2)],
                              [wi[0][0], wi[1][0], m["hTb"][b]], [pzi[b]])
                L.append(s_zi)

                def s_sig():
                    for b in grp:
                        act(A[b].ap, pzf[b].ap, AF.Sigmoid, [pzf[b]], [A[b]], scale=-1.0)
                        rel(pzf[b])
                L.append(s_sig)

                def s_v():
                    for b in grp:
                        act(vbs[b].ap, pzi[b].ap, AF.Copy, [pzi[b]], [vbs[b]])
                        rel(pzi[b])
                L.append(s_v)

                def s_k():
                    for b in grp:
                        tt(A[b].ap, A[b].ap, oml.ap, ALU.mult, [A[b], oml], [A[b]])
                L.append(s_k)

                def s_ln():
                    for b in grp:
                        act(Bq[b].ap, A[b].ap, AF.Ln, [A[b]], [Bq[b]], scale=-1.0, bias=1.0)
                L.append(s_ln)

                def s_cum():
                    for b in grp:
                        pBr[b] = nps()
                        pe_mm([(pBr[b].ap, [(trr.ap, Bq[b].ap)])], [trr, Bq[b]], [pBr[b]])
                        if state_only:
                            pDc[b] = nps()
                            pe_mm([(pDc[b].ap[:, h * 2:(h + 1) * 2], [(Bq[b].ap[:, h * 128:(h + 1) * 128], cind.ap)]) for h in range(4)],
                                  [cind, Bq[b]], [pDc[b]])
                        else:
                            pB[b] = nps()
                            pe_mm([(pB[b].ap, [(tri.ap, Bq[b].ap)])], [tri, Bq[b]], [pB[b]])
                            pBT[b] = nps()
                            pe_mm([(pBT[b].ap[:, h * 128:(h + 1) * 128], [(Bq[b].ap[:, h * 128:(h + 1) * 128], tri.ap)]) for h in range(4)],
                                  [tri, Bq[b]], [pBT[b]])
                L.append(s_cum)

                def s_er():
                    for b in grp:
                        act(C[b].ap, pBr[b].ap, AF.Exp, [pBr[b]], [C[b]])
                        rel(pBr[b])
                L.append(s_er)

                def s_kh():
                    for b in grp:
                        tt(khb[b].ap, A[b].ap, C[b].ap, ALU.mult, [A[b], C[b]], [khb[b]])
                L.append(s_kh)

                if state_only:
                    def s_dec():
                        for b in grp:
                            act(dec_sb[:, b, :], pDc[b].ap[:, 0:8], AF.Exp, [pDc[b]], [dec])
                            rel(pDc[b])
                    L.append(s_dec)
                else:
                    def s_et():
                        for b in grp:
                            act(m["ET"][:, :, TB[b]], pBT[b].ap.rearrange("p (h n) -> p h n", h=4), AF.Exp, [pBT[b]], [m["ETb"][b]])
                            rel(pBT[b])
                    L.append(s_et)

                    def s_eb():
                        for b in grp:
                            act(C[b].ap, pB[b].ap, AF.Exp, [pB[b]], [C[b]], scale=-1.0)
                            rel(pB[b])
                    L.append(s_eb)

                    def s_kt():
                        for b in grp:
                            kt[b] = t16(m)
                            tt(kt[b].ap, A[b].ap, C[b].ap, ALU.mult, [A[b], C[b]], [kt[b]])
                    L.append(s_kt)

                    def s_tr():
                        for b in grp:
                            pt = npt()
                            pe_tr([(pt.ap[:, h * 128:(h + 1) * 128], kt[b].ap[:, h * 128:(h + 1) * 128]) for h in range(4)], [kt[b]], [pt])
                            cp(m["ktT"][:, :, TB[b]], pt.ap[:, 0:512].rearrange("p (h n) -> p h n", h=4), [pt], [m["ktTb"][b]])
                            rel(pt)
                    L.append(s_tr)

                def s_ps():
                    for b in grp:
                        for c in range(2):
                            cs = slice(c * 64, (c + 1) * 64)
                            pS[b, c] = nps()
                            pe_mm([(pS[b, c].ap[:, h * 128:(h + 1) * 128], [(khb[b].ap[cs, h * 128:(h + 1) * 128], vbs[b].ap[cs, h * 128:(h + 1) * 128])]) for h in range(4)],
                                  [khb[b], vbs[b]], [pS[b, c]])
                L.append(s_ps)

                def s_chain():
                    for b in grp:
                        for c in range(2):
                            So = m["S"][st_["si"] % 2]; Sn = m["S"][(st_["si"] + 1) % 2]
                            st_["si"] += 1
                            for h in range(4):
                                hs = slice(h * 128, (h + 1) * 128)
                                if state_only:
                                    dap = dec_sb[:, b, h * 2 + c:h * 2 + c + 1]; dbf = dec
                                else:
                                    col = b * 128 + c * 64 + 63
                                    dap = m["ET"][:, h, col:col + 1]; dbf = m["ETb"][b]
                                stt(Sn.ap[:, hs], So.ap[:, hs], dap, pS[b, c].ap[:, hs], ALU.mult, ALU.add, [So, dbf, pS[b, c]], [Sn])
                            Sidx[b * 2 + c] = st_["gc"] % 9
                            st_["gc"] += 1
                            sb_n = m["Sbf"][st_["gc"] % 9]
                            cp(sb_n.ap, Sn.ap, [Sn], [sb_n])
                            rel(pS[b, c])
                L.append(s_chain)
                return L

            pzc = {}

            def take_c(c, p):
                zcs = t32(m)
                act(zcs.ap, p.ap, AF.Copy, [p], [zcs])
                pzc[c] = zcs

            def take_u(c, p):
                tt(m["u"][:, c, 2:514], pzc[c].ap, p.ap, ALU.mult, [pzc[c], p], [m["ub"][c]])

            def halo(c):
                cp(m["u"][:, c, 0:2], m["u"][:, c, 512:514], [m["ub"][c]], [m["ub"][c]])

            def conv_u_units():
                us = []
                for q in (0, 2):
                    us += fm_units(m, 2560 + q * 128, 2, (lambda q: lambda c, p: take_c(q + c, p))(q))
                    us += fm_units(m, 3072 + q * 128, 2, (lambda q: lambda c, p: take_u(q + c, p))(q))
                return us

            def take_g(h, p):
                act(m["GT"][:, h, :], p.ap, AF.Silu, [p], [m["GTb"][h]])

            fill = []
            if not state_only:
                fill = fm_units(m, 1536, 4, take_g) + conv_u_units()
            elif want_u:
                fill = conv_u_units()
            interleave([tm_stages((b,)) for b in range(4)] + [fill], TM_OFF)
            relw(wf[0][0], wf[1][0], wi[0][0], wi[1][0])
            if state_only:
                if want_u:
                    for c in range(4):
                        halo(c)
                return
            def take_q(h, p):
                sq = t32(m)
                act(sq.ap, p.ap, AF.Silu, [p], [sq])
                stt(m["qT"][:, h, :], sq.ap, 128.0 ** -0.5, m["ET"][:, h, :], ALU.mult, ALU.mult, [sq] + m["ETb"], [m["qTb"][h]])
            fm_proj(m, 0, 4, take_q)


            def o_stages(grp):
                pSc = {}; pO = {}; pN = {}; sqo = {}
                L = []

                def s0():
                    for b in grp:
                        pSc[b] = nps()
                        pe_mm([(pSc[b].ap[:, h * 128:(h + 1) * 128], [(m["ktT"][:, h, TB[b]], m["qT"][:, h, TB[b]])]) for h in range(4)],
                              [m["ktTb"][b]] + m["qTb"], [pSc[b]])
                L.append(s0)

                def s1():
                    for b in grp:
                        PT = m["PT"][b % 2]
                        tt(PT.ap, pSc[b].ap.rearrange("p (h n) -> p h n", h=4), maskb.ap, ALU.mult, [pSc[b], maskb], [PT])
                        rel(pSc[b])
                L.append(s1)

                def s2():
                    for b in grp:
                        PT = m["PT"][b % 2]; v = vbs[b]
                        pO[b] = nps()
                        groups = []
                        rds = [v, PT] + m["qTb"]
                        for h in range(4):
                            for c in range(2):
                                cs = slice(c * 64, (c + 1) * 64)
                                sbf = m["Sbf"][Sidx[b * 2 + c]]
                                rds.append(sbf)
                                groups.append((pO[b].ap[:, h * 128 + c * 64:h * 128 + (c + 1) * 64],
                                               [(v.ap[cs, h * 128:(h + 1) * 128], PT.ap[cs, h, c * 64:(c + 1) * 64]),
                                                (sbf.ap[:, h * 128:(h + 1) * 128], m["qT"][:, h, b * 128 + c * 64:b * 128 + (c + 1) * 64])]))
                        pe_mm(groups, rds, [pO[b]])
                L.append(s2)

                def s3():
                    for b in grp:
                        sqo[b] = t16(m)
                        act(sqo[b].ap, pO[b].ap, AF.Square, [pO[b]], [sqo[b]])
                L.append(s3)

                def s4():
                    for b in grp:
                        pN[b] = nps()
                        pe_mm([(pN[b].ap[:, h * 128:(h + 1) * 128], [(ones.ap, sqo[b].ap[:, h * 128:(h + 1) * 128])]) for h in range(4)],
                              [ones, sqo[b]], [pN[b]])
                L.append(s4)

                def s5():
                    for b in grp:
                        act(A[b].ap, pN[b].ap, AF.Ln, [pN[b]], [A[b]], scale=1.0 / 128, bias=EPS)
                        rel(pN[b])
                L.append(s5)

                def s6():
                    for b in grp:
                        act(C[b].ap, A[b].ap, AF.Exp, [A[b]], [C[b]], scale=-0.5)
                L.append(s6)

                def s7():
                    for b in grp:
                        tt(A[b].ap, pO[b].ap, C[b].ap, ALU.mult, [pO[b], C[b]], [A[b]])
                        rel(pO[b])
                L.append(s7)

                def s8():
                    for b in grp:
                        for h in range(4):
                            stt(m["yT"][:, h, TB[b]], A[b].ap[:, h * 128:(h + 1) * 128], hg_sb[:, h:h + 1], m["GT"][:, h, TB[b]], ALU.mult, ALU.mult,
                                [A[b], hg, m["GTb"][h]], [m["yTb"][h]])
                L.append(s8)
                return L

            def take_b(c, p):
                a1 = t32(m)
                ts(a1.ap, m["u"][:, c, 2:514], cw_sb[:, c, 2:3], None, ALU.mult, None, [m["ub"][c], cw], [a1])
                a2 = t32(m)
                stt(a2.ap, m["u"][:, c, 1:513], cw_sb[:, c, 1:2], a1.ap, ALU.mult, ALU.add, [m["ub"][c], cw, a1], [a2])
                a3 = t32(m)
                stt(a3.ap, m["u"][:, c, 0:512], cw_sb[:, c, 0:1], a2.ap, ALU.mult, ALU.add, [m["ub"][c], cw, a2], [a3])
                tt(m["yT"][:, 4 + c, :], a3.ap, p.ap, ALU.mult, [a3, p], [m["yTb"][4 + c]])
                halo(c)

            interleave([o_stages((b,)) for b in range(4)] + [fm_units(m, 2048, 4, take_b)], O_OFF)

            for q in range(4):
                rb, wap = w_scr("w_out", q)
                for b in range(4):
                    tb = slice(b * 128, (b + 1) * 128)
                    p = nps()
                    pe_mm([(p.ap[:, 0:256], [(m["yT"][:, fc, tb], wap[:, fc, :]) for fc in range(8)])], [rb] + m["yTb"], [p])
                    xs = blks[b].ap[:, q * 256:(q + 1) * 256]
                    tt(xs, p.ap[:, 0:256], xs, ALU.add, [p, blks[b]], [blks[b]])
                    rel(p)
                relw(rb)

        def xattn():
            hT = c16(0, [128, 8, 512]); hTb = [Bf(hT[:, :, b * 128:(b + 1) * 128]) for b in range(4)]
            qmT = c16(4096, [128, 8, 512]); qmTb = [Bf(qmT[:, c, :]) for c in range(8)]
            kmT = c16(8192, [128, 8, 256]); kmTb = Bf(kmT)
            vm = c16(10240, [128, 2, 1024]); vmb = Bf(vm)
            mnT = c16(12288, [128, 8, 256]); mnTb = [Bf(mnT[:, :, b * 128:(b + 1) * 128]) for b in range(2)]
            pbuf = [Bf(c16(12288 + i * 1024, [128, 4, 256])) for i in range(4)]
            pTb = [Bf(c16(16384 + i * 1024, [128, 8, 128])) for i in range(4)]
            attT = c16(20480, [128, 8, 512]); attTb = [Bf(attT[:, :, b * 128:(b + 1) * 128]) for b in range(4)]
            hn = [Bf(c32(i * 512, [128, 512]).bitcast(BF16)) for i in range(2)]
            junk = Bf(c32(1024, [128, 512]).bitcast(BF16))
            memx = [Bf(c32(1536 + i * 1024, [128, 1024])) for i in range(2)]
            pe_ = [Bf(c32(3584 + i * 1024, [128, 4, 256])) for i in range(4)]
            scale = 256.0 ** -0.5
            sm4 = c32(7680, [128, 128])
            for i in range(2):
                sch.dma(SP, xl_slot[i], (lambda i: lambda e: e.dma_start(out=memx[i].ap, in_=mem_d[i * 128:(i + 1) * 128, :]))(i), (), [memx[i]])
            load_gain(g_mem)
            norm_T(memx, mnT, mnTb, hn, junk)
            for q in range(4):
                rb, wap = w_cols(w_kv, q * 256)
                for j in range(2):
                    p = nps()
                    pe_mm([(p.ap[:, 0:256], [(wap[:, k, j * 128:(j + 1) * 128], mnT[:, k, :]) for k in range(8)])], [rb] + mnTb, [p])
                    cp(kmT[:, q * 2 + j, :], p.ap[:, 0:256], [p], [kmTb], eng=ACT)
                    rel(p)
                relw(rb)
            for q in range(4):
                rb, wap = w_cols(w_kv, 1024 + q * 256)
                for mb in range(2):
                    p = nps()
                    pe_mm([(p.ap[:, 0:256], [(mnT[:, k, mb * 128:(mb + 1) * 128], wap[:, k, :]) for k in range(8)])], [rb] + mnTb, [p])
                    cp(vm[:, mb, q * 256:(q + 1) * 256], p.ap[:, 0:256], [p], [vmb], eng=ACT)
                    rel(p)
                relw(rb)
            sch.barrier()
            for ti in range(4):
                blks = xb[ti * 4:ti * 4 + 4]
                load_gain(g_xat)
                norm_T(blks, hT, hTb, hn, junk)
                for q in range(4):
                    rb, wap = w_scr("w_q", q)
                    for j in range(2):
                        p = nps()
                        pe_mm([(p.ap, [(wap[:, k, j * 128:(j + 1) * 128], hT[:, k, :]) for k in range(8)])], [rb] + hTb, [p])
                        cp(qmT[:, q * 2 + j, :], p.ap, [p], [qmTb[q * 2 + j]], eng=ACT if j else DVE)
                        rel(p)
                    relw(rb)
                TB = [slice(b * 128, (b + 1) * 128) for b in range(4)]
                def a_stages(grp):
                    so = {b: b * 32 for b in grp}
                    smb = {b: Bf(sm4[:, b * 32:(b + 1) * 32]) for b in grp}
                    pscs = {}; pts = {}; pAs = {}
                    L = []

                    def s0():
                        for b in grp:
                            ms(sm4[:, so[b]:so[b] + 4], 0.0, [smb[b]])
                        for b in grp:
                            for hp in range(2):
                                p = nps()
                                pe_mm([(p.ap[:, hh * 256:(hh + 1) * 256],
                                        [(qmT[:, (hp * 2 + hh) * 2 + dc, TB[b]], kmT[:, (hp * 2 + hh) * 2 + dc, :]) for dc in range(2)]) for hh in range(2)],
                                      qmTb + [kmTb], [p])
                                pscs[b, hp] = p
                    L.append(s0)

                    def s1():
                        for b in grp:
                            for hp in range(2):
                                p = pscs[b, hp]
                                sch.op(DVE, (lambda o, i: lambda e: e.tensor_reduce(o, i, AX.X, ALU.max))(
                                    sm4[:, so[b] + 8 + hp * 2:so[b] + 8 + hp * 2 + 2], p.ap.rearrange("p (h n) -> p h n", h=2)), [p], [smb[b]])
                            ts(sm4[:, so[b] + 16:so[b] + 20], sm4[:, so[b] + 8:so[b] + 12], -scale, None, ALU.mult, None, [smb[b]], [smb[b]])
                    L.append(s1)

                    def s2():
                        for b in grp:
                            for h in range(4):
                                p = pscs[b, h // 2]
                                act(pe_[b].ap[:, h, :], p.ap[:, (h % 2) * 256:(h % 2 + 1) * 256], AF.Exp, [p, smb[b]], [pe_[b], smb[b]],
                                    scale=scale, bias=sm4[:, so[b] + 16 + h:so[b] + 17 + h], accum_out=sm4[:, so[b] + h:so[b] + h + 1])
                            rel(pscs[b, 0], pscs[b, 1])
                    L.append(s2)

                    def s3():
                        for b in grp:
                            sch.op(DVE, (lambda o: lambda e: e.reciprocal(sm4[:, o + 24:o + 28], sm4[:, o:o + 4]))(so[b]), [smb[b]], [smb[b]])
                            for h in range(4):
                                ts(pbuf[b].ap[:, h, :], pe_[b].ap[:, h, :], sm4[:, so[b] + 24 + h:so[b] + 25 + h], None, ALU.mult, None,
                                   [pe_[b], smb[b]], [pbuf[b]])
                    L.append(s3)

                    def s4():
                        for b in grp:
                            pts[b] = npt()
                            pe_tr([(pts[b].ap[:, (h * 2 + mc) * 128:(h * 2 + mc + 1) * 128], pbuf[b].ap[:, h, mc * 128:(mc + 1) * 128])
                                   for h in range(4) for mc in range(2)], [pbuf[b]], [pts[b]])
                    L.append(s4)

                    def s5():
                        for b in grp:
                            cp(pTb[b].ap, pts[b].ap.rearrange("p (k n) -> p k n", k=8), [pts[b]], [pTb[b]], eng=ACT)
                            rel(pts[b])
                    L.append(s5)

                    def s6():
                        for b in grp:
                            pT = pTb[b]
                            for half in range(2):
                                pA = nps()
                                pe_mm([(pA.ap[:, (ci - half * 4) * 128:(ci - half * 4 + 1) * 128],
                                        [(vm[:, mc, ci * 128:(ci + 1) * 128], pT.ap[:, (ci // 2) * 2 + mc, :]) for mc in range(2)])
                                       for ci in range(half * 4, half * 4 + 4)], [vmb, pT], [pA])
                                pAs[b, half] = pA
                    L.append(s6)

                    def s7():
                        for b in grp:
                            cp(attT[:, 0:4, TB[b]], pAs[b, 0].ap.rearrange("p (k n) -> p k n", k=4), [pAs[b, 0]], [attTb[b]], eng=ACT)
                            cp(attT[:, 4:8, TB[b]], pAs[b, 1].ap.rearrange("p (k n) -> p k n", k=4), [pAs[b, 1]], [attTb[b]])
                            rel(pAs[b, 0], pAs[b, 1])
                    L.append(s7)
                    return L

                interleave([a_stages((b,)) for b in range(4)], A_OFF)
                for q in range(4):
                    rb, wap = w_scr("w_o", q)
                    for b in range(4):
                        tb = slice(b * 128, (b + 1) * 128)
                        p = nps()
                        pe_mm([(p.ap[:, 0:256], [(attT[:, fc, tb], wap[:, fc, :]) for fc in range(8)])], [rb] + attTb, [p])
                        xs = blks[b].ap[:, q * 256:(q + 1) * 256]
                        tt(xs, p.ap[:, 0:256], xs, ALU.add, [p, blks[b]], [blks[b]])
                        rel(p)
                    relw(rb)

        def final(do_norm=True):
            ob = [Bf(c32(i * 1024, [128, 1024])) for i in range(2)]
            junk = Bf(c32(2048, [128, 512]).bitcast(BF16))
            if do_norm:
                load_gain(g_fin)
                ms(ss_sb[:, 0:32], 0.0, [ss])
                for b in range(NB):
                    act(junk.ap, xb[b].ap, AF.Square, [xb[b]], [junk, ss], accum_out=ss_sb[:, b:b + 1])
                act(ss_sb[:, 16:32], ss_sb[:, 0:16], AF.Sqrt, [ss], [ss], scale=1.0 / D, bias=EPS)
                sch.op(DVE, lambda e: e.reciprocal(ss_sb[:, 0:16], ss_sb[:, 16:32]), [ss], [ss])
            for b in range(NB):
                o = ob[b % 2]
                if do_norm:
                    stt(o.ap, xb[b].ap, ss_sb[:, b:b + 1], gb.ap, ALU.mult, ALU.mult, [xb[b], ss, gb], [o])
                else:
                    cp(o.ap, xb[b].ap, [xb[b]], [o])
                sch.dma(SP, st_slot[b], (lambda b, o: lambda e: e.dma_start(out=out_d[b * 128:(b + 1) * 128, :], in_=o.ap))(b, o), [o], ())

        m = None
        if pre_tok > 0 and (stop_after is None or stop_after >= 2):
            load_x(xp_d, npb)
            ffn(g_ffn1, w1g, w1u, w1d, npb)
            sch.barrier()
            m = mix_setup()
            mix_init_state(m)
            for ti in range(npb // 4):
                mix_tile(m, xb[ti * 4:ti * 4 + 4], True, ti == npb // 4 - 1)
            sch.barrier()
        load_x(x_d, NB)
        ffn(g_ffn1, w1g, w1u, w1d, NB)
        sch.barrier()
        if stop_after is None or stop_after >= 2:
            if m is None:
                m = mix_setup()
                mix_init_state(m)
            else:
                mix_restore_state(m)
            for ti in range(4):
                mix_tile(m, xb[ti * 4:ti * 4 + 4], False, False)
            sch.barrier()
        if stop_after is None or stop_after >= 3:
            xattn()
            sch.barrier()
        if stop_after is None or stop_after >= 4:
            ffn(g_ffn2, w2g, w2u, w2d, NB)
            sch.barrier()
        final(do_norm=(stop_after is None))
        sch.finish()
        sch.replay(block)
    return nc


def _consts():
    i = np.arange(128)
    same = (i[:, None] // 64) == (i[None, :] // 64)
    tri = (same & (i[:, None] <= i[None, :])).astype(np.float32)
    trirev = (same & (i[:, None] > i[None, :])).astype(np.float32)
    cind = np.stack([(i < 64), (i >= 64)], axis=1).astype(np.float32)
    return tri, trirev, cind, np.eye(128, dtype=np.float32)


def make_in_maps(inputs, pre_tok=PRE_TOK):
    f = lambda k: np.ascontiguousarray(np.asarray(inputs[k], dtype=np.float32))
    x = f("x"); mem = f("mem")
    tri, trirev, cind, ident = _consts()
    shared = {
        "ffn1_norm": f("ffn1_norm").reshape(1, D), "mix_norm": f("mix_norm").reshape(1, D),
        "xattn_norm": f("xattn_norm").reshape(1, D), "mem_norm": f("mem_norm").reshape(1, D),
        "ffn2_norm": f("ffn2_norm").reshape(1, D), "final_norm": f("final_norm").reshape(1, D),
        "ffn1_gate": f("ffn1_gate")[0], "ffn1_up": f("ffn1_up")[0], "ffn1_down": f("ffn1_down")[0],
        "ffn2_gate": f("ffn2_gate")[0], "ffn2_up": f("ffn2_up")[0], "ffn2_down": f("ffn2_down")[0],
        "w_in": f("w_in")[0], "w_out": f("w_out")[0], "w_q_mem": f("w_q_mem")[0],
        "w_kv_mem": f("w_kv_mem")[0], "w_o_mem": f("w_o_mem")[0],
        "lb_param": f("lb_param"), "hg": np.ascontiguousarray(f("hgrn_out_norm").reshape(4, 128).T),
        "conv_w": f("conv_w")[0],
        "c_tri": tri, "c_trirev": trirev, "c_cind": cind, "c_ident": ident,
    }
    maps = []
    npre = max(pre_tok, 128)
    for c in range(8):
        b, half = c // 2, c % 2
        t0 = half * TOK
        if half == 0 or pre_tok == 0:
            xp = np.zeros((npre, D), np.float32)
        else:
            xp = np.ascontiguousarray(x[b, t0 - npre:t0])
        d = dict(shared)
        d["x"] = np.ascontiguousarray(x[b, t0:t0 + TOK])
        d["xp"] = xp
        d["mem"] = mem[b]
        maps.append(d)
    return maps


def kernel(**inputs):
    nc = build_nc()
    maps = make_in_maps(inputs)
    res = run_bass_kernel_spmd(nc, maps, core_ids=list(range(8)))
    out = np.empty((4, 2 * TOK, D), np.float32)
    for c in range(8):
        out[c // 2, (c % 2) * TOK:(c % 2 + 1) * TOK] = res.results[c]["out"]
    return out
```

```python
import numpy as np
from contextlib import ExitStack
import concourse.bass as bass
import concourse.mybir as mybir
from concourse.bass_utils import run_bass_kernel_spmd

F32 = mybir.dt.float32
BF16 = mybir.dt.bfloat16
AF = mybir.ActivationFunctionType
ALU = mybir.AluOpType
AX = mybir.AxisListType

D = 1024
DFF = 2816
TOK = 2048
NB = TOK // 128
EPS = 1e-6
PRE_TOK = 2048
STOP_AFTER = None
TM_OFF = [0, 3, 6, 9, 2]
O_OFF = [0, 2, 4, 6, 1]
A_OFF = [0, 2, 4, 6]


class _Src:
    def __init__(self, name, sem):
        self.name = name
        self.sem = sem
        self.count = 0


class _Eng(_Src):
    def __init__(self, name, sem):
        super().__init__(name, sem)
        self.items = []
        self.waited = {}


class Bf:
    __slots__ = ("ap", "lw", "rd")

    def __init__(self, ap):
        self.ap = ap
        self.lw = None
        self.rd = {}


class Sched:
    def __init__(self, nc, stack):
        self.nc = nc
        self.stack = stack
        self.srcs = []
        self.pe = self._eng("pe")
        self.act = self._eng("act")
        self.dve = self._eng("dve")
        self.pool = self._eng("pool")
        self.sp = self._eng("sp")
        self.engs = [self.pe, self.act, self.dve, self.pool, self.sp]

    def _eng(self, name):
        e = _Eng(name, self.stack.enter_context(self.nc.semaphore("s_" + name)))
        self.srcs.append(e)
        return e

    def slot(self, name):
        s = _Src(name, self.stack.enter_context(self.nc.semaphore("d_" + name)))
        self.srcs.append(s)
        return s

    def _wait(self, eng, src, val):
        if val <= 0 or eng.waited.get(src, 0) >= val:
            return
        eng.waited[src] = val
        eng.items.append(("w", src, val))

    def op(self, eng, fn, reads=(), writes=()):
        for t in reads:
            if t.lw is not None:
                src, val = t.lw
                if src is eng and eng is self.pe:
                    continue
                self._wait(eng, src, val)
        for t in writes:
            if t.lw is not None and t.lw[0] is not eng:
                self._wait(eng, *t.lw)
            for src, val in t.rd.items():
                if src is not eng:
                    self._wait(eng, src, val)
        eng.count += 1
        eng.items.append(("o", fn, eng, eng.count))
        for t in reads:
            t.rd[eng] = eng.count
        for t in writes:
            t.lw = (eng, eng.count)
            t.rd = {}

    def dma(self, q, slot, fn, reads=(), writes=()):
        for t in reads:
            if t.lw is not None:
                self._wait(q, *t.lw)
        for t in writes:
            if t.lw is not None:
                self._wait(q, *t.lw)
            for src, val in t.rd.items():
                self._wait(q, src, val)
        slot.count += 16
        q.items.append(("o", fn, slot, slot.count))
        for t in reads:
            t.rd[slot] = slot.count
        for t in writes:
            t.lw = (slot, slot.count)
            t.rd = {}

    def barrier(self):
        for e in self.engs:
            for s in self.srcs:
                if s is not e:
                    self._wait(e, s, s.count)

    def finish(self):
        for s in self.srcs:
            if s is not self.sp:
                self._wait(self.sp, s, s.count)

    def replay(self, block):
        import bisect
        need = {}
        for e in self.engs:
            for it in e.items:
                if it[0] == "w":
                    need.setdefault(it[1], set()).add(it[2])
        rank = {src: sorted(v) for src, v in need.items()}
        engset = set(self.engs)

        def run(items):
            def body(e):
                for it in items:
                    if it[0] == "w":
                        src, val = it[1], it[2]
                        if src in engset:
                            val = bisect.bisect_left(rank[src], val) + 1
                        e.wait_ge(src.sem, val)
                    else:
                        ins = it[1](e)
                        src, c = it[2], it[3]
                        if src in engset:
                            if c in need.get(src, ()):
                                ins.then_inc(src.sem, 1)
                        else:
                            ins.then_inc(src.sem, 16)
            return body
        block.tensor(run(self.pe.items))
        block.scalar(run(self.act.items))
        block.vector(run(self.dve.items))
        block.gpsimd(run(self.pool.items))
        block.sync(run(self.sp.items))


def build_nc(pre_tok=PRE_TOK, stop_after=STOP_AFTER):
    nc = bass.Bass("TRN2", target_bir_lowering=False)
    npb = pre_tok // 128

    def din(name, shape):
        return nc.dram_tensor(name, list(shape), F32, kind="ExternalInput").ap()

    x_d = din("x", [TOK, D])
    xp_d = din("xp", [max(pre_tok, 128), D])
    mem_d = din("mem", [256, D])
    g_ffn1 = din("ffn1_norm", [1, D]); g_mix = din("mix_norm", [1, D]); g_xat = din("xattn_norm", [1, D])
    g_mem = din("mem_norm", [1, D]); g_ffn2 = din("ffn2_norm", [1, D]); g_fin = din("final_norm", [1, D])
    w1g = din("ffn1_gate", [D, DFF]); w1u = din("ffn1_up", [D, DFF]); w1d = din("ffn1_down", [DFF, D])
    w2g = din("ffn2_gate", [D, DFF]); w2u = din("ffn2_up", [D, DFF]); w2d = din("ffn2_down", [DFF, D])
    w_in = din("w_in", [D, 3584]); w_out = din("w_out", [D, D])
    w_q = din("w_q_mem", [D, D]); w_kv = din("w_kv_mem", [D, 2 * D]); w_o = din("w_o_mem", [D, D])
    lbp_d = din("lb_param", [2, 512]); hg_d = din("hg", [128, 4]); cw_d = din("conv_w", [512, 3])
    tri_d = din("c_tri", [128, 128]); trr_d = din("c_trirev", [128, 128]); cind_d = din("c_cind", [128, 2])
    idn_d = din("c_ident", [128, 128])
    out_d = nc.dram_tensor("out", [TOK, D], F32, kind="ExternalOutput").ap()
    wscr_d = nc.dram_tensor("wscr", [26, 128, 2048], BF16).ap()

    with ExitStack() as st:
        sch = Sched(nc, st)
        sb = lambda name, shape, dt: st.enter_context(nc.sbuf_tensor(name, list(shape), dt))
        x_sb = sb("x_sb", [128, NB, D], F32)
        ring = sb("ring", [128, 6, 2048], BF16)
        a16 = sb("a16", [128, 26112], BF16)
        a32 = sb("a32", [128, 13376], F32)
        gb_sb = sb("gb", [128, D], F32)
        oml_sb = sb("oml", [128, 512], F32)
        mask_sb = sb("maskb", [128, 4, 128], F32)
        tri_sb = sb("tri", [128, 128], F32)
        trr_sb = sb("trr", [128, 128], F32)
        cind_sb = sb("cind", [128, 2], F32)
        idf_sb = sb("idf", [128, 128], F32)
        idb_sb = sb("idb", [128, 128], BF16)
        one_sb = sb("ones", [128, 128], BF16)
        hg_sb = sb("hgs", [128, 4], F32)
        cw_sb = sb("cws", [128, 4, 3], F32)
        ss_sb = sb("ss", [128, 64], F32)
        sm_sb = sb("smx", [128, 64], F32)
        dec_sb = sb("dec", [128, 4, 8], F32)
        psf = [st.enter_context(nc.psum_tensor("psf%d" % i, [128, 512], F32)) for i in range(6)]
        pst = [st.enter_context(nc.psum_tensor("pst%d" % i, [128, 1024], BF16)) for i in range(2)]
        block = st.enter_context(nc.Block())

        PE, ACT, DVE, POOL, SP = sch.pe, sch.act, sch.dve, sch.pool, sch.sp

        xb = [Bf(x_sb[:, b, :]) for b in range(NB)]
        ringb = [Bf(ring[:, u, :]) for u in range(6)]
        ring_slot = [sch.slot("ring%d" % u) for u in range(6)]
        xl_slot = [sch.slot("xl%d" % b) for b in range(NB)]
        st_slot = [sch.slot("st%d" % b) for b in range(NB)]
        c_slot = sch.slot("const")
        g_slot = sch.slot("gain")
        PSF = [Bf(p[:]) for p in psf]
        PST = [Bf(p[:]) for p in pst]
        gb = Bf(gb_sb[:]); oml = Bf(oml_sb[:]); lbp_sb = a32[:, 0:1024].rearrange("p (a b) -> p a b", a=2); lbp = Bf(lbp_sb); maskb = Bf(mask_sb[:])
        tri = Bf(tri_sb[:]); trr = Bf(trr_sb[:]); cind = Bf(cind_sb[:]); idf = Bf(idf_sb[:])
        idb = Bf(idb_sb[:]); ones = Bf(one_sb[:]); hg = Bf(hg_sb[:]); cw = Bf(cw_sb[:])
        ss = Bf(ss_sb[:]); sm = Bf(sm_sb[:]); dec = Bf(dec_sb[:])
        cnt = {"psf": 0, "pst": 0, "ring": 0}

        from collections import deque
        free_f = deque(PSF); free_t = deque(PST)

        def nps():
            if not free_f:
                raise RuntimeError("PSUM fp32 pool exhausted at emission")
            return free_f.popleft()

        def npt():
            if not free_t:
                raise RuntimeError("PSUM bf16 pool exhausted at emission")
            return free_t.popleft()

        def rel(*bs):
            for b in bs:
                (free_t if b in PST else free_f).append(b)

        def mmg(groups):
            def fn(e):
                ins = None
                for out, pairs in groups:
                    n = len(pairs)
                    for i, (l, r) in enumerate(pairs):
                        ins = e.matmul(out, l, r, start=(i == 0), stop=(i == n - 1))
                return ins
            return fn

        def pe_mm(groups, reads, writes):
            sch.op(PE, mmg(groups), reads, writes)

        def pe_tr(pairs, reads, writes):
            def fn(e):
                ins = None
                for o, i in pairs:
                    ins = e.transpose(o, i, idb.ap)
                return ins
            sch.op(PE, fn, list(reads) + [idb], writes)

        def act(out, in_, func, reads, writes, **kw):
            sch.op(ACT, lambda e: e.activation(out, in_, func, **kw), reads, writes)

        def tt(out, a, b, op, reads, writes, eng=None):
            sch.op(eng or DVE, lambda e: e.tensor_tensor(out, a, b, op), reads, writes)

        def stt(out, a, s, b, op0, op1, reads, writes, eng=None):
            sch.op(eng or DVE, lambda e: e.scalar_tensor_tensor(out, a, s, b, op0, op1), reads, writes)

        def ts(out, a, s1, s2, op0, op1, reads, writes, eng=None):
            if s2 is None:
                sch.op(eng or DVE, lambda e: e.tensor_scalar(out, a, s1, None, op0), reads, writes)
            else:
                sch.op(eng or DVE, lambda e: e.tensor_scalar(out, a, s1, s2, op0, op1), reads, writes)

        def cp(out, a, reads, writes, eng=None):
            if eng is ACT:
                sch.op(ACT, lambda e: e.activation(out, a, AF.Copy), reads, writes)
            else:
                sch.op(eng or DVE, lambda e: e.tensor_copy(out, a), reads, writes)

        def ms(out, val, writes, eng=None):
            sch.op(eng or DVE, lambda e: e.memset(out, val), (), writes)

        def wload(dst_ap, src_ap):
            if not ring_free:
                raise RuntimeError("weight ring exhausted at emission")
            u = ring_free.popleft()
            rb = ringb[u]
            sch.dma(POOL, ring_slot[u], lambda e: e.dma_start(out=dst_ap(ring[:, u, :]), in_=src_ap), (), [rb])
            return rb

        ring_free = deque(range(6))

        def relw(*rbs):
            for rb in rbs:
                ring_free.append(ringb.index(rb))

        scrB = [Bf(wscr_d[u]) for u in range(26)]
        scr_slot = [sch.slot("scr%d" % u) for u in range(26)]
        SCR = {"w_in": 0, "w_out": 14, "w_q": 18, "w_o": 22}
        scr_src = [(w_in, i * 256) for i in range(14)] + [(w_out, q * 256) for q in range(4)] \
            + [(w_q, q * 256) for q in range(4)] + [(w_o, q * 256) for q in range(4)]
        scr_pending = [2, 3, 4, 5] + [u for u in range(26) if u not in (2, 3, 4, 5)]

        def precast(n):
            for _ in range(n):
                if not scr_pending:
                    return
                u = scr_pending.pop(0)
                wd, c0 = scr_src[u]
                src = wd[:, c0:c0 + 256].rearrange("(k p) n -> p k n", p=128)
                dst = wscr_d[u].rearrange("p (k n) -> p k n", k=8)
                sch.dma(POOL, scr_slot[u], (lambda dst, src: lambda e: e.dma_start(out=dst, in_=src))(dst, src), (), [scrB[u]])

        def w_scr(name, q):
            u = SCR[name] + q
            assert u not in scr_pending
            if not ring_free:
                raise RuntimeError("weight ring exhausted at emission")
            r = ring_free.popleft()
            rb = ringb[r]
            sch.dma(SP, ring_slot[r], (lambda r, u: lambda e: e.dma_start(out=ring[:, r, :], in_=wscr_d[u]))(r, u), [scrB[u]], [rb])
            return rb, rb.ap.rearrange("p (k n) -> p k n", k=8)

        def w_cols(wd, c0, n=256):
            src = wd[:, c0:c0 + n].rearrange("(k p) n -> p k n", p=128)
            rb = wload(lambda r: r[:, 0:8 * n].rearrange("p (k n) -> p k n", k=8), src)
            return rb, rb.ap[:, 0:8 * n].rearrange("p (k n) -> p k n", k=8)

        def w_rows(wd, r0):
            src = wd[r0:r0 + 256, :].rearrange("(j p) n -> p j n", p=128)
            rb = wload(lambda r: r.rearrange("p (j n) -> p j n", j=2), src)
            return rb, rb.ap.rearrange("p (j n) -> p j n", j=2)

        def cdma(dst, src, b):
            sch.dma(SP, c_slot, lambda e: e.dma_start(out=dst, in_=src), (), [b])
        cdma(tri.ap, tri_d, tri); cdma(trr.ap, trr_d, trr); cdma(cind.ap, cind_d, cind); cdma(idf.ap, idn_d, idf)
        for h in range(4):
            cdma(mask_sb[:, h, :], tri_d, maskb)
        cdma(hg.ap, hg_d, hg)
        cdma(cw.ap, cw_d.rearrange("(c p) j -> p c j", p=128), cw)
        cdma(lbp_sb[:, 0, :], lbp_d[0:1, :].partition_broadcast(128), lbp)
        cdma(lbp_sb[:, 1, :], lbp_d[1:2, :].partition_broadcast(128), lbp)
        for b_ in (tri, trr, cind, idf, maskb, hg, cw, lbp):
            b_.lw = (c_slot, c_slot.count)
        cp(idb.ap, idf.ap, [idf], [idb])
        ms(ones.ap, 1.0, [ones])
        tt(oml.ap, lbp_sb[:, 1, :], lbp_sb[:, 0, :], ALU.subtract, [lbp], [oml])
        act(oml.ap, oml.ap, AF.Sigmoid, [oml], [oml])
        sch.barrier()

        def load_gain(gd):
            sch.dma(SP, g_slot, lambda e: e.dma_start(out=gb.ap, in_=gd.partition_broadcast(128)), (), [gb])

        def load_x(src, nblk):
            for b in range(nblk):
                sch.dma(SP, xl_slot[b],
                        (lambda b: lambda e: e.dma_start(out=xb[b].ap, in_=src[b * 128:(b + 1) * 128, :]))(b),
                        (), [xb[b]])

        def c16(off, shape):
            n = int(np.prod(shape[1:]))
            ap = a16[:, off:off + n]
            if len(shape) == 3:
                ap = ap.rearrange("p (a b) -> p a b", a=shape[1])
            return ap

        def c32(off, shape):
            n = int(np.prod(shape[1:]))
            ap = a32[:, off:off + n]
            if len(shape) == 3:
                ap = ap.rearrange("p (a b) -> p a b", a=shape[1])
            return ap

        def norm_T(srcs, hT_ap, hT_bufs, hn_bufs, junk, col0=0):
            n = len(srcs)
            ms(ss_sb[:, 0:2 * n], 0.0, [ss])
            for j, s in enumerate(srcs):
                act(junk.ap, s.ap, AF.Square, [s], [junk, ss], accum_out=ss_sb[:, j:j + 1])
            act(ss_sb[:, n:2 * n], ss_sb[:, 0:n], AF.Ln, [ss], [ss], scale=1.0 / D, bias=EPS)
            act(ss_sb[:, 0:n], ss_sb[:, n:2 * n], AF.Exp, [ss], [ss], scale=-0.5)
            for j, s in enumerate(srcs):
                hn = hn_bufs[j % len(hn_bufs)]
                stt(hn.ap, s.ap, ss_sb[:, j:j + 1], gb.ap, ALU.mult, ALU.mult, [s, ss, gb], [hn])
                pt = npt()
                pe_tr([(pt.ap[:, k * 128:(k + 1) * 128], hn.ap[:, k * 128:(k + 1) * 128]) for k in range(8)], [hn], [pt])
                cp(hT_ap[:, :, col0 + j * 128:col0 + (j + 1) * 128], pt.ap.rearrange("p (k n) -> p k n", k=8),
                   [pt], [hT_bufs[j]], eng=ACT)
                rel(pt)

        def ffn(gd, wg, wu, wd, nblk):
            hT_ap = c16(0, [128, 8, 2048])
            hTb = [Bf(hT_ap[:, :, b * 128:(b + 1) * 128]) for b in range(nblk)]
            hid_ap = [c16(16384 + i * 4096, [128, 2, 2048]) for i in range(2)]
            hidb = [[[Bf(hid_ap[i][:, j, t * 512:(t + 1) * 512]) for t in range(4)] for j in range(2)] for i in range(2)]
            hn = [Bf(c32(i * 1024, [128, 1024]).bitcast(BF16)[:, 0:1024]) for i in range(2)]
            junk = Bf(c32(2048, [128, 1024]).bitcast(BF16)[:, 0:1024])
            sg = [Bf(c32(3072 + i * 512, [128, 512])) for i in range(3)]
            load_gain(gd)
            norm_T(xb[:nblk], hT_ap, hTb, hn, junk)
            ntt = nblk // 4
            NG = DFF // 256
            units = {}

            def loadg(g):
                units[g] = (w_cols(wg, g * 256), w_cols(wu, g * 256), w_rows(wd, g * 256))

            def gu(g):
                (gb_, gap), (ub_, uap), _ = units[g]
                i = g % 2
                for j in range(2):
                    for t in range(ntt):
                        pg = nps(); pu = nps()
                        rd = [gb_, ub_] + hTb[t * 4:t * 4 + 4]
                        pe_mm([(pg.ap, [(gap[:, k, j * 128:(j + 1) * 128], hT_ap[:, k, t * 512:(t + 1) * 512]) for k in range(8)])], rd, [pg])
                        pe_mm([(pu.ap, [(uap[:, k, j * 128:(j + 1) * 128], hT_ap[:, k, t * 512:(t + 1) * 512]) for k in range(8)])], rd, [pu])
                        s = sg[(j * ntt + t) % 3]
                        act(s.ap, pg.ap, AF.Silu, [pg], [s])
                        tt(hidb[i][j][t].ap, s.ap, pu.ap, ALU.mult, [s, pu], [hidb[i][j][t]])
                        rel(pg, pu)

            def down(g):
                _, _, (db_, dap) = units[g]
                i = g % 2
                for b in range(nblk):
                    for ch in range(2):
                        pd = nps()
                        pe_mm([(pd.ap, [(hid_ap[i][:, j, b * 128:(b + 1) * 128], dap[:, j, ch * 512:(ch + 1) * 512]) for j in range(2)])],
                              [db_, hidb[i][0][b // 4], hidb[i][1][b // 4]], [pd])
                        xs = xb[b].ap[:, ch * 512:(ch + 1) * 512]
                        stt(xs, pd.ap, 0.5, xs, ALU.mult, ALU.add, [pd, xb[b]], [xb[b]])
                        rel(pd)
                relw(units[g][0][0], units[g][1][0], units[g][2][0])
                del units[g]

            loadg(0)
            loadg(1)
            precast(4)
            gu(0)
            for g in range(NG):
                if g + 1 < NG:
                    gu(g + 1)
                down(g)
                if g + 2 < NG:
                    loadg(g + 2)
                precast(3)

        def interleave(lists, offsets):
            T = max(o + len(l) for l, o in zip(lists, offsets))
            for t in range(T):
                for l, o in zip(lists, offsets):
                    k = t - o
                    if 0 <= k < len(l):
                        l[k]()

        st_ = {"gc": 0}

        def mix_setup():
            m = {}
            m["hT"] = c16(0, [128, 8, 512]); m["hTb"] = [Bf(m["hT"][:, :, b * 128:(b + 1) * 128]) for b in range(4)]
            m["t16"] = [Bf(c16(4096 + i * 512, [128, 512])) for i in range(4)]
            m["v"] = [Bf(c16(6144 + i * 512, [128, 512])) for i in range(4)]
            m["kh"] = [Bf(c16(8192 + i * 512, [128, 512])) for i in range(4)]
            m["ktT"] = c16(10240, [128, 4, 512]); m["ktTb"] = [Bf(m["ktT"][:, :, b * 128:(b + 1) * 128]) for b in range(4)]
            m["qT"] = c16(12288, [128, 4, 512]); m["qTb"] = [Bf(m["qT"][:, h, :]) for h in range(4)]
            m["PT"] = [Bf(c16(14336 + i * 512, [128, 4, 128])) for i in range(2)]
            m["Sbf"] = [Bf(c16(15360 + i * 512, [128, 512])) for i in range(9)]
            m["GT"] = c16(19968, [128, 4, 512]); m["GTb"] = [Bf(m["GT"][:, h, :]) for h in range(4)]
            m["yT"] = c16(22016, [128, 8, 512]); m["yTb"] = [Bf(m["yT"][:, c, :]) for c in range(8)]
            m["A"] = [Bf(c32(i * 512, [128, 512])) for i in range(4)]
            m["Bq"] = [Bf(c32(2048 + i * 512, [128, 512])) for i in range(2)]
            m["C"] = [Bf(c32(3072 + i * 512, [128, 512])) for i in range(4)]
            m["t32"] = [Bf(c32(5120 + i * 512, [128, 512])) for i in range(3)]
            m["ET"] = c32(6656, [128, 4, 512]); m["ETb"] = [Bf(m["ET"][:, :, b * 128:(b + 1) * 128]) for b in range(4)]
            m["S"] = [Bf(c32(8704 + i * 512, [128, 512])) for i in range(2)]
            m["u"] = c32(9728, [128, 4, 514]); m["ub"] = [Bf(m["u"][:, c, :]) for c in range(4)]
            m["hn"] = [Bf(c32(11784 + i * 512, [128, 512]).bitcast(BF16)) for i in range(2)]
            m["junk"] = Bf(c32(12808, [128, 512]).bitcast(BF16))
            m["i16"] = 0; m["i32"] = 0
            return m

        def t16(m):
            m["i16"] += 1
            return m["t16"][m["i16"] % 4]

        def t32(m):
            m["i32"] += 1
            return m["t32"][m["i32"] % 3]

        def mix_init_state(m):
            ms(m["S"][0].ap, 0.0, [m["S"][0]])
            ms(m["Sbf"][0].ap, 0.0, [m["Sbf"][0]])
            ms(m["u"], 0.0, m["ub"])
            st_["gc"] = 0
            st_["si"] = 0

        def mix_restore_state(m):
            sb_c = m["Sbf"][st_["gc"] % 9]
            cp(sb_c.ap, m["S"][st_["si"] % 2].ap, [m["S"][st_["si"] % 2]], [sb_c], eng=ACT)

        def fm_units(m, c0, nchunks, consume):
            units = []
            hold = {}
            for q in range(0, nchunks, 2):
                for j in range(2):
                    def u(q=q, j=j):
                        if j == 0:
                            hold[q] = w_scr("w_in", (c0 + q * 128) // 256)
                        rb, wap = hold[q]
                        p = nps()
                        pe_mm([(p.ap, [(wap[:, k, j * 128:(j + 1) * 128], m["hT"][:, k, :]) for k in range(8)])],
                              [rb] + m["hTb"], [p])
                        consume(q + j, p)
                        rel(p)
                        if j == 1:
                            relw(rb)
                    units.append(u)
            return units

        def fm_proj(m, c0, nchunks, consume):
            for u in fm_units(m, c0, nchunks, consume):
                u()

        def mix_tile(m, blks, state_only, want_u):
            load_gain(g_mix)
            norm_T(blks, m["hT"], m["hTb"], m["hn"], m["junk"])
            hT = m["hT"]
            wf = [w_scr("w_in", 2), w_scr("w_in", 3)]
            wi = [w_scr("w_in", 4), w_scr("w_in", 5)]
            vbs = m["v"]; Sidx = [None] * 8
            TB = [slice(b * 128, (b + 1) * 128) for b in range(4)]
            A = m["A"]; C = m["C"]; Bq = [m["Bq"][b % 2] for b in range(4)]; khb = m["kh"]

            def tm_stages(grp):
                pzf = {}; pzi = {}; pBr = {}; pDc = {}; pB = {}; pBT = {}; kt = {}; pS = {}
                L = []

                def s_zf():
                    for b in grp:
                        pzf[b] = nps()
                        pe_mm([(pzf[b].ap[:, q * 256:(q + 1) * 256], [(hT[:, k, TB[b]], wf[q][1][:, k, :]) for k in range(8)]) for q in range(2)],
                              [wf[0][0], wf[1][0], m["hTb"][b]], [pzf[b]])
                L.append(s_zf)

                def s_zi():
                    for b in grp:
                        pzi[b] = nps()
                        pe_mm([(pzi[b].ap[:, q * 256:(q + 1) * 256], [(hT[:, k, TB[b]], wi[q][1][:, k, :]) for k in range(8)]) for q in range(2)],
                              [wi[0][0], wi[1][0], m["hTb"][b]], [pzi[b]])
                L.append(s_zi)

                def s_sig():
                    for b in grp:
                        act(A[b].ap, pzf[b].ap, AF.Exp, [pzf[b]], [A[b]])
                        rel(pzf[b])
                        act(A[b].ap, A[b].ap, AF.Ln, [A[b]], [A[b]], bias=1.0)
                        act(A[b].ap, A[b].ap, AF.Exp, [A[b]], [A[b]], scale=-1.0)
                L.append(s_sig)

                def s_v():
                    for b in grp:
                        act(vbs[b].ap, pzi[b].ap, AF.Copy, [pzi[b]], [vbs[b]])
                        rel(pzi[b])
                L.append(s_v)

                def s_k():
                    for b in grp:
                        tt(A[b].ap, A[b].ap, oml.ap, ALU.mult, [A[b], oml], [A[b]])
                L.append(s_k)

                def s_ln():
                    for b in grp:
                        act(Bq[b].ap, A[b].ap, AF.Ln, [A[b]], [Bq[b]], scale=-1.0, bias=1.0)
                L.append(s_ln)

                def s_cum():
                    for b in grp:
                        pBr[b] = nps()
                        pe_mm([(pBr[b].ap, [(trr.ap, Bq[b].ap)])], [trr, Bq[b]], [pBr[b]])
                        if state_only:
                            pDc[b] = nps()
                            pe_mm([(pDc[b].ap[:, h * 2:(h + 1) * 2], [(Bq[b].ap[:, h * 128:(h + 1) * 128], cind.ap)]) for h in range(4)],
                                  [cind, Bq[b]], [pDc[b]])
                        else:
                            pB[b] = nps()
                            pe_mm([(pB[b].ap, [(tri.ap, Bq[b].ap)])], [tri, Bq[b]], [pB[b]])
                            pBT[b] = nps()
                            pe_mm([(pBT[b].ap[:, h * 128:(h + 1) * 128], [(Bq[b].ap[:, h * 128:(h + 1) * 128], tri.ap)]) for h in range(4)],
                                  [tri, Bq[b]], [pBT[b]])
                L.append(s_cum)

                def s_er():
                    for b in grp:
                        act(C[b].ap, pBr[b].ap, AF.Exp, [pBr[b]], [C[b]])
                        rel(pBr[b])
                L.append(s_er)

                def s_kh():
                    for b in grp:
                        tt(khb[b].ap, A[b].ap, C[b].ap, ALU.mult, [A[b], C[b]], [khb[b]])
                L.append(s_kh)

                if state_only:
                    def s_dec():
                        for b in grp:
                            act(dec_sb[:, b, :], pDc[b].ap[:, 0:8], AF.Exp, [pDc[b]], [dec])
                            rel(pDc[b])
                    L.append(s_dec)
                else:
                    def s_et():
                        for b in grp:
                            act(m["ET"][:, :, TB[b]], pBT[b].ap.rearrange("p (h n) -> p h n", h=4), AF.Exp, [pBT[b]], [m["ETb"][b]])
                            rel(pBT[b])
                    L.append(s_et)

                    def s_eb():
                        for b in grp:
                            act(C[b].ap, pB[b].ap, AF.Exp, [pB[b]], [C[b]], scale=-1.0)
                            rel(pB[b])
                    L.append(s_eb)

                    def s_kt():
                        for b in grp:
                            kt[b] = t16(m)
                            tt(kt[b].ap, A[b].ap, C[b].ap, ALU.mult, [A[b], C[b]], [kt[b]])
                    L.append(s_kt)

                    def s_tr():
                        for b in grp:
                            pt = npt()
                            pe_tr([(pt.ap[:, h * 128:(h + 1) * 128], kt[b].ap[:, h * 128:(h + 1) * 128]) for h in range(4)], [kt[b]], [pt])
                            cp(m["ktT"][:, :, TB[b]], pt.ap[:, 0:512].rearrange("p (h n) -> p h n", h=4), [pt], [m["ktTb"][b]])
                            rel(pt)
                    L.append(s_tr)

                def s_ps():
                    for b in grp:
                        for c in range(2):
                            cs = slice(c * 64, (c + 1) * 64)
                            pS[b, c] = nps()
                            pe_mm([(pS[b, c].ap[:, h * 128:(h + 1) * 128], [(khb[b].ap[cs, h * 128:(h + 1) * 128], vbs[b].ap[cs, h * 128:(h + 1) * 128])]) for h in range(4)],
                                  [khb[b], vbs[b]], [pS[b, c]])
                L.append(s_ps)

                def s_chain():
                    for b in grp:
                        for c in range(2):
                            So = m["S"][st_["si"] % 2]; Sn = m["S"][(st_["si"] + 1) % 2]
                            st_["si"] += 1
                            for h in range(4):
                                hs = slice(h * 128, (h + 1) * 128)
                                if state_only:
                                    dap = dec_sb[:, b, h * 2 + c:h * 2 + c + 1]; dbf = dec
                                else:
                                    col = b * 128 + c * 64 + 63
                                    dap = m["ET"][:, h, col:col + 1]; dbf = m["ETb"][b]
                                stt(Sn.ap[:, hs], So.ap[:, hs], dap, pS[b, c].ap[:, hs], ALU.mult, ALU.add, [So, dbf, pS[b, c]], [Sn])
                            Sidx[b * 2 + c] = st_["gc"] % 9
                            st_["gc"] += 1
                            sb_n = m["Sbf"][st_["gc"] % 9]
                            cp(sb_n.ap, Sn.ap, [Sn], [sb_n])
                            rel(pS[b, c])
                L.append(s_chain)
                return L

            pzc = {}

            def take_c(c, p):
                zcs = t32(m)
                act(zcs.ap, p.ap, AF.Copy, [p], [zcs])
                pzc[c] = zcs

            def take_u(c, p):
                tt(m["u"][:, c, 2:514], pzc[c].ap, p.ap, ALU.mult, [pzc[c], p], [m["ub"][c]])

            def halo(c):
                cp(m["u"][:, c, 0:2], m["u"][:, c, 512:514], [m["ub"][c]], [m["ub"][c]])

            def conv_u_units():
                us = []
                for q in (0, 2):
                    us += fm_units(m, 2560 + q * 128, 2, (lambda q: lambda c, p: take_c(q + c, p))(q))
                    us += fm_units(m, 3072 + q * 128, 2, (lambda q: lambda c, p: take_u(q + c, p))(q))
                return us

            def sig_of(p):
                sg_ = t32(m)
                act(sg_.ap, p.ap, AF.Exp, [p], [sg_], scale=-1.0)
                act(sg_.ap, sg_.ap, AF.Ln, [sg_], [sg_], bias=1.0)
                act(sg_.ap, sg_.ap, AF.Exp, [sg_], [sg_], scale=-1.0)
                return sg_

            def take_g(h, p):
                sg_ = sig_of(p)
                tt(m["GT"][:, h, :], sg_.ap, p.ap, ALU.mult, [sg_, p], [m["GTb"][h]])

            fill = []
            if not state_only:
                fill = fm_units(m, 1536, 4, take_g) + conv_u_units()
            elif want_u:
                fill = conv_u_units()
            interleave([tm_stages((b,)) for b in range(4)] + [fill], TM_OFF)
            relw(wf[0][0], wf[1][0], wi[0][0], wi[1][0])
            if state_only:
                if want_u:
                    for c in range(4):
                        halo(c)
                return
            def take_q(h, p):
                sq = sig_of(p)
                tt(sq.ap, sq.ap, p.ap, ALU.mult, [sq, p], [sq])
                stt(m["qT"][:, h, :], sq.ap, 128.0 ** -0.5, m["ET"][:, h, :], ALU.mult, ALU.mult, [sq] + m["ETb"], [m["qTb"][h]])
            fm_proj(m, 0, 4, take_q)


            def o_stages(grp):
                pSc = {}; pO = {}; pN = {}; sqo = {}
                L = []

                def s0():
                    for b in grp:
                        pSc[b] = nps()
                        pe_mm([(pSc[b].ap[:, h * 128:(h + 1) * 128], [(m["ktT"][:, h, TB[b]], m["qT"][:, h, TB[b]])]) for h in range(4)],
                              [m["ktTb"][b]] + m["qTb"], [pSc[b]])
                L.append(s0)

                def s1():
                    for b in grp:
                        PT = m["PT"][b % 2]
                        tt(PT.ap, pSc[b].ap.rearrange("p (h n) -> p h n", h=4), maskb.ap, ALU.mult, [pSc[b], maskb], [PT])
                        rel(pSc[b])
                L.append(s1)

                def s2():
                    for b in grp:
                        PT = m["PT"][b % 2]; v = vbs[b]
                        pO[b] = nps()
                        groups = []
                        rds = [v, PT] + m["qTb"]
                        for h in range(4):
                            for c in range(2):
                                cs = slice(c * 64, (c + 1) * 64)
                                sbf = m["Sbf"][Sidx[b * 2 + c]]
                                rds.append(sbf)
                                groups.append((pO[b].ap[:, h * 128 + c * 64:h * 128 + (c + 1) * 64],
                                               [(v.ap[cs, h * 128:(h + 1) * 128], PT.ap[cs, h, c * 64:(c + 1) * 64]),
                                                (sbf.ap[:, h * 128:(h + 1) * 128], m["qT"][:, h, b * 128 + c * 64:b * 128 + (c + 1) * 64])]))
                        pe_mm(groups, rds, [pO[b]])
                L.append(s2)

                def s3():
                    for b in grp:
                        sqo[b] = t16(m)
                        act(sqo[b].ap, pO[b].ap, AF.Square, [pO[b]], [sqo[b]])
                L.append(s3)

                def s4():
                    for b in grp:
                        pN[b] = nps()
                        pe_mm([(pN[b].ap[:, h * 128:(h + 1) * 128], [(ones.ap, sqo[b].ap[:, h * 128:(h + 1) * 128])]) for h in range(4)],
                              [ones, sqo[b]], [pN[b]])
                L.append(s4)

                def s5():
                    for b in grp:
                        act(A[b].ap, pN[b].ap, AF.Ln, [pN[b]], [A[b]], scale=1.0 / 128, bias=EPS)
                        rel(pN[b])
                L.append(s5)

                def s6():
                    for b in grp:
                        act(C[b].ap, A[b].ap, AF.Exp, [A[b]], [C[b]], scale=-0.5)
                L.append(s6)

                def s7():
                    for b in grp:
                        tt(A[b].ap, pO[b].ap, C[b].ap, ALU.mult, [pO[b], C[b]], [A[b]])
                        rel(pO[b])
                L.append(s7)

                def s8():
                    for b in grp:
                        for h in range(4):
                            stt(m["yT"][:, h, TB[b]], A[b].ap[:, h * 128:(h + 1) * 128], hg_sb[:, h:h + 1], m["GT"][:, h, TB[b]], ALU.mult, ALU.mult,
                                [A[b], hg, m["GTb"][h]], [m["yTb"][h]])
                L.append(s8)
                return L

            def take_b(c, p):
                a1 = t32(m)
                ts(a1.ap, m["u"][:, c, 2:514], cw_sb[:, c, 2:3], None, ALU.mult, None, [m["ub"][c], cw], [a1])
                a2 = t32(m)
                stt(a2.ap, m["u"][:, c, 1:513], cw_sb[:, c, 1:2], a1.ap, ALU.mult, ALU.add, [m["ub"][c], cw, a1], [a2])
                a3 = t32(m)
                stt(a3.ap, m["u"][:, c, 0:512], cw_sb[:, c, 0:1], a2.ap, ALU.mult, ALU.add, [m["ub"][c], cw, a2], [a3])
                tt(m["yT"][:, 4 + c, :], a3.ap, p.ap, ALU.mult, [a3, p], [m["yTb"][4 + c]])
                halo(c)

            interleave([o_stages((b,)) for b in range(4)] + [fm_units(m, 2048, 4, take_b)], O_OFF)

            for q in range(4):
                rb, wap = w_scr("w_out", q)
                for b in range(4):
                    tb = slice(b * 128, (b + 1) * 128)
                    p = nps()
                    pe_mm([(p.ap[:, 0:256], [(m["yT"][:, fc, tb], wap[:, fc, :]) for fc in range(8)])], [rb] + m["yTb"], [p])
                    xs = blks[b].ap[:, q * 256:(q + 1) * 256]
                    tt(xs, p.ap[:, 0:256], xs, ALU.add, [p, blks[b]], [blks[b]])
                    rel(p)
                relw(rb)

        def xattn():
            hT = c16(0, [128, 8, 512]); hTb = [Bf(hT[:, :, b * 128:(b + 1) * 128]) for b in range(4)]
            qmT = c16(4096, [128, 8, 512]); qmTb = [Bf(qmT[:, c, :]) for c in range(8)]
            kmT = c16(8192, [128, 8, 256]); kmTb = Bf(kmT)
            vm = c16(10240, [128, 2, 1024]); vmb = Bf(vm)
            mnT = c16(12288, [128, 8, 256]); mnTb = [Bf(mnT[:, :, b * 128:(b + 1) * 128]) for b in range(2)]
            pbuf = [Bf(c16(12288 + i * 1024, [128, 4, 256])) for i in range(4)]
            pTb = [Bf(c16(16384 + i * 1024, [128, 8, 128])) for i in range(4)]
            attT = c16(20480, [128, 8, 512]); attTb = [Bf(attT[:, :, b * 128:(b + 1) * 128]) for b in range(4)]
            hn = [Bf(c32(i * 512, [128, 512]).bitcast(BF16)) for i in range(2)]
            junk = Bf(c32(1024, [128, 512]).bitcast(BF16))
            memx = [Bf(c32(1536 + i * 1024, [128, 1024])) for i in range(2)]
            pe_ = [Bf(c32(3584 + i * 1024, [128, 4, 256])) for i in range(4)]
            scale = 256.0 ** -0.5
            sm4 = c32(7680, [128, 128])
            for i in range(2):
                sch.dma(SP, xl_slot[i], (lambda i: lambda e: e.dma_start(out=memx[i].ap, in_=mem_d[i * 128:(i + 1) * 128, :]))(i), (), [memx[i]])
            load_gain(g_mem)
            norm_T(memx, mnT, mnTb, hn, junk)
            for q in range(4):
                rb, wap = w_cols(w_kv, q * 256)
                for j in range(2):
                    p = nps()
                    pe_mm([(p.ap[:, 0:256], [(wap[:, k, j * 128:(j + 1) * 128], mnT[:, k, :]) for k in range(8)])], [rb] + mnTb, [p])
                    cp(kmT[:, q * 2 + j, :], p.ap[:, 0:256], [p], [kmTb], eng=ACT)
                    rel(p)
                relw(rb)
            for q in range(4):
                rb, wap = w_cols(w_kv, 1024 + q * 256)
                for mb in range(2):
                    p = nps()
                    pe_mm([(p.ap[:, 0:256], [(mnT[:, k, mb * 128:(mb + 1) * 128], wap[:, k, :]) for k in range(8)])], [rb] + mnTb, [p])
                    cp(vm[:, mb, q * 256:(q + 1) * 256], p.ap[:, 0:256], [p], [vmb], eng=ACT)
                    rel(p)
                relw(rb)
            sch.barrier()
            for ti in range(4):
                blks = xb[ti * 4:ti * 4 + 4]
                load_gain(g_xat)
                norm_T(blks, hT, hTb, hn, junk)
                for q in range(4):
                    rb, wap = w_scr("w_q", q)
                    for j in range(2):
                        p = nps()
                        pe_mm([(p.ap, [(wap[:, k, j * 128:(j + 1) * 128], hT[:, k, :]) for k in range(8)])], [rb] + hTb, [p])
                        cp(qmT[:, q * 2 + j, :], p.ap, [p], [qmTb[q * 2 + j]], eng=ACT if j else DVE)
                        rel(p)
                    relw(rb)
                TB = [slice(b * 128, (b + 1) * 128) for b in range(4)]
                def a_stages(grp):
                    so = {b: b * 32 for b in grp}
                    smb = {b: Bf(sm4[:, b * 32:(b + 1) * 32]) for b in grp}
                    pscs = {}; pts = {}; pAs = {}
                    L = []

                    def s0():
                        for b in grp:
                            ms(sm4[:, so[b]:so[b] + 4], 0.0, [smb[b]])
                        for b in grp:
                            for hp in range(2):
                                p = nps()
                                pe_mm([(p.ap[:, hh * 256:(hh + 1) * 256],
                                        [(qmT[:, (hp * 2 + hh) * 2 + dc, TB[b]], kmT[:, (hp * 2 + hh) * 2 + dc, :]) for dc in range(2)]) for hh in range(2)],
                                      qmTb + [kmTb], [p])
                                pscs[b, hp] = p
                    L.append(s0)

                    def s1():
                        for b in grp:
                            for hp in range(2):
                                p = pscs[b, hp]
                                sch.op(DVE, (lambda o, i: lambda e: e.tensor_reduce(o, i, AX.X, ALU.max))(
                                    sm4[:, so[b] + 8 + hp * 2:so[b] + 8 + hp * 2 + 2], p.ap.rearrange("p (h n) -> p h n", h=2)), [p], [smb[b]])
                            ts(sm4[:, so[b] + 16:so[b] + 20], sm4[:, so[b] + 8:so[b] + 12], -scale, None, ALU.mult, None, [smb[b]], [smb[b]])
                    L.append(s1)

                    def s2():
                        for b in grp:
                            for h in range(4):
                                p = pscs[b, h // 2]
                                act(pe_[b].ap[:, h, :], p.ap[:, (h % 2) * 256:(h % 2 + 1) * 256], AF.Exp, [p, smb[b]], [pe_[b], smb[b]],
                                    scale=scale, bias=sm4[:, so[b] + 16 + h:so[b] + 17 + h], accum_out=sm4[:, so[b] + h:so[b] + h + 1])
                            rel(pscs[b, 0], pscs[b, 1])
                    L.append(s2)

                    def s3():
                        for b in grp:
                            sch.op(DVE, (lambda o: lambda e: e.reciprocal(sm4[:, o + 24:o + 28], sm4[:, o:o + 4]))(so[b]), [smb[b]], [smb[b]])
                            for h in range(4):
                                ts(pbuf[b].ap[:, h, :], pe_[b].ap[:, h, :], sm4[:, so[b] + 24 + h:so[b] + 25 + h], None, ALU.mult, None,
                                   [pe_[b], smb[b]], [pbuf[b]])
                    L.append(s3)

                    def s4():
                        for b in grp:
                            pts[b] = npt()
                            pe_tr([(pts[b].ap[:, (h * 2 + mc) * 128:(h * 2 + mc + 1) * 128], pbuf[b].ap[:, h, mc * 128:(mc + 1) * 128])
                                   for h in range(4) for mc in range(2)], [pbuf[b]], [pts[b]])
                    L.append(s4)

                    def s5():
                        for b in grp:
                            cp(pTb[b].ap, pts[b].ap.rearrange("p (k n) -> p k n", k=8), [pts[b]], [pTb[b]], eng=ACT)
                            rel(pts[b])
                    L.append(s5)

                    def s6():
                        for b in grp:
                            pT = pTb[b]
                            for half in range(2):
                                pA = nps()
                                pe_mm([(pA.ap[:, (ci - half * 4) * 128:(ci - half * 4 + 1) * 128],
                                        [(vm[:, mc, ci * 128:(ci + 1) * 128], pT.ap[:, (ci // 2) * 2 + mc, :]) for mc in range(2)])
                                       for ci in range(half * 4, half * 4 + 4)], [vmb, pT], [pA])
                                pAs[b, half] = pA
                    L.append(s6)

                    def s7():
                        for b in grp:
                            cp(attT[:, 0:4, TB[b]], pAs[b, 0].ap.rearrange("p (k n) -> p k n", k=4), [pAs[b, 0]], [attTb[b]], eng=ACT)
                            cp(attT[:, 4:8, TB[b]], pAs[b, 1].ap.rearrange("p (k n) -> p k n", k=4), [pAs[b, 1]], [attTb[b]])
                            rel(pAs[b, 0], pAs[b, 1])
                    L.append(s7)
                    return L

                interleave([a_stages((b,)) for b in range(4)], A_OFF)
                for q in range(4):
                    rb, wap = w_scr("w_o", q)
                    for b in range(4):
                        tb = slice(b * 128, (b + 1) * 128)
                        p = nps()
                        pe_mm([(p.ap[:, 0:256], [(attT[:, fc, tb], wap[:, fc, :]) for fc in range(8)])], [rb] + attTb, [p])
                        xs = blks[b].ap[:, q * 256:(q + 1) * 256]
                        tt(xs, p.ap[:, 0:256], xs, ALU.add, [p, blks[b]], [blks[b]])
                        rel(p)
                    relw(rb)

        def final(do_norm=True):
            ob = [Bf(c32(i * 1024, [128, 1024])) for i in range(2)]
            junk = Bf(c32(2048, [128, 512]).bitcast(BF16))
            if do_norm:
                load_gain(g_fin)
                ms(ss_sb[:, 0:32], 0.0, [ss])
                for b in range(NB):
                    act(junk.ap, xb[b].ap, AF.Square, [xb[b]], [junk, ss], accum_out=ss_sb[:, b:b + 1])
                act(ss_sb[:, 16:32], ss_sb[:, 0:16], AF.Sqrt, [ss], [ss], scale=1.0 / D, bias=EPS)
                sch.op(DVE, lambda e: e.reciprocal(ss_sb[:, 0:16], ss_sb[:, 16:32]), [ss], [ss])
            for b in range(NB):
                o = ob[b % 2]
                if do_norm:
                    stt(o.ap, xb[b].ap, ss_sb[:, b:b + 1], gb.ap, ALU.mult, ALU.mult, [xb[b], ss, gb], [o])
                else:
                    cp(o.ap, xb[b].ap, [xb[b]], [o])
                sch.dma(SP, st_slot[b], (lambda b, o: lambda e: e.dma_start(out=out_d[b * 128:(b + 1) * 128, :], in_=o.ap))(b, o), [o], ())

        m = None
        if pre_tok > 0 and (stop_after is None or stop_after >= 2):
            load_x(xp_d, npb)
            ffn(g_ffn1, w1g, w1u, w1d, npb)
            sch.barrier()
            m = mix_setup()
            mix_init_state(m)
            for ti in range(npb // 4):
                mix_tile(m, xb[ti * 4:ti * 4 + 4], True, ti == npb // 4 - 1)
            sch.barrier()
        load_x(x_d, NB)
        ffn(g_ffn1, w1g, w1u, w1d, NB)
        sch.barrier()
        if stop_after is None or stop_after >= 2:
            if m is None:
                m = mix_setup()
                mix_init_state(m)
            else:
                mix_restore_state(m)
            for ti in range(4):
                mix_tile(m, xb[ti * 4:ti * 4 + 4], False, False)
            sch.barrier()
        if stop_after is None or stop_after >= 3:
            xattn()
            sch.barrier()
        if stop_after is None or stop_after >= 4:
            ffn(g_ffn2, w2g, w2u, w2d, NB)
            sch.barrier()
        final(do_norm=(stop_after is None))
        sch.finish()
        sch.replay(block)
    return nc


def _consts():
    i = np.arange(128)
    same = (i[:, None] // 64) == (i[None, :] // 64)
    tri = (same & (i[:, None] <= i[None, :])).astype(np.float32)
    trirev = (same & (i[:, None] > i[None, :])).astype(np.float32)
    cind = np.stack([(i < 64), (i >= 64)], axis=1).astype(np.float32)
    return tri, trirev, cind, np.eye(128, dtype=np.float32)


def make_in_maps(inputs, pre_tok=PRE_TOK):
    f = lambda k: np.ascontiguousarray(np.asarray(inputs[k], dtype=np.float32))
    x = f("x"); mem = f("mem")
    tri, trirev, cind, ident = _consts()
    shared = {
        "ffn1_norm": f("ffn1_norm").reshape(1, D), "mix_norm": f("mix_norm").reshape(1, D),
        "xattn_norm": f("xattn_norm").reshape(1, D), "mem_norm": f("mem_norm").reshape(1, D),
        "ffn2_norm": f("ffn2_norm").reshape(1, D), "final_norm": f("final_norm").reshape(1, D),
        "ffn1_gate": f("ffn1_gate")[0], "ffn1_up": f("ffn1_up")[0], "ffn1_down": f("ffn1_down")[0],
        "ffn2_gate": f("ffn2_gate")[0], "ffn2_up": f("ffn2_up")[0], "ffn2_down": f("ffn2_down")[0],
        "w_in": f("w_in")[0], "w_out": f("w_out")[0], "w_q_mem": f("w_q_mem")[0],
        "w_kv_mem": f("w_kv_mem")[0], "w_o_mem": f("w_o_mem")[0],
        "lb_param": f("lb_param"), "hg": np.ascontiguousarray(f("hgrn_out_norm").reshape(4, 128).T),
        "conv_w": f("conv_w")[0],
        "c_tri": tri, "c_trirev": trirev, "c_cind": cind, "c_ident": ident,
    }
    maps = []
    npre = max(pre_tok, 128)
    for c in range(8):
        b, half = c // 2, c % 2
        t0 = half * TOK
        if half == 0 or pre_tok == 0:
            xp = np.zeros((npre, D), np.float32)
        else:
            xp = np.ascontiguousarray(x[b, t0 - npre:t0])
        d = dict(shared)
        d["x"] = np.ascontiguousarray(x[b, t0:t0 + TOK])
        d["xp"] = xp
        d["mem"] = mem[b]
        maps.append(d)
    return maps


def kernel(**inputs):
    nc = build_nc()
    maps = make_in_maps(inputs)
    res = run_bass_kernel_spmd(nc, maps, core_ids=list(range(8)))
    out = np.empty((4, 2 * TOK, D), np.float32)
    for c in range(8):
        out[c // 2, (c % 2) * TOK:(c % 2 + 1) * TOK] = res.results[c]["out"]
    return out
```

```python
import numpy as np
from contextlib import ExitStack
import concourse.bass as bass
import concourse.mybir as mybir
from concourse.bass_utils import run_bass_kernel_spmd

F32 = mybir.dt.float32
BF16 = mybir.dt.bfloat16
AF = mybir.ActivationFunctionType
ALU = mybir.AluOpType
AX = mybir.AxisListType

D = 1024
DFF = 2816
TOK = 2048
NB = TOK // 128
EPS = 1e-6
PRE_TOK = 2048
STOP_AFTER = None
TM_OFF = [0, 4, 8, 12, 2]
O_OFF = [0, 1, 2, 3, 1]
A_OFF = [0, 1, 2, 3]


class _Src:
    def __init__(self, name, sem):
        self.name = name
        self.sem = sem
        self.count = 0


class _Eng(_Src):
    def __init__(self, name, sem):
        super().__init__(name, sem)
        self.items = []
        self.waited = {}


class Bf:
    __slots__ = ("ap", "lw", "rd")

    def __init__(self, ap):
        self.ap = ap
        self.lw = None
        self.rd = {}


class Sched:
    def __init__(self, nc, stack):
        self.nc = nc
        self.stack = stack
        self.srcs = []
        self.pe = self._eng("pe")
        self.act = self._eng("act")
        self.dve = self._eng("dve")
        self.pool = self._eng("pool")
        self.sp = self._eng("sp")
        self.engs = [self.pe, self.act, self.dve, self.pool, self.sp]

    def _eng(self, name):
        e = _Eng(name, self.stack.enter_context(self.nc.semaphore("s_" + name)))
        self.srcs.append(e)
        return e

    def slot(self, name):
        s = _Src(name, self.stack.enter_context(self.nc.semaphore("d_" + name)))
        self.srcs.append(s)
        return s

    def _wait(self, eng, src, val):
        if val <= 0 or eng.waited.get(src, 0) >= val:
            return
        eng.waited[src] = val
        eng.items.append(("w", src, val))

    def op(self, eng, fn, reads=(), writes=()):
        for t in reads:
            if t.lw is not None:
                src, val = t.lw
                if src is eng and eng is self.pe:
                    continue
                self._wait(eng, src, val)
        for t in writes:
            if t.lw is not None and t.lw[0] is not eng:
                self._wait(eng, *t.lw)
            for src, val in t.rd.items():
                if src is not eng:
                    self._wait(eng, src, val)
        eng.count += 1
        eng.items.append(("o", fn, eng, eng.count))
        for t in reads:
            t.rd[eng] = eng.count
        for t in writes:
            t.lw = (eng, eng.count)
            t.rd = {}

    def dma(self, q, slot, fn, reads=(), writes=()):
        for t in reads:
            if t.lw is not None:
                self._wait(q, *t.lw)
        for t in writes:
            if t.lw is not None:
                self._wait(q, *t.lw)
            for src, val in t.rd.items():
                self._wait(q, src, val)
        slot.count += 16
        q.items.append(("o", fn, slot, slot.count))
        for t in reads:
            t.rd[slot] = slot.count
        for t in writes:
            t.lw = (slot, slot.count)
            t.rd = {}

    def barrier(self):
        for e in self.engs:
            for s in self.srcs:
                if s is not e:
                    self._wait(e, s, s.count)

    def finish(self):
        for s in self.srcs:
            if s is not self.sp:
                self._wait(self.sp, s, s.count)

    def replay(self, block):
        import bisect
        need = {}
        for e in self.engs:
            for it in e.items:
                if it[0] == "w":
                    need.setdefault(it[1], set()).add(it[2])
        rank = {src: sorted(v) for src, v in need.items()}
        engset = set(self.engs)

        def run(items):
            def body(e):
                for it in items:
                    if it[0] == "w":
                        src, val = it[1], it[2]
                        if src in engset:
                            val = bisect.bisect_left(rank[src], val) + 1
                        e.wait_ge(src.sem, val)
                    else:
                        ins = it[1](e)
                        src, c = it[2], it[3]
                        if src in engset:
                            if c in need.get(src, ()):
                                ins.then_inc(src.sem, 1)
                        else:
                            ins.then_inc(src.sem, 16)
            return body
        block.tensor(run(self.pe.items))
        block.scalar(run(self.act.items))
        block.vector(run(self.dve.items))
        block.gpsimd(run(self.pool.items))
        block.sync(run(self.sp.items))


def build_nc(pre_tok=PRE_TOK, stop_after=STOP_AFTER):
    nc = bass.Bass("TRN2", target_bir_lowering=False)
    npb = pre_tok // 128

    def din(name, shape):
        return nc.dram_tensor(name, list(shape), F32, kind="ExternalInput").ap()

    x_d = din("x", [TOK, D])
    xp_d = din("xp", [max(pre_tok, 128), D])
    mem_d = din("mem", [256, D])
    g_ffn1 = din("ffn1_norm", [1, D]); g_mix = din("mix_norm", [1, D]); g_xat = din("xattn_norm", [1, D])
    g_mem = din("mem_norm", [1, D]); g_ffn2 = din("ffn2_norm", [1, D]); g_fin = din("final_norm", [1, D])
    w1g = din("ffn1_gate", [D, DFF]); w1u = din("ffn1_up", [D, DFF]); w1d = din("ffn1_down", [DFF, D])
    w2g = din("ffn2_gate", [D, DFF]); w2u = din("ffn2_up", [D, DFF]); w2d = din("ffn2_down", [DFF, D])
    w_in = din("w_in", [D, 3584]); w_out = din("w_out", [D, D])
    w_q = din("w_q_mem", [D, D]); w_kv = din("w_kv_mem", [D, 2 * D]); w_o = din("w_o_mem", [D, D])
    lbp_d = din("lb_param", [2, 512]); hg_d = din("hg", [128, 4]); cw_d = din("conv_w", [512, 3])
    tri_d = din("c_tri", [128, 128]); trr_d = din("c_trirev", [128, 128]); cind_d = din("c_cind", [128, 2])
    idn_d = din("c_ident", [128, 128])
    out_d = nc.dram_tensor("out", [TOK, D], F32, kind="ExternalOutput").ap()
    wscr_d = nc.dram_tensor("wscr", [26, 128, 2048], BF16).ap()

    with ExitStack() as st:
        sch = Sched(nc, st)
        sb = lambda name, shape, dt: st.enter_context(nc.sbuf_tensor(name, list(shape), dt))
        x_sb = sb("x_sb", [128, NB, D], F32)
        ring = sb("ring", [128, 6, 2048], BF16)
        a16 = sb("a16", [128, 26112], BF16)
        a32 = sb("a32", [128, 13376], F32)
        gb_sb = sb("gb", [128, D], F32)
        oml_sb = sb("oml", [128, 512], F32)
        mask_sb = sb("maskb", [128, 4, 128], F32)
        tri_sb = sb("tri", [128, 128], F32)
        trr_sb = sb("trr", [128, 128], F32)
        cind_sb = sb("cind", [128, 2], F32)
        idf_sb = sb("idf", [128, 128], F32)
        idb_sb = sb("idb", [128, 128], BF16)
        one_sb = sb("ones", [128, 128], BF16)
        hg_sb = sb("hgs", [128, 4], F32)
        cw_sb = sb("cws", [128, 4, 3], F32)
        ss_sb = sb("ss", [128, 64], F32)
        sm_sb = sb("smx", [128, 64], F32)
        dec_sb = sb("dec", [128, 4, 8], F32)
        psf = [st.enter_context(nc.psum_tensor("psf%d" % i, [128, 512], F32)) for i in range(6)]
        pst = [st.enter_context(nc.psum_tensor("pst%d" % i, [128, 1024], BF16)) for i in range(2)]
        block = st.enter_context(nc.Block())

        PE, ACT, DVE, POOL, SP = sch.pe, sch.act, sch.dve, sch.pool, sch.sp

        xb = [Bf(x_sb[:, b, :]) for b in range(NB)]
        ringb = [Bf(ring[:, u, :]) for u in range(6)]
        ring_slot = [sch.slot("ring%d" % u) for u in range(6)]
        xl_slot = [sch.slot("xl%d" % b) for b in range(NB)]
        st_slot = [sch.slot("st%d" % b) for b in range(NB)]
        c_slot = sch.slot("const")
        g_slot = sch.slot("gain")
        PSF = [Bf(p[:]) for p in psf]
        PST = [Bf(p[:]) for p in pst]
        gb = Bf(gb_sb[:]); oml = Bf(oml_sb[:]); lbp_sb = a32[:, 0:1024].rearrange("p (a b) -> p a b", a=2); lbp = Bf(lbp_sb); maskb = Bf(mask_sb[:])
        tri = Bf(tri_sb[:]); trr = Bf(trr_sb[:]); cind = Bf(cind_sb[:]); idf = Bf(idf_sb[:])
        idb = Bf(idb_sb[:]); ones = Bf(one_sb[:]); hg = Bf(hg_sb[:]); cw = Bf(cw_sb[:])
        ss = Bf(ss_sb[:]); sm = Bf(sm_sb[:]); dec = Bf(dec_sb[:])
        cnt = {"psf": 0, "pst": 0, "ring": 0}

        from collections import deque
        free_f = deque(PSF); free_t = deque(PST)

        def nps():
            if not free_f:
                raise RuntimeError("PSUM fp32 pool exhausted at emission")
            return free_f.popleft()

        def npt():
            if not free_t:
                raise RuntimeError("PSUM bf16 pool exhausted at emission")
            return free_t.popleft()

        def rel(*bs):
            for b in bs:
                (free_t if b in PST else free_f).append(b)

        def mmg(groups):
            def fn(e):
                ins = None
                for out, pairs in groups:
                    n = len(pairs)
                    for i, (l, r) in enumerate(pairs):
                        ins = e.matmul(out, l, r, start=(i == 0), stop=(i == n - 1))
                return ins
            return fn

        def pe_mm(groups, reads, writes):
            sch.op(PE, mmg(groups), reads, writes)

        def pe_tr(pairs, reads, writes):
            def fn(e):
                ins = None
                for o, i in pairs:
                    ins = e.transpose(o, i, idb.ap)
                return ins
            sch.op(PE, fn, list(reads) + [idb], writes)

        def act(out, in_, func, reads, writes, **kw):
            sch.op(ACT, lambda e: e.activation(out, in_, func, **kw), reads, writes)

        def tt(out, a, b, op, reads, writes, eng=None):
            sch.op(eng or DVE, lambda e: e.tensor_tensor(out, a, b, op), reads, writes)

        def stt(out, a, s, b, op0, op1, reads, writes, eng=None):
            sch.op(eng or DVE, lambda e: e.scalar_tensor_tensor(out, a, s, b, op0, op1), reads, writes)

        def ts(out, a, s1, s2, op0, op1, reads, writes, eng=None):
            if s2 is None:
                sch.op(eng or DVE, lambda e: e.tensor_scalar(out, a, s1, None, op0), reads, writes)
            else:
                sch.op(eng or DVE, lambda e: e.tensor_scalar(out, a, s1, s2, op0, op1), reads, writes)

        def cp(out, a, reads, writes, eng=None):
            if eng is ACT:
                sch.op(ACT, lambda e: e.activation(out, a, AF.Copy), reads, writes)
            else:
                sch.op(eng or DVE, lambda e: e.tensor_copy(out, a), reads, writes)

        def ms(out, val, writes, eng=None):
            sch.op(eng or DVE, lambda e: e.memset(out, val), (), writes)

        def wload(dst_ap, src_ap):
            if not ring_free:
                raise RuntimeError("weight ring exhausted at emission")
            u = ring_free.popleft()
            rb = ringb[u]
            sch.dma(POOL, ring_slot[u], lambda e: e.dma_start(out=dst_ap(ring[:, u, :]), in_=src_ap), (), [rb])
            return rb

        ring_free = deque(range(6))

        def relw(*rbs):
            for rb in rbs:
                ring_free.append(ringb.index(rb))

        scrB = [Bf(wscr_d[u]) for u in range(26)]
        scr_slot = [sch.slot("scr%d" % u) for u in range(26)]
        SCR = {"w_in": 0, "w_out": 14, "w_q": 18, "w_o": 22}
        scr_src = [(w_in, i * 256) for i in range(14)] + [(w_out, q * 256) for q in range(4)] \
            + [(w_q, q * 256) for q in range(4)] + [(w_o, q * 256) for q in range(4)]
        scr_pending = [2, 3, 4, 5] + [u for u in range(26) if u not in (2, 3, 4, 5)]

        def precast(n):
            for _ in range(n):
                if not scr_pending:
                    return
                u = scr_pending.pop(0)
                wd, c0 = scr_src[u]
                src = wd[:, c0:c0 + 256].rearrange("(k p) n -> p k n", p=128)
                dst = wscr_d[u].rearrange("p (k n) -> p k n", k=8)
                sch.dma(POOL, scr_slot[u], (lambda dst, src: lambda e: e.dma_start(out=dst, in_=src))(dst, src), (), [scrB[u]])

        def w_scr(name, q):
            u = SCR[name] + q
            assert u not in scr_pending
            if not ring_free:
                raise RuntimeError("weight ring exhausted at emission")
            r = ring_free.popleft()
            rb = ringb[r]
            sch.dma(SP, ring_slot[r], (lambda r, u: lambda e: e.dma_start(out=ring[:, r, :], in_=wscr_d[u]))(r, u), [scrB[u]], [rb])
            return rb, rb.ap.rearrange("p (k n) -> p k n", k=8)

        def w_cols(wd, c0, n=256):
            src = wd[:, c0:c0 + n].rearrange("(k p) n -> p k n", p=128)
            rb = wload(lambda r: r[:, 0:8 * n].rearrange("p (k n) -> p k n", k=8), src)
            return rb, rb.ap[:, 0:8 * n].rearrange("p (k n) -> p k n", k=8)

        def w_rows(wd, r0):
            src = wd[r0:r0 + 256, :].rearrange("(j p) n -> p j n", p=128)
            rb = wload(lambda r: r.rearrange("p (j n) -> p j n", j=2), src)
            return rb, rb.ap.rearrange("p (j n) -> p j n", j=2)

        def cdma(dst, src, b):
            sch.dma(SP, c_slot, lambda e: e.dma_start(out=dst, in_=src), (), [b])
        cdma(tri.ap, tri_d, tri); cdma(trr.ap, trr_d, trr); cdma(cind.ap, cind_d, cind); cdma(idf.ap, idn_d, idf)
        for h in range(4):
            cdma(mask_sb[:, h, :], tri_d, maskb)
        cdma(hg.ap, hg_d, hg)
        cdma(cw.ap, cw_d.rearrange("(c p) j -> p c j", p=128), cw)
        cdma(lbp_sb[:, 0, :], lbp_d[0:1, :].partition_broadcast(128), lbp)
        cdma(lbp_sb[:, 1, :], lbp_d[1:2, :].partition_broadcast(128), lbp)
        for b_ in (tri, trr, cind, idf, maskb, hg, cw, lbp):
            b_.lw = (c_slot, c_slot.count)
        cp(idb.ap, idf.ap, [idf], [idb])
        ms(ones.ap, 1.0, [ones])
        tt(oml.ap, lbp_sb[:, 1, :], lbp_sb[:, 0, :], ALU.subtract, [lbp], [oml])
        act(oml.ap, oml.ap, AF.Sigmoid, [oml], [oml])
        sch.barrier()

        def load_gain(gd):
            sch.dma(SP, g_slot, lambda e: e.dma_start(out=gb.ap, in_=gd.partition_broadcast(128)), (), [gb])

        def load_x(src, nblk):
            for b in range(nblk):
                sch.dma(SP, xl_slot[b],
                        (lambda b: lambda e: e.dma_start(out=xb[b].ap, in_=src[b * 128:(b + 1) * 128, :]))(b),
                        (), [xb[b]])

        def c16(off, shape):
            n = int(np.prod(shape[1:]))
            ap = a16[:, off:off + n]
            if len(shape) == 3:
                ap = ap.rearrange("p (a b) -> p a b", a=shape[1])
            return ap

        def c32(off, shape):
            n = int(np.prod(shape[1:]))
            ap = a32[:, off:off + n]
            if len(shape) == 3:
                ap = ap.rearrange("p (a b) -> p a b", a=shape[1])
            return ap

        def norm_T(srcs, hT_ap, hT_bufs, hn_bufs, junk, col0=0):
            n = len(srcs)
            ms(ss_sb[:, 0:2 * n], 0.0, [ss])
            for j, s in enumerate(srcs):
                act(junk.ap, s.ap, AF.Square, [s], [junk, ss], accum_out=ss_sb[:, j:j + 1])
            act(ss_sb[:, n:2 * n], ss_sb[:, 0:n], AF.Sqrt, [ss], [ss], scale=1.0 / D, bias=EPS)
            sch.op(DVE, lambda e: e.reciprocal(ss_sb[:, 0:n], ss_sb[:, n:2 * n]), [ss], [ss])
            for j, s in enumerate(srcs):
                hn = hn_bufs[j % len(hn_bufs)]
                stt(hn.ap, s.ap, ss_sb[:, j:j + 1], gb.ap, ALU.mult, ALU.mult, [s, ss, gb], [hn])
                pt = npt()
                pe_tr([(pt.ap[:, k * 128:(k + 1) * 128], hn.ap[:, k * 128:(k + 1) * 128]) for k in range(8)], [hn], [pt])
                cp(hT_ap[:, :, col0 + j * 128:col0 + (j + 1) * 128], pt.ap.rearrange("p (k n) -> p k n", k=8),
                   [pt], [hT_bufs[j]], eng=ACT)
                rel(pt)

        def ffn(gd, wg, wu, wd, nblk):
            hT_ap = c16(0, [128, 8, 2048])
            hTb = [Bf(hT_ap[:, :, b * 128:(b + 1) * 128]) for b in range(nblk)]
            hid_ap = [c16(16384 + i * 4096, [128, 2, 2048]) for i in range(2)]
            hidb = [[[Bf(hid_ap[i][:, j, t * 512:(t + 1) * 512]) for t in range(4)] for j in range(2)] for i in range(2)]
            hn = [Bf(c32(i * 1024, [128, 1024]).bitcast(BF16)[:, 0:1024]) for i in range(2)]
            junk = Bf(c32(2048, [128, 1024]).bitcast(BF16)[:, 0:1024])
            sg = [Bf(c32(3072 + i * 512, [128, 512])) for i in range(3)]
            load_gain(gd)
            norm_T(xb[:nblk], hT_ap, hTb, hn, junk)
            ntt = nblk // 4
            NG = DFF // 256
            units = {}

            def loadg(g):
                units[g] = (w_cols(wg, g * 256), w_cols(wu, g * 256), w_rows(wd, g * 256))

            def gu(g):
                (gb_, gap), (ub_, uap), _ = units[g]
                i = g % 2
                for j in range(2):
                    for t in range(ntt):
                        pg = nps(); pu = nps()
                        rd = [gb_, ub_] + hTb[t * 4:t * 4 + 4]
                        pe_mm([(pg.ap, [(gap[:, k, j * 128:(j + 1) * 128], hT_ap[:, k, t * 512:(t + 1) * 512]) for k in range(8)])], rd, [pg])
                        pe_mm([(pu.ap, [(uap[:, k, j * 128:(j + 1) * 128], hT_ap[:, k, t * 512:(t + 1) * 512]) for k in range(8)])], rd, [pu])
                        s = sg[(j * ntt + t) % 3]
                        act(s.ap, pg.ap, AF.Silu, [pg], [s])
                        tt(hidb[i][j][t].ap, s.ap, pu.ap, ALU.mult, [s, pu], [hidb[i][j][t]])
                        rel(pg, pu)

            def down(g):
                _, _, (db_, dap) = units[g]
                i = g % 2
                for b in range(nblk):
                    for ch in range(2):
                        pd = nps()
                        pe_mm([(pd.ap, [(hid_ap[i][:, j, b * 128:(b + 1) * 128], dap[:, j, ch * 512:(ch + 1) * 512]) for j in range(2)])],
                              [db_, hidb[i][0][b // 4], hidb[i][1][b // 4]], [pd])
                        xs = xb[b].ap[:, ch * 512:(ch + 1) * 512]
                        stt(xs, pd.ap, 0.5, xs, ALU.mult, ALU.add, [pd, xb[b]], [xb[b]])
                        rel(pd)
                relw(units[g][0][0], units[g][1][0], units[g][2][0])
                del units[g]

            loadg(0)
            loadg(1)
            precast(4)
            gu(0)
            for g in range(NG):
                if g + 1 < NG:
                    gu(g + 1)
                down(g)
                if g + 2 < NG:
                    loadg(g + 2)
                precast(3)

        def interleave(lists, offsets):
            T = max(o + len(l) for l, o in zip(lists, offsets))
            for t in range(T):
                for l, o in zip(lists, offsets):
                    k = t - o
                    if 0 <= k < len(l):
                        l[k]()

        st_ = {"gc": 0}

        def mix_setup():
            m = {}
            m["hT"] = c16(0, [128, 8, 512]); m["hTb"] = [Bf(m["hT"][:, :, b * 128:(b + 1) * 128]) for b in range(4)]
            m["t16"] = [Bf(c16(4096 + i * 512, [128, 512])) for i in range(4)]
            m["v"] = [Bf(c16(6144 + i * 512, [128, 512])) for i in range(4)]
            m["kh"] = [Bf(c16(8192 + i * 512, [128, 512])) for i in range(4)]
            m["ktT"] = c16(10240, [128, 4, 512]); m["ktTb"] = [Bf(m["ktT"][:, :, b * 128:(b + 1) * 128]) for b in range(4)]
            m["qT"] = c16(12288, [128, 4, 512]); m["qTb"] = [Bf(m["qT"][:, h, :]) for h in range(4)]
            m["PT"] = [Bf(c16(14336 + i * 512, [128, 4, 128])) for i in range(2)]
            m["Sbf"] = [Bf(c16(15360 + i * 512, [128, 512])) for i in range(9)]
            m["GT"] = c16(19968, [128, 4, 512]); m["GTb"] = [Bf(m["GT"][:, h, :]) for h in range(4)]
            m["yT"] = c16(22016, [128, 8, 512]); m["yTb"] = [Bf(m["yT"][:, c, :]) for c in range(8)]
            m["A"] = [Bf(c32(i * 512, [128, 512])) for i in range(4)]
            m["Bq"] = [Bf(c32(2048 + i * 512, [128, 512])) for i in range(2)]
            m["C"] = [Bf(c32(3072 + i * 512, [128, 512])) for i in range(4)]
            m["t32"] = [Bf(c32(5120 + i * 512, [128, 512])) for i in range(3)]
            m["ET"] = c32(6656, [128, 4, 512]); m["ETb"] = [Bf(m["ET"][:, :, b * 128:(b + 1) * 128]) for b in range(4)]
            m["S"] = [Bf(c32(8704 + i * 512, [128, 512])) for i in range(2)]
            m["u"] = c32(9728, [128, 4, 514]); m["ub"] = [Bf(m["u"][:, c, :]) for c in range(4)]
            m["hn"] = [Bf(c32(11784 + i * 512, [128, 512]).bitcast(BF16)) for i in range(2)]
            m["junk"] = Bf(c32(12808, [128, 512]).bitcast(BF16))
            m["i16"] = 0; m["i32"] = 0
            return m

        def t16(m):
            m["i16"] += 1
            return m["t16"][m["i16"] % 4]

        def t32(m):
            m["i32"] += 1
            return m["t32"][m["i32"] % 3]

        def mix_init_state(m):
            ms(m["S"][0].ap, 0.0, [m["S"][0]])
            ms(m["Sbf"][0].ap, 0.0, [m["Sbf"][0]])
            ms(m["u"], 0.0, m["ub"])
            st_["gc"] = 0
            st_["si"] = 0

        def mix_restore_state(m):
            sb_c = m["Sbf"][st_["gc"] % 9]
            cp(sb_c.ap, m["S"][st_["si"] % 2].ap, [m["S"][st_["si"] % 2]], [sb_c], eng=ACT)

        def fm_units(m, c0, nchunks, consume):
            units = []
            hold = {}
            for q in range(0, nchunks, 2):
                for j in range(2):
                    def u(q=q, j=j):
                        if j == 0:
                            hold[q] = w_scr("w_in", (c0 + q * 128) // 256)
                        rb, wap = hold[q]
                        p = nps()
                        pe_mm([(p.ap, [(wap[:, k, j * 128:(j + 1) * 128], m["hT"][:, k, :]) for k in range(8)])],
                              [rb] + m["hTb"], [p])
                        consume(q + j, p)
                        rel(p)
                        if j == 1:
                            relw(rb)
                    units.append(u)
            return units

        def fm_proj(m, c0, nchunks, consume):
            for u in fm_units(m, c0, nchunks, consume):
                u()

        def mix_tile(m, blks, state_only, want_u):
            load_gain(g_mix)
            norm_T(blks, m["hT"], m["hTb"], m["hn"], m["junk"])
            hT = m["hT"]
            wf = [w_scr("w_in", 2), w_scr("w_in", 3)]
            wi = [w_scr("w_in", 4), w_scr("w_in", 5)]
            vbs = m["v"]; Sidx = [None] * 8
            TB = [slice(b * 128, (b + 1) * 128) for b in range(4)]
            A = m["A"]; C = m["C"]; Bq = [m["Bq"][b % 2] for b in range(4)]; khb = m["kh"]

            def tm_stages(grp):
                pzf = {}; pzi = {}; pBr = {}; pDc = {}; pB = {}; pBT = {}; kt = {}; pS = {}
                L = []

                def s_zf():
                    for b in grp:
                        pzf[b] = nps()
                        pe_mm([(pzf[b].ap[:, q * 256:(q + 1) * 256], [(hT[:, k, TB[b]], wf[q][1][:, k, :]) for k in range(8)]) for q in range(2)],
                              [wf[0][0], wf[1][0], m["hTb"][b]], [pzf[b]])
                L.append(s_zf)

                def s_zi():
                    for b in grp:
                        pzi[b] = nps()
                        pe_mm([(pzi[b].ap[:, q * 256:(q + 1) * 256], [(hT[:, k, TB[b]], wi[q][1][:, k, :]) for k in range(8)]) for q in range(2)],
                              [wi[0][0], wi[1][0], m["hTb"][b]], [pzi[b]])
                L.append(s_zi)

                def s_sig():
                    for b in grp:
                        act(A[b].ap, pzf[b].ap, AF.Sigmoid, [pzf[b]], [A[b]], scale=-1.0)
                        rel(pzf[b])
                L.append(s_sig)

                def s_v():
                    for b in grp:
                        act(vbs[b].ap, pzi[b].ap, AF.Copy, [pzi[b]], [vbs[b]])
                        rel(pzi[b])
                L.append(s_v)

                def s_k():
                    for b in grp:
                        tt(A[b].ap, A[b].ap, oml.ap, ALU.mult, [A[b], oml], [A[b]])
                L.append(s_k)

                def s_ln():
                    for b in grp:
                        act(Bq[b].ap, A[b].ap, AF.Ln, [A[b]], [Bq[b]], scale=-1.0, bias=1.0)
                L.append(s_ln)

                def s_cum():
                    for b in grp:
                        pBr[b] = nps()
                        pe_mm([(pBr[b].ap, [(trr.ap, Bq[b].ap)])], [trr, Bq[b]], [pBr[b]])
                        if state_only:
                            pDc[b] = nps()
                            pe_mm([(pDc[b].ap[:, h * 2:(h + 1) * 2], [(Bq[b].ap[:, h * 128:(h + 1) * 128], cind.ap)]) for h in range(4)],
                                  [cind, Bq[b]], [pDc[b]])
                        else:
                            pB[b] = nps()
                            pe_mm([(pB[b].ap, [(tri.ap, Bq[b].ap)])], [tri, Bq[b]], [pB[b]])
                            pBT[b] = nps()
                            pe_mm([(pBT[b].ap[:, h * 128:(h + 1) * 128], [(Bq[b].ap[:, h * 128:(h + 1) * 128], tri.ap)]) for h in range(4)],
                                  [tri, Bq[b]], [pBT[b]])
                L.append(s_cum)

                def s_er():
                    for b in grp:
                        act(C[b].ap, pBr[b].ap, AF.Exp, [pBr[b]], [C[b]])
                        rel(pBr[b])
                L.append(s_er)

                def s_kh():
                    for b in grp:
                        tt(khb[b].ap, A[b].ap, C[b].ap, ALU.mult, [A[b], C[b]], [khb[b]])
                L.append(s_kh)

                if state_only:
                    def s_dec():
                        for b in grp:
                            act(dec_sb[:, b, :], pDc[b].ap[:, 0:8], AF.Exp, [pDc[b]], [dec])
                            rel(pDc[b])
                    L.append(s_dec)
                else:
                    def s_et():
                        for b in grp:
                            act(m["ET"][:, :, TB[b]], pBT[b].ap.rearrange("p (h n) -> p h n", h=4), AF.Exp, [pBT[b]], [m["ETb"][b]])
                            rel(pBT[b])
                    L.append(s_et)

                    def s_eb():
                        for b in grp:
                            act(C[b].ap, pB[b].ap, AF.Exp, [pB[b]], [C[b]], scale=-1.0)
                            rel(pB[b])
                    L.append(s_eb)

                    def s_kt():
                        for b in grp:
                            kt[b] = t16(m)
                            tt(kt[b].ap, A[b].ap, C[b].ap, ALU.mult, [A[b], C[b]], [kt[b]])
                    L.append(s_kt)

                    def s_tr():
                        for b in grp:
                            pt = npt()
                            pe_tr([(pt.ap[:, h * 128:(h + 1) * 128], kt[b].ap[:, h * 128:(h + 1) * 128]) for h in range(4)], [kt[b]], [pt])
                            cp(m["ktT"][:, :, TB[b]], pt.ap[:, 0:512].rearrange("p (h n) -> p h n", h=4), [pt], [m["ktTb"][b]])
                            rel(pt)
                    L.append(s_tr)

                def s_ps():
                    for b in grp:
                        for c in range(2):
                            cs = slice(c * 64, (c + 1) * 64)
                            pS[b, c] = nps()
                            pe_mm([(pS[b, c].ap[:, h * 128:(h + 1) * 128], [(khb[b].ap[cs, h * 128:(h + 1) * 128], vbs[b].ap[cs, h * 128:(h + 1) * 128])]) for h in range(4)],
                                  [khb[b], vbs[b]], [pS[b, c]])
                L.append(s_ps)

                def s_chain():
                    for b in grp:
                        for c in range(2):
                            So = m["S"][st_["si"] % 2]; Sn = m["S"][(st_["si"] + 1) % 2]
                            st_["si"] += 1
                            for h in range(4):
                                hs = slice(h * 128, (h + 1) * 128)
                                if state_only:
                                    dap = dec_sb[:, b, h * 2 + c:h * 2 + c + 1]; dbf = dec
                                else:
                                    col = b * 128 + c * 64 + 63
                                    dap = m["ET"][:, h, col:col + 1]; dbf = m["ETb"][b]
                                stt(Sn.ap[:, hs], So.ap[:, hs], dap, pS[b, c].ap[:, hs], ALU.mult, ALU.add, [So, dbf, pS[b, c]], [Sn])
                            Sidx[b * 2 + c] = st_["gc"] % 9
                            st_["gc"] += 1
                            sb_n = m["Sbf"][st_["gc"] % 9]
                            cp(sb_n.ap, Sn.ap, [Sn], [sb_n])
                            rel(pS[b, c])
                L.append(s_chain)
                return L

            pzc = {}

            def take_c(c, p):
                zcs = t32(m)
                act(zcs.ap, p.ap, AF.Copy, [p], [zcs])
                pzc[c] = zcs

            def take_u(c, p):
                tt(m["u"][:, c, 2:514], pzc[c].ap, p.ap, ALU.mult, [pzc[c], p], [m["ub"][c]])

            def halo(c):
                cp(m["u"][:, c, 0:2], m["u"][:, c, 512:514], [m["ub"][c]], [m["ub"][c]])

            def conv_u_units():
                us = []
                for q in (0, 2):
                    us += fm_units(m, 2560 + q * 128, 2, (lambda q: lambda c, p: take_c(q + c, p))(q))
                    us += fm_units(m, 3072 + q * 128, 2, (lambda q: lambda c, p: take_u(q + c, p))(q))
                return us

            def take_g(h, p):
                act(m["GT"][:, h, :], p.ap, AF.Silu, [p], [m["GTb"][h]])

            fill = []
            if not state_only:
                fill = fm_units(m, 1536, 4, take_g) + conv_u_units()
            elif want_u:
                fill = conv_u_units()
            interleave([tm_stages((b,)) for b in range(4)] + [fill], TM_OFF)
            relw(wf[0][0], wf[1][0], wi[0][0], wi[1][0])
            if state_only:
                if want_u:
                    for c in range(4):
                        halo(c)
                return
            def take_q(h, p):
                sq = t32(m)
                act(sq.ap, p.ap, AF.Silu, [p], [sq])
                stt(m["qT"][:, h, :], sq.ap, 128.0 ** -0.5, m["ET"][:, h, :], ALU.mult, ALU.mult, [sq] + m["ETb"], [m["qTb"][h]])
            fm_proj(m, 0, 4, take_q)


            def o_stages(grp):
                pSc = {}; pO = {}; pN = {}; sqo = {}
                L = []

                def s0():
                    for b in grp:
                        pSc[b] = nps()
                        pe_mm([(pSc[b].ap[:, h * 128:(h + 1) * 128], [(m["ktT"][:, h, TB[b]], m["qT"][:, h, TB[b]])]) for h in range(4)],
                              [m["ktTb"][b]] + m["qTb"], [pSc[b]])
                L.append(s0)

                def s1():
                    for b in grp:
                        PT = m["PT"][b % 2]
                        tt(PT.ap, pSc[b].ap.rearrange("p (h n) -> p h n", h=4), maskb.ap, ALU.mult, [pSc[b], maskb], [PT])
                        rel(pSc[b])
                L.append(s1)

                def s2():
                    for b in grp:
                        PT = m["PT"][b % 2]; v = vbs[b]
                        pO[b] = nps()
                        groups = []
                        rds = [v, PT] + m["qTb"]
                        for h in range(4):
                            for c in range(2):
                                cs = slice(c * 64, (c + 1) * 64)
                                sbf = m["Sbf"][Sidx[b * 2 + c]]
                                rds.append(sbf)
                                groups.append((pO[b].ap[:, h * 128 + c * 64:h * 128 + (c + 1) * 64],
                                               [(v.ap[cs, h * 128:(h + 1) * 128], PT.ap[cs, h, c * 64:(c + 1) * 64]),
                                                (sbf.ap[:, h * 128:(h + 1) * 128], m["qT"][:, h, b * 128 + c * 64:b * 128 + (c + 1) * 64])]))
                        pe_mm(groups, rds, [pO[b]])
                L.append(s2)

                def s3():
                    for b in grp:
                        sqo[b] = t16(m)
                        act(sqo[b].ap, pO[b].ap, AF.Square, [pO[b]], [sqo[b]])
                L.append(s3)

                def s4():
                    for b in grp:
                        pN[b] = nps()
                        pe_mm([(pN[b].ap[:, h * 128:(h + 1) * 128], [(ones.ap, sqo[b].ap[:, h * 128:(h + 1) * 128])]) for h in range(4)],
                              [ones, sqo[b]], [pN[b]])
                L.append(s4)

                def s5():
                    for b in grp:
                        act(A[b].ap, pN[b].ap, AF.Ln, [pN[b]], [A[b]], scale=1.0 / 128, bias=EPS)
                        rel(pN[b])
                L.append(s5)

                def s6():
                    for b in grp:
                        act(C[b].ap, A[b].ap, AF.Exp, [A[b]], [C[b]], scale=-0.5)
                L.append(s6)

                def s7():
                    for b in grp:
                        tt(A[b].ap, pO[b].ap, C[b].ap, ALU.mult, [pO[b], C[b]], [A[b]])
                        rel(pO[b])
                L.append(s7)

                def s8():
                    for b in grp:
                        for h in range(4):
                            stt(m["yT"][:, h, TB[b]], A[b].ap[:, h * 128:(h + 1) * 128], hg_sb[:, h:h + 1], m["GT"][:, h, TB[b]], ALU.mult, ALU.mult,
                                [A[b], hg, m["GTb"][h]], [m["yTb"][h]])
                L.append(s8)
                return L

            def take_b(c, p):
                a1 = t32(m)
                ts(a1.ap, m["u"][:, c, 2:514], cw_sb[:, c, 2:3], None, ALU.mult, None, [m["ub"][c], cw], [a1])
                a2 = t32(m)
                stt(a2.ap, m["u"][:, c, 1:513], cw_sb[:, c, 1:2], a1.ap, ALU.mult, ALU.add, [m["ub"][c], cw, a1], [a2])
                a3 = t32(m)
                stt(a3.ap, m["u"][:, c, 0:512], cw_sb[:, c, 0:1], a2.ap, ALU.mult, ALU.add, [m["ub"][c], cw, a2], [a3])
                tt(m["yT"][:, 4 + c, :], a3.ap, p.ap, ALU.mult, [a3, p], [m["yTb"][4 + c]])
                halo(c)

            interleave([o_stages((b,)) for b in range(4)] + [fm_units(m, 2048, 4, take_b)], O_OFF)

            for q in range(4):
                rb, wap = w_scr("w_out", q)
                for b in range(4):
                    tb = slice(b * 128, (b + 1) * 128)
                    p = nps()
                    pe_mm([(p.ap[:, 0:256], [(m["yT"][:, fc, tb], wap[:, fc, :]) for fc in range(8)])], [rb] + m["yTb"], [p])
                    xs = blks[b].ap[:, q * 256:(q + 1) * 256]
                    tt(xs, p.ap[:, 0:256], xs, ALU.add, [p, blks[b]], [blks[b]])
                    rel(p)
                relw(rb)

        def xattn():
            hT = c16(0, [128, 8, 512]); hTb = [Bf(hT[:, :, b * 128:(b + 1) * 128]) for b in range(4)]
            qmT = c16(4096, [128, 8, 512]); qmTb = [Bf(qmT[:, c, :]) for c in range(8)]
            kmT = c16(8192, [128, 8, 256]); kmTb = Bf(kmT)
            vm = c16(10240, [128, 2, 1024]); vmb = Bf(vm)
            mnT = c16(12288, [128, 8, 256]); mnTb = [Bf(mnT[:, :, b * 128:(b + 1) * 128]) for b in range(2)]
            pbuf = [Bf(c16(12288 + i * 1024, [128, 4, 256])) for i in range(4)]
            pTb = [Bf(c16(16384 + i * 1024, [128, 8, 128])) for i in range(4)]
            attT = c16(20480, [128, 8, 512]); attTb = [Bf(attT[:, :, b * 128:(b + 1) * 128]) for b in range(4)]
            hn = [Bf(c32(i * 512, [128, 512]).bitcast(BF16)) for i in range(2)]
            junk = Bf(c32(1024, [128, 512]).bitcast(BF16))
            memx = [Bf(c32(1536 + i * 1024, [128, 1024])) for i in range(2)]
            pe_ = [Bf(c32(3584 + i * 1024, [128, 4, 256])) for i in range(4)]
            scale = 256.0 ** -0.5
            sm4 = c32(7680, [128, 128])
            for i in range(2):
                sch.dma(SP, xl_slot[i], (lambda i: lambda e: e.dma_start(out=memx[i].ap, in_=mem_d[i * 128:(i + 1) * 128, :]))(i), (), [memx[i]])
            load_gain(g_mem)
            norm_T(memx, mnT, mnTb, hn, junk)
            for q in range(4):
                rb, wap = w_cols(w_kv, q * 256)
                for j in range(2):
                    p = nps()
                    pe_mm([(p.ap[:, 0:256], [(wap[:, k, j * 128:(j + 1) * 128], mnT[:, k, :]) for k in range(8)])], [rb] + mnTb, [p])
                    cp(kmT[:, q * 2 + j, :], p.ap[:, 0:256], [p], [kmTb], eng=ACT)
                    rel(p)
                relw(rb)
            for q in range(4):
                rb, wap = w_cols(w_kv, 1024 + q * 256)
                for mb in range(2):
                    p = nps()
                    pe_mm([(p.ap[:, 0:256], [(mnT[:, k, mb * 128:(mb + 1) * 128], wap[:, k, :]) for k in range(8)])], [rb] + mnTb, [p])
                    cp(vm[:, mb, q * 256:(q + 1) * 256], p.ap[:, 0:256], [p], [vmb], eng=ACT)
                    rel(p)
                relw(rb)
            sch.barrier()
            for ti in range(4):
                blks = xb[ti * 4:ti * 4 + 4]
                load_gain(g_xat)
                norm_T(blks, hT, hTb, hn, junk)
                for q in range(4):
                    rb, wap = w_scr("w_q", q)
                    for j in range(2):
                        p = nps()
                        pe_mm([(p.ap, [(wap[:, k, j * 128:(j + 1) * 128], hT[:, k, :]) for k in range(8)])], [rb] + hTb, [p])
                        cp(qmT[:, q * 2 + j, :], p.ap, [p], [qmTb[q * 2 + j]], eng=ACT if j else DVE)
                        rel(p)
                    relw(rb)
                TB = [slice(b * 128, (b + 1) * 128) for b in range(4)]
                def a_stages(grp):
                    so = {b: b * 32 for b in grp}
                    smb = {b: Bf(sm4[:, b * 32:(b + 1) * 32]) for b in grp}
                    pscs = {}; pts = {}; pAs = {}
                    L = []

                    def s0():
                        for b in grp:
                            ms(sm4[:, so[b]:so[b] + 4], 0.0, [smb[b]])
                        for b in grp:
                            for hp in range(2):
                                p = nps()
                                pe_mm([(p.ap[:, hh * 256:(hh + 1) * 256],
                                        [(qmT[:, (hp * 2 + hh) * 2 + dc, TB[b]], kmT[:, (hp * 2 + hh) * 2 + dc, :]) for dc in range(2)]) for hh in range(2)],
                                      qmTb + [kmTb], [p])
                                pscs[b, hp] = p
                    L.append(s0)

                    def s1():
                        for b in grp:
                            for hp in range(2):
                                p = pscs[b, hp]
                                sch.op(DVE, (lambda o, i: lambda e: e.tensor_reduce(o, i, AX.X, ALU.max))(
                                    sm4[:, so[b] + 8 + hp * 2:so[b] + 8 + hp * 2 + 2], p.ap.rearrange("p (h n) -> p h n", h=2)), [p], [smb[b]])
                            ts(sm4[:, so[b] + 16:so[b] + 20], sm4[:, so[b] + 8:so[b] + 12], -scale, None, ALU.mult, None, [smb[b]], [smb[b]])
                    L.append(s1)

                    def s2():
                        for b in grp:
                            for h in range(4):
                                p = pscs[b, h // 2]
                                act(pe_[b].ap[:, h, :], p.ap[:, (h % 2) * 256:(h % 2 + 1) * 256], AF.Exp, [p, smb[b]], [pe_[b], smb[b]],
                                    scale=scale, bias=sm4[:, so[b] + 16 + h:so[b] + 17 + h], accum_out=sm4[:, so[b] + h:so[b] + h + 1])
                            rel(pscs[b, 0], pscs[b, 1])
                    L.append(s2)

                    def s3():
                        for b in grp:
                            sch.op(DVE, (lambda o: lambda e: e.reciprocal(sm4[:, o + 24:o + 28], sm4[:, o:o + 4]))(so[b]), [smb[b]], [smb[b]])
                            for h in range(4):
                                ts(pbuf[b].ap[:, h, :], pe_[b].ap[:, h, :], sm4[:, so[b] + 24 + h:so[b] + 25 + h], None, ALU.mult, None,
                                   [pe_[b], smb[b]], [pbuf[b]])
                    L.append(s3)

                    def s4():
                        for b in grp:
                            pts[b] = npt()
                            pe_tr([(pts[b].ap[:, (h * 2 + mc) * 128:(h * 2 + mc + 1) * 128], pbuf[b].ap[:, h, mc * 128:(mc + 1) * 128])
                                   for h in range(4) for mc in range(2)], [pbuf[b]], [pts[b]])
                    L.append(s4)

                    def s5():
                        for b in grp:
                            cp(pTb[b].ap, pts[b].ap.rearrange("p (k n) -> p k n", k=8), [pts[b]], [pTb[b]], eng=ACT)
                            rel(pts[b])
                    L.append(s5)

                    def s6():
                        for b in grp:
                            pT = pTb[b]
                            for half in range(2):
                                pA = nps()
                                pe_mm([(pA.ap[:, (ci - half * 4) * 128:(ci - half * 4 + 1) * 128],
                                        [(vm[:, mc, ci * 128:(ci + 1) * 128], pT.ap[:, (ci // 2) * 2 + mc, :]) for mc in range(2)])
                                       for ci in range(half * 4, half * 4 + 4)], [vmb, pT], [pA])
                                pAs[b, half] = pA
                    L.append(s6)

                    def s7():
                        for b in grp:
                            cp(attT[:, 0:4, TB[b]], pAs[b, 0].ap.rearrange("p (k n) -> p k n", k=4), [pAs[b, 0]], [attTb[b]], eng=ACT)
                            cp(attT[:, 4:8, TB[b]], pAs[b, 1].ap.rearrange("p (k n) -> p k n", k=4), [pAs[b, 1]], [attTb[b]])
                            rel(pAs[b, 0], pAs[b, 1])
                    L.append(s7)
                    return L

                interleave([a_stages((b,)) for b in range(4)], A_OFF)
                for q in range(4):
                    rb, wap = w_scr("w_o", q)
                    for b in range(4):
                        tb = slice(b * 128, (b + 1) * 128)
                        p = nps()
                        pe_mm([(p.ap[:, 0:256], [(attT[:, fc, tb], wap[:, fc, :]) for fc in range(8)])], [rb] + attTb, [p])
                        xs = blks[b].ap[:, q * 256:(q + 1) * 256]
                        tt(xs, p.ap[:, 0:256], xs, ALU.add, [p, blks[b]], [blks[b]])
                        rel(p)
                    relw(rb)

        def final(do_norm=True):
            ob = [Bf(c32(i * 1024, [128, 1024])) for i in range(2)]
            junk = Bf(c32(2048, [128, 512]).bitcast(BF16))
            if do_norm:
                load_gain(g_fin)
                ms(ss_sb[:, 0:32], 0.0, [ss])
                for b in range(NB):
                    act(junk.ap, xb[b].ap, AF.Square, [xb[b]], [junk, ss], accum_out=ss_sb[:, b:b + 1])
                act(ss_sb[:, 16:32], ss_sb[:, 0:16], AF.Sqrt, [ss], [ss], scale=1.0 / D, bias=EPS)
                sch.op(DVE, lambda e: e.reciprocal(ss_sb[:, 0:16], ss_sb[:, 16:32]), [ss], [ss])
            for b in range(NB):
                o = ob[b % 2]
                if do_norm:
                    stt(o.ap, xb[b].ap, ss_sb[:, b:b + 1], gb.ap, ALU.mult, ALU.mult, [xb[b], ss, gb], [o])
                else:
                    cp(o.ap, xb[b].ap, [xb[b]], [o])
                sch.dma(SP, st_slot[b], (lambda b, o: lambda e: e.dma_start(out=out_d[b * 128:(b + 1) * 128, :], in_=o.ap))(b, o), [o], ())

        m = None
        if pre_tok > 0 and (stop_after is None or stop_after >= 2):
            load_x(xp_d, npb)
            ffn(g_ffn1, w1g, w1u, w1d, npb)
            sch.barrier()
            m = mix_setup()
            mix_init_state(m)
            for ti in range(npb // 4):
                mix_tile(m, xb[ti * 4:ti * 4 + 4], True, ti == npb // 4 - 1)
            sch.barrier()
        load_x(x_d, NB)
        ffn(g_ffn1, w1g, w1u, w1d, NB)
        sch.barrier()
        if stop_after is None or stop_after >= 2:
            if m is None:
                m = mix_setup()
                mix_init_state(m)
            else:
                mix_restore_state(m)
            for ti in range(4):
                mix_tile(m, xb[ti * 4:ti * 4 + 4], False, False)
            sch.barrier()
        if stop_after is None or stop_after >= 3:
            xattn()
            sch.barrier()
        if stop_after is None or stop_after >= 4:
            ffn(g_ffn2, w2g, w2u, w2d, NB)
            sch.barrier()
        final(do_norm=(stop_after is None))
        sch.finish()
        sch.replay(block)
    return nc


def _consts():
    i = np.arange(128)
    same = (i[:, None] // 64) == (i[None, :] // 64)
    tri = (same & (i[:, None] <= i[None, :])).astype(np.float32)
    trirev = (same & (i[:, None] > i[None, :])).astype(np.float32)
    cind = np.stack([(i < 64), (i >= 64)], axis=1).astype(np.float32)
    return tri, trirev, cind, np.eye(128, dtype=np.float32)


def make_in_maps(inputs, pre_tok=PRE_TOK):
    f = lambda k: np.ascontiguousarray(np.asarray(inputs[k], dtype=np.float32))
    x = f("x"); mem = f("mem")
    tri, trirev, cind, ident = _consts()
    shared = {
        "ffn1_norm": f("ffn1_norm").reshape(1, D), "mix_norm": f("mix_norm").reshape(1, D),
        "xattn_norm": f("xattn_norm").reshape(1, D), "mem_norm": f("mem_norm").reshape(1, D),
        "ffn2_norm": f("ffn2_norm").reshape(1, D), "final_norm": f("final_norm").reshape(1, D),
        "ffn1_gate": f("ffn1_gate")[0], "ffn1_up": f("ffn1_up")[0], "ffn1_down": f("ffn1_down")[0],
        "ffn2_gate": f("ffn2_gate")[0], "ffn2_up": f("ffn2_up")[0], "ffn2_down": f("ffn2_down")[0],
        "w_in": f("w_in")[0], "w_out": f("w_out")[0], "w_q_mem": f("w_q_mem")[0],
        "w_kv_mem": f("w_kv_mem")[0], "w_o_mem": f("w_o_mem")[0],
        "lb_param": f("lb_param"), "hg": np.ascontiguousarray(f("hgrn_out_norm").reshape(4, 128).T),
        "conv_w": f("conv_w")[0],
        "c_tri": tri, "c_trirev": trirev, "c_cind": cind, "c_ident": ident,
    }
    maps = []
    npre = max(pre_tok, 128)
    for c in range(8):
        b, half = c // 2, c % 2
        t0 = half * TOK
        if half == 0 or pre_tok == 0:
            xp = np.zeros((npre, D), np.float32)
        else:
            xp = np.ascontiguousarray(x[b, t0 - npre:t0])
        d = dict(shared)
        d["x"] = np.ascontiguousarray(x[b, t0:t0 + TOK])
        d["xp"] = xp
        d["mem"] = mem[b]
        maps.append(d)
    return maps


def kernel(**inputs):
    nc = build_nc()
    maps = make_in_maps(inputs)
    res = run_bass_kernel_spmd(nc, maps, core_ids=list(range(8)))
    out = np.empty((4, 2 * TOK, D), np.float32)
    for c in range(8):
        out[c // 2, (c % 2) * TOK:(c % 2 + 1) * TOK] = res.results[c]["out"]
    return out
```
